# Optimizing a Trainium2 kernel written in Bass

```python
import jax, jax.numpy as jnp
from jax import lax
import numpy as np

D_MODEL = 1024
BATCH = 16
SEQ = 4096
DEPTH = 4

N_MIXERS = 3
N_POOL_LAYERS = (DEPTH + 2) // 3
N_MLA_LAYERS = (DEPTH + 1) // 3
N_CONV_LAYERS = DEPTH // 3

POOL_WINDOWS = (2, 4, 8, 16)
N_POOL_GROUPS = len(POOL_WINDOWS)
POOL_GROUP = D_MODEL // N_POOL_GROUPS

MLA_HEADS = D_MODEL // 64
QK_NOPE = 64
QK_ROPE = 32
V_HEAD = 64
Q_LORA = 3 * D_MODEL // 4
KV_LORA = D_MODEL // 4
ROPE_THETA = 10000.0
Q_BLOCK = 128

CONV_WIDTH = 3
FFN_HIDDEN = 2816

DEEPNORM_ALPHA = (2 * DEPTH) ** 0.25
DEEPNORM_BETA = (8 * DEPTH) ** -0.25
LN_EPS = 1e-5
RMS_EPS = 1e-6

kernel_name = 'hybrid_pool_mla_shortconv_deepnorm_adaln'


def layer_norm(x, g, b):
    xf = x.astype(jnp.float32)
    mu = jnp.mean(xf, axis=-1, keepdims=True)
    var = jnp.mean(jnp.square(xf - mu), axis=-1, keepdims=True)
    return ((xf - mu) * lax.rsqrt(var + LN_EPS) * g + b).astype(x.dtype)


def rms_norm(x, g):
    xf = x.astype(jnp.float32)
    y = xf * lax.rsqrt(jnp.mean(jnp.square(xf), axis=-1, keepdims=True) + RMS_EPS)
    return (y * g).astype(x.dtype)


def causal_dwconv(u, w):
    ch = u.shape[-1]
    return lax.conv_general_dilated(
        u, w[:, None, :].astype(u.dtype), window_strides=(1,), padding=[(CONV_WIDTH - 1, 0)],
        dimension_numbers=('NWC', 'WIO', 'NWC'), feature_group_count=ch)


def rope_tables(positions):
    inv_freq = ROPE_THETA ** (-jnp.arange(0, QK_ROPE, 2, dtype=jnp.float32) / QK_ROPE)
    ang = positions.astype(jnp.float32)[..., None] * inv_freq
    return jnp.cos(ang), jnp.sin(ang)


def apply_rope(x, cos, sin):
    half = x.shape[-1] // 2
    x1, x2 = x[..., :half], x[..., half:]
    cos = cos.astype(x.dtype)
    sin = sin.astype(x.dtype)
    return jnp.concatenate([x1 * cos - x2 * sin, x1 * sin + x2 * cos], axis=-1)


def pool_mixer(u, w_groups, scale):
    b, s, d = u.shape
    ug = u.reshape(b, s, N_POOL_GROUPS, POOL_GROUP)
    cs = jnp.cumsum(ug.astype(jnp.float32), axis=1)
    t = jnp.arange(s)
    means = []
    for g, w in enumerate(POOL_WINDOWS):
        csg = cs[:, :, g]
        prev = jnp.pad(csg, ((0, 0), (w, 0), (0, 0)))[:, :s]
        count = jnp.minimum(t + 1, w).astype(jnp.float32)[None, :, None]
        means.append((csg - prev) / count)
    pooled = jnp.stack(means, axis=2).astype(u.dtype) - ug
    y = jnp.einsum('bsgc,gcd->bsgd', pooled, w_groups).reshape(b, s, d)
    return y * scale


def mla_mixer(u, positions, w_a, q_norm, w_uq, kv_norm, w_ukv, w_o):
    b, s, _ = u.shape
    a = u @ w_a
    cq = rms_norm(a[..., :Q_LORA], q_norm)
    ckv = rms_norm(a[..., Q_LORA:Q_LORA + KV_LORA], kv_norm)
    k_pe = a[..., Q_LORA + KV_LORA:]
    q = (cq @ w_uq).reshape(b, s, MLA_HEADS, QK_NOPE + QK_ROPE)
    q_nope, q_pe = q[..., :QK_NOPE], q[..., QK_NOPE:]
    kv = (ckv @ w_ukv).reshape(b, s, MLA_HEADS, QK_NOPE + V_HEAD)
    k_nope, v = kv[..., :QK_NOPE], kv[..., QK_NOPE:]
    cos, sin = rope_tables(positions)
    q_pe = apply_rope(q_pe, cos[:, :, None], sin[:, :, None])
    k_pe = apply_rope(k_pe, cos, sin)
    sm_scale = (QK_NOPE + QK_ROPE) ** -0.5
    nb = s // Q_BLOCK
    qn_blocks = jnp.moveaxis(q_nope.reshape(b, nb, Q_BLOCK, MLA_HEADS, QK_NOPE), 1, 0)
    qp_blocks = jnp.moveaxis(q_pe.reshape(b, nb, Q_BLOCK, MLA_HEADS, QK_ROPE), 1, 0)
    key_idx = jnp.arange(s)
    neg = jnp.finfo(jnp.float32).min

    def attend(args):
        qn, qp, blk = args
        sc = (jnp.einsum('bqhd,bkhd->bhqk', qn, k_nope)
              + jnp.einsum('bqhr,bkr->bhqk', qp, k_pe)).astype(jnp.float32) * sm_scale
        q_idx = blk * Q_BLOCK + jnp.arange(Q_BLOCK)
        sc = jnp.where(key_idx[None, :] <= q_idx[:, None], sc, neg)
        p = jax.nn.softmax(sc, axis=-1).astype(v.dtype)
        return jnp.einsum('bhqk,bkhd->bqhd', p, v)

    o = lax.map(attend, (qn_blocks, qp_blocks, jnp.arange(nb)))
    o = jnp.moveaxis(o, 0, 1).reshape(b, s, MLA_HEADS * V_HEAD)
    return o @ w_o


def short_conv_mixer(u, w_in, conv_w, w_out):
    gb, gc, h = jnp.split(u @ w_in, 3, axis=-1)
    return (gb * causal_dwconv(gc * h, conv_w)) @ w_out


def conv_glu_ffn(u, w_up, conv_w, conv_b, w_down):
    h = causal_dwconv(u @ w_up, conv_w) + conv_b
    val, gate = jnp.split(h, 2, axis=-1)
    return (jax.nn.silu(gate) * val) @ w_down


def setup_inputs(seed: int = 0) -> dict:
    key = jax.random.key(seed)
    ks = iter(jax.random.split(key, 32))

    def nrm(shape, scale):
        return jax.random.normal(next(ks), shape, jnp.float32) * scale

    D, H, F = D_MODEL, MLA_HEADS, FFN_HIDDEN
    x = nrm((BATCH, SEQ, D), 1.0)
    c = nrm((BATCH, D), 1.0)
    offsets = jax.random.randint(next(ks), (BATCH, 1), 0, 1024)
    positions = (offsets + jnp.arange(SEQ)[None, :]).astype(jnp.int32)
    mod_w = nrm((DEPTH, D, 6 * D), 0.1 * D ** -0.5)
    mod_b = nrm((DEPTH, 6 * D), 0.02)
    ln_g = 1.0 + nrm((DEPTH, 2, D), 0.02)
    ln_b = nrm((DEPTH, 2, D), 0.02)
    pool_w = nrm((N_POOL_LAYERS, N_POOL_GROUPS, POOL_GROUP, POOL_GROUP), POOL_GROUP ** -0.5 * DEEPNORM_BETA)
    pool_scale = 1.0 + nrm((N_POOL_LAYERS, D), 0.1)
    mla_w_a = nrm((N_MLA_LAYERS, D, Q_LORA + KV_LORA + QK_ROPE), D ** -0.5)
    mla_q_norm = 1.0 + nrm((N_MLA_LAYERS, Q_LORA), 0.02)
    mla_w_uq = nrm((N_MLA_LAYERS, Q_LORA, H * (QK_NOPE + QK_ROPE)), Q_LORA ** -0.5)
    mla_kv_norm = 1.0 + nrm((N_MLA_LAYERS, KV_LORA), 0.02)
    mla_w_ukv = nrm((N_MLA_LAYERS, KV_LORA, H * (QK_NOPE + V_HEAD)), KV_LORA ** -0.5)
    mla_w_o = nrm((N_MLA_LAYERS, H * V_HEAD, D), (H * V_HEAD) ** -0.5 * DEEPNORM_BETA)
    sc_w_in = nrm((N_CONV_LAYERS, D, 3 * D), D ** -0.5)
    sc_conv = nrm((N_CONV_LAYERS, CONV_WIDTH, D), CONV_WIDTH ** -0.5)
    sc_w_out = nrm((N_CONV_LAYERS, D, D), D ** -0.5 * DEEPNORM_BETA)
    ffn_w_up = nrm((DEPTH, D, 2 * F), D ** -0.5)
    ffn_conv = nrm((DEPTH, CONV_WIDTH, 2 * F), CONV_WIDTH ** -0.5)
    ffn_conv_b = nrm((DEPTH, 2 * F), 0.02)
    ffn_w_down = nrm((DEPTH, F, D), F ** -0.5 * DEEPNORM_BETA)
    return {'x': x, 'c': c, 'positions': positions, 'mod_w': mod_w, 'mod_b': mod_b,
            'ln_g': ln_g, 'ln_b': ln_b, 'pool_w': pool_w, 'pool_scale': pool_scale,
            'mla_w_a': mla_w_a, 'mla_q_norm': mla_q_norm, 'mla_w_uq': mla_w_uq,
            'mla_kv_norm': mla_kv_norm, 'mla_w_ukv': mla_w_ukv, 'mla_w_o': mla_w_o,
            'sc_w_in': sc_w_in, 'sc_conv': sc_conv, 'sc_w_out': sc_w_out,
            'ffn_w_up': ffn_w_up, 'ffn_conv': ffn_conv, 'ffn_conv_b': ffn_conv_b, 'ffn_w_down': ffn_w_down}


def reference(x, c, positions, mod_w, mod_b, ln_g, ln_b, pool_w, pool_scale,
              mla_w_a, mla_q_norm, mla_w_uq, mla_kv_norm, mla_w_ukv, mla_w_o,
              sc_w_in, sc_conv, sc_w_out, ffn_w_up, ffn_conv, ffn_conv_b, ffn_w_down):
    cond = jax.nn.silu(c)
    for i in range(DEPTH):
        mod = (cond @ mod_w[i] + mod_b[i])[:, None, :]
        sh1, sc1, g1, sh2, sc2, g2 = jnp.split(mod, 6, axis=-1)
        u = x * (1.0 + sc1) + sh1
        kind, j = i % N_MIXERS, i // N_MIXERS
        if kind == 0:
            y = pool_mixer(u, pool_w[j], pool_scale[j])
        elif kind == 1:
            y = mla_mixer(u, positions, mla_w_a[j], mla_q_norm[j], mla_w_uq[j],
                          mla_kv_norm[j], mla_w_ukv[j], mla_w_o[j])
        else:
            y = short_conv_mixer(u, sc_w_in[j], sc_conv[j], sc_w_out[j])
        x = layer_norm(DEEPNORM_ALPHA * x + (1.0 + g1) * y, ln_g[i, 0], ln_b[i, 0])
        u = x * (1.0 + sc2) + sh2
        y = conv_glu_ffn(u, ffn_w_up[i], ffn_conv[i], ffn_conv_b[i], ffn_w_down[i])
        x = layer_norm(DEEPNORM_ALPHA * x + (1.0 + g2) * y, ln_g[i, 1], ln_b[i, 1])
    return x
```

```python
import numpy as np
import ml_dtypes
import concourse.bass as bass
import concourse.mybir as mybir
from concourse.bass_utils import run_bass_kernel_spmd
from contextlib import ExitStack

F32 = mybir.dt.float32
BF16 = mybir.dt.bfloat16
I32 = mybir.dt.int32
AF = mybir.ActivationFunctionType
ALU = mybir.AluOpType

D = 1024
DEPTH = 4
NC8 = 8
FF = 2816
HC = 44
GC = 22
HEADS = 16
ALPHA = float((2 * DEPTH) ** 0.25)
LN_EPS = 1e-5
RMS_EPS = 1e-6
SM_SCALE = float(96 ** -0.5)


class Buf:
    __slots__ = ("name", "w", "r")

    def __init__(self, name=""):
        self.name = name
        self.w = None
        self.r = []


def bufs(n, name=""):
    return [Buf(f"{name}{i}") for i in range(n)]


class Op:
    __slots__ = ("eng", "fn", "deps", "dwaits", "sig", "idx", "semkey", "phase")


class Sched:
    ENGS = ("pe", "act", "dve", "pool", "sp")

    def __init__(self, nc, top, same_engine_sync=False):
        self.nc = nc
        self.top = top
        self.ops = {e: [] for e in self.ENGS}
        self.last_real = {e: None for e in self.ENGS}
        self.dma_cnt = {}
        self.same_engine_sync = same_engine_sync
        self.phase = 0
        self.esem = None
        self.dsem = {}
        self.sigbase = {e: 0 for e in self.ENGS}
        self.waited = {e: {} for e in self.ENGS}
        self.nops = 0

    def op(self, eng, fn, reads=(), writes=(), semkey=None):
        o = Op()
        o.eng = eng
        o.fn = fn
        o.deps = set()
        o.dwaits = {}
        o.sig = False
        o.idx = 0
        o.semkey = semkey
        o.phase = self.phase
        for b in reads:
            if b.w is not None:
                self._dep(o, b.w)
        for b in writes:
            if b.w is not None:
                self._dep(o, b.w)
            for r in b.r:
                self._dep(o, r)
        for b in reads:
            b.r.append(o)
        for b in writes:
            b.w = o
            b.r = []
        if semkey is not None:
            self.dma_cnt[semkey] = self.dma_cnt.get(semkey, 0) + 1
        self.ops[eng].append(o)
        self.last_real[eng] = o
        self.nops += 1
        return o

    def _dep(self, o, d):
        if d is o:
            return
        if d.phase != self.phase:
            return
        if d.semkey is not None:
            k = d.semkey
            o.dwaits[k] = max(o.dwaits.get(k, 0), self.dma_cnt[k])
            return
        if d.eng == o.eng and (d.eng == "pe" or not self.same_engine_sync):
            return
        d.sig = True
        o.deps.add(d)

    def barrier(self):
        lasts = [self.last_real[e] for e in self.ENGS if self.last_real[e] is not None and self.last_real[e].phase == self.phase]
        for e in self.ENGS:
            o = Op()
            o.eng = e
            o.fn = None
            o.deps = set()
            o.dwaits = dict(self.dma_cnt)
            o.sig = False
            o.idx = 0
            o.semkey = None
            o.phase = self.phase
            for d in lasts:
                if d.semkey is not None or d.eng == e:
                    continue
                d.sig = True
                o.deps.add(d)
            self.ops[e].append(o)

    def flush(self):
        nc = self.nc
        self.barrier()
        if self.esem is None:
            self.esem = {e: self.top.enter_context(nc.semaphore("s_" + e)) for e in self.ENGS if e != "sp"}
        for k in self.dma_cnt:
            if k not in self.dsem:
                self.dsem[k] = self.top.enter_context(nc.semaphore("d_" + str(k)))
        esem, dsem = self.esem, self.dsem
        for e in self.ENGS:
            c = self.sigbase[e]
            for o in self.ops[e]:
                if o.sig:
                    c += 1
                    o.idx = c
            self.sigbase[e] = c
        ops = self.ops
        waited_all = self.waited

        def run(engname):
            def body(eng):
                waited = waited_all[engname]
                for o in ops[engname]:
                    ws = {}
                    for d in o.deps:
                        s = esem[d.eng]
                        if ws.get(s, 0) < d.idx:
                            ws[s] = d.idx
                    for k, v in o.dwaits.items():
                        s = dsem[k]
                        if ws.get(s, 0) < 16 * v:
                            ws[s] = 16 * v
                    for s, v in ws.items():
                        if waited.get(s, 0) < v:
                            eng.wait_ge(s, v)
                            waited[s] = v
                    if o.fn is None:
                        continue
                    ins = o.fn(eng)
                    if o.semkey is not None:
                        ins.then_inc(dsem[o.semkey], 16)
                    elif o.sig:
                        ins.then_inc(esem[engname], 1)
            return body

        with nc.Block() as block:
            block.tensor(run("pe"))
            block.scalar(run("act"))
            block.vector(run("dve"))
            block.gpsimd(run("pool"))
            block.sync(run("sp"))
        self.ops = {e: [] for e in self.ENGS}
        self.phase += 1


def colpack(vec):
    v = np.asarray(vec, dtype=np.float32).reshape(-1, 128)
    return np.ascontiguousarray(v.T)


class ColTable:
    def __init__(self):
        self.blocks = []
        self.off = {}
        self.n = 0

    def add(self, name, arr):
        arr = np.asarray(arr, dtype=np.float32)
        assert arr.shape[0] == 128
        self.off[name] = self.n
        self.blocks.append(arr)
        self.n += arr.shape[1]

    def build(self):
        return np.ascontiguousarray(np.concatenate(self.blocks, axis=1))


def cvec_layout(inp=None):
    ct = ColTable()
    z = lambda n: np.zeros((n,), np.float32)
    g = (lambda k, n: inp[k]) if inp is not None else None
    for l in range(DEPTH):
        for j in range(2):
            ct.add(f"lng{l}_{j}", colpack(inp["ln_g"][l, j] if inp else z(D)))
            ct.add(f"lnb{l}_{j}", colpack(inp["ln_b"][l, j] if inp else z(D)))
        ct.add(f"modb{l}", colpack(inp["mod_b"][l] if inp else z(6 * D)))
        for k in range(3):
            ct.add(f"fcw{l}_{k}", colpack(inp["ffn_conv"][l, k] if inp else z(2 * FF)))
        ct.add(f"fcb{l}", colpack(inp["ffn_conv_b"][l] if inp else z(2 * FF)))
    for j in range(2):
        ct.add(f"pscale{j}", colpack(inp["pool_scale"][j] if inp else z(D)))
    ct.add("qnorm", colpack(inp["mla_q_norm"][0] if inp else z(768)))
    ct.add("kvnorm", colpack(inp["mla_kv_norm"][0] if inp else z(256)))
    for k in range(3):
        ct.add(f"scw{k}", colpack(inp["sc_conv"][0, k] if inp else z(D)))
    p = np.arange(128)
    invf = (10000.0 ** (-np.arange(0, 32, 2, dtype=np.float32) / 32)).astype(np.float32)
    ct.add("invf", invf[p % 16].reshape(128, 1))
    ct.add("sgn", np.where((p % 32) < 16, -1.0, 1.0).astype(np.float32).reshape(128, 1))
    corr = np.zeros((128, 64), np.float32)
    for gi, w in enumerate((2, 4, 8, 16)):
        t = np.arange(16)
        corr[:, gi * 16:(gi + 1) * 16] = (w / np.minimum(t + 1, w)).astype(np.float32)[None, :]
    ct.add("corr", corr)
    ct.add("eps_ln", np.full((128, 1), LN_EPS, np.float32))
    ct.add("eps_rms", np.full((128, 1), RMS_EPS, np.float32))
    return ct


def make_masks():
    k = np.arange(128)[:, None]
    q = np.arange(512)[None, :]
    m = np.stack([((d * 128 + k) <= q) for d in range(4)], axis=1)
    return np.ascontiguousarray(m.astype(np.float32).astype(ml_dtypes.bfloat16))


def build_program(nseq, S, layers=(0, 1, 2, 3), stop_after=None):
    nc = bass.Bass("TRN2", target_bir_lowering=False)
    NCV = cvec_layout().n
    coff = cvec_layout().off

    def din(name, shape, dt=F32):
        return nc.dram_tensor(name, list(shape), dt, kind="ExternalInput").ap()

    def dscr(name, shape, dt=F32):
        return nc.dram_tensor(name, list(shape), dt).ap()

    xT = din("xT", [nseq, D, S])
    cT = din("cT", [128, NC8 * nseq])
    pos = din("pos", [nseq, S], I32)
    cvec_d = din("cvec", [128, NCV])
    masks_d = din("masks", [128, 4, 512], BF16)
    mod_w = din("mod_w", [DEPTH, D, 6 * D])
    pool_w = din("pool_w", [2, 4, 256, 256])
    w_a = din("w_a", [D, 1056])
    w_a_sw = din("w_a_sw", [D, 32])
    w_uq_nope = din("w_uq_nope", [768, 1024])
    w_uq_pe = din("w_uq_pe", [768, 512])
    w_uq_pesw = din("w_uq_pesw", [768, 512])
    w_uk = din("w_uk", [256, 1024])
    w_uv = din("w_uv", [256, 1024])
    w_o = din("w_o", [D, D])
    sc_w_in = din("sc_w_in", [D, 3 * D])
    sc_w_out = din("sc_w_out", [D, D])
    ffn_w_up = din("ffn_w_up", [DEPTH, D, 2 * FF])
    ffn_w_down = din("ffn_w_down", [DEPTH, FF, D])
    yT = nc.dram_tensor("yT", [nseq, D, S], F32, kind="ExternalOutput").ap()
    sA = dscr("sA", [nseq, D, S])
    sB = dscr("sB", [nseq, D, S])
    cqn_d = dscr("cqn_d", [nseq, 768, S], BF16)
    ckvn_d = dscr("ckvn_d", [nseq, 256, S], BF16)
    kpe_d = dscr("kpe_d", [nseq, 32, S], BF16)
    qpe_d = dscr("qpe_d", [nseq, 512, S], BF16)
    oT_d = dscr("oT_d", [nseq, D, S], BF16)
    g_d = dscr("g_d", [nseq, FF, S], BF16)

    top = ExitStack()
    S_ = Sched(nc, top)
    op = S_.op
    with top:
        uid = [0]

        def sbt(es, name, shape, dt):
            uid[0] += 1
            return es.enter_context(nc.sbuf_tensor(f"{name}_u{uid[0]}", list(shape), dt))

        cv = sbt(top, "cv", [128, NCV], F32)
        modv = sbt(top, "modv", [128, DEPTH, 6, nseq, NC8], F32)
        ones_ln = sbt(top, "ones_ln", [128, 128], F32)
        ones_q = sbt(top, "ones_q", [128, 128], F32)
        ones_kv = sbt(top, "ones_kv", [128, 128], F32)
        psum = top.enter_context(nc.psum_tensor("psum", [128, 8 * 512], F32))
        PB = bufs(8, "psb")
        B_cv = Buf("cv")
        B_modv = Buf("modv")
        B_ones = Buf("ones")

        def bank(b, n=512):
            return psum[:, b * 512:b * 512 + n]

        def col(name, i=0):
            o_ = coff[name] + i
            return cv[:, o_:o_ + 1]

        op("sp", lambda e: e.dma_start(out=cv[:], in_=cvec_d), writes=[B_cv], semkey="cv")
        op("pool", lambda e: e.memset(ones_ln[:], 1.0 / D), writes=[B_ones])
        op("pool", lambda e: e.memset(ones_q[:], 1.0 / 768), writes=[B_ones])
        op("pool", lambda e: e.memset(ones_kv[:], 1.0 / 256), writes=[B_ones])

        with ExitStack() as es:
            cond = sbt(es, "cond", [128, NC8 * nseq], F32)
            mw = [sbt(es, f"mw{i}", [128, NC8, D], F32) for i in range(2)]
            B_cond = Buf("cond")
            B_mw = bufs(2, "mw")
            op("sp", lambda e: e.dma_start(out=cond[:], in_=cT), writes=[B_cond], semkey="cond")
            op("act", lambda e: e.activation(out=cond[:], in_=cond[:], func=AF.Silu), reads=[B_cond], writes=[B_cond])
            pi = 0
            for l in layers:
                for j in range(6):
                    s = pi % 2
                    src = mod_w[l].rearrange("(kc p) n -> p kc n", p=128)[:, :, j * D:(j + 1) * D]
                    op("sp", (lambda s=s, src=src: lambda e: e.dma_start(out=mw[s][:], in_=src))(), writes=[B_mw[s]], semkey=f"mw{s}")
                    pb = pi % 2
                    for oc in range(NC8):
                        def mm(e, s=s, oc=oc, pb=pb):
                            ins = None
                            for kc in range(NC8):
                                ins = e.matmul(bank(pb)[:, oc * nseq:(oc + 1) * nseq], lhsT=mw[s][:, kc, oc * 128:(oc + 1) * 128],
                                               rhs=cond[:, kc * nseq:(kc + 1) * nseq], start=(kc == 0), stop=(kc == NC8 - 1))
                            return ins
                        op("pe", mm, reads=[B_mw[s], B_cond], writes=[PB[pb]])
                    for b in range(nseq):
                        def ev(e, l=l, j=j, b=b, pb=pb):
                            src_ = bank(pb)[:, 0:NC8 * nseq].rearrange("p (o b) -> p o b", b=nseq)[:, :, b]
                            mb = cv[:, coff[f"modb{l}"] + j * NC8: coff[f"modb{l}"] + (j + 1) * NC8]
                            return e.tensor_tensor(out=modv[:, l, j, b, :], in0=src_, in1=mb, op=ALU.add)
                        op("dve", ev, reads=[PB[pb], B_cv], writes=[B_modv])
                    pi += 1
                for j in (1, 2, 4, 5):
                    for b in range(nseq):
                        op("dve", (lambda l=l, j=j, b=b: lambda e: e.tensor_scalar(out=modv[:, l, j, b, :], in0=modv[:, l, j, b, :], scalar1=1.0, scalar2=None, op0=ALU.add))(),
                           reads=[B_modv], writes=[B_modv])
                if l % 3 == 0:
                    pj = l // 3
                    for b in range(nseq):
                        def gs(e, l=l, b=b, pj=pj):
                            ps_ = cv[:, coff[f"pscale{pj}"]: coff[f"pscale{pj}"] + NC8]
                            return e.tensor_tensor(out=modv[:, l, 2, b, :], in0=modv[:, l, 2, b, :], in1=ps_, op=ALU.mult)
                        op("dve", gs, reads=[B_modv, B_cv], writes=[B_modv])
            S_.flush()

        def mcol(l, j, b, c):
            return modv[:, l, j, b, c:c + 1]

        class LNRes:
            pass

        def ln_alloc(es, T):
            r = LNRes()
            r.sq = [sbt(es, f"ln_sq{i}", [128, T], F32) for i in range(2)]
            r.zs = sbt(es, "ln_zs", [128, T], F32)
            r.zq = sbt(es, "ln_zq", [128, T], F32)
            r.mean = sbt(es, "ln_mean", [128, T], F32)
            r.rstd = sbt(es, "ln_rstd", [128, T], F32)
            r.B_sq = bufs(2, "lnsq")
            r.B_zs = Buf("lnzs")
            r.B_zq = Buf("lnzq")
            r.B_mean = Buf("lnmean")
            r.B_rstd = Buf("lnrstd")
            r.T = T
            return r

        def ln_reduce(r, zc, Bz):
            op("pool", lambda e: e.tensor_tensor(out=r.zs[:], in0=zc[0], in1=zc[1], op=ALU.add), reads=[Bz[0], Bz[1]], writes=[r.B_zs])
            for c in range(2, NC8):
                op("pool", (lambda c=c: lambda e: e.tensor_tensor(out=r.zs[:], in0=r.zs[:], in1=zc[c], op=ALU.add))(), reads=[Bz[c], r.B_zs], writes=[r.B_zs])
            for c in range(NC8):
                s = c % 2
                op("act", (lambda c=c, s=s: lambda e: e.activation(out=r.sq[s][:], in_=zc[c], func=AF.Square))(), reads=[Bz[c]], writes=[r.B_sq[s]])
                if c == 1:
                    op("pool", lambda e: e.tensor_tensor(out=r.zq[:], in0=r.sq[0][:], in1=r.sq[1][:], op=ALU.add), reads=[r.B_sq[0], r.B_sq[1]], writes=[r.B_zq])
                elif c >= 2:
                    op("pool", (lambda s=s: lambda e: e.tensor_tensor(out=r.zq[:], in0=r.zq[:], in1=r.sq[s][:], op=ALU.add))(), reads=[r.B_sq[s], r.B_zq], writes=[r.B_zq])

        def ln_finish(r, zc, Bz, l, j, pb_m, pb_q):
            T = r.T
            op("pe", lambda e: e.matmul(bank(pb_m, T), lhsT=ones_ln[:], rhs=r.zs[:], start=True, stop=True), reads=[r.B_zs, B_ones], writes=[PB[pb_m]])
            op("pe", lambda e: e.matmul(bank(pb_q, T), lhsT=ones_ln[:], rhs=r.zq[:], start=True, stop=True), reads=[r.B_zq, B_ones], writes=[PB[pb_q]])
            op("act", lambda e: e.activation(out=r.mean[:], in_=bank(pb_m, T), func=AF.Copy), reads=[PB[pb_m]], writes=[r.B_mean])
            op("dve", lambda e: e.tensor_tensor(out=r.rstd[:], in0=r.mean[:], in1=r.mean[:], op=ALU.mult), reads=[r.B_mean], writes=[r.B_rstd])
            op("dve", lambda e: e.tensor_tensor(out=r.rstd[:], in0=bank(pb_q, T), in1=r.rstd[:], op=ALU.subtract), reads=[PB[pb_q], r.B_rstd], writes=[r.B_rstd])
            op("dve", lambda e: e.tensor_scalar(out=r.rstd[:], in0=r.rstd[:], scalar1=0.0, scalar2=None, op0=ALU.max), reads=[r.B_rstd], writes=[r.B_rstd])
            op("act", lambda e: e.activation(out=r.rstd[:], in_=r.rstd[:], func=AF.Sqrt, bias=col("eps_ln")), reads=[r.B_rstd, B_cv], writes=[r.B_rstd])
            op("dve", lambda e: e.reciprocal(out=r.rstd[:], in_=r.rstd[:]), reads=[r.B_rstd], writes=[r.B_rstd])
            for c in range(NC8):
                op("dve", (lambda c=c: lambda e: e.tensor_tensor(out=zc[c], in0=zc[c], in1=r.mean[:], op=ALU.subtract))(), reads=[Bz[c], r.B_mean], writes=[Bz[c]])
                op("dve", (lambda c=c: lambda e: e.tensor_tensor(out=zc[c], in0=zc[c], in1=r.rstd[:], op=ALU.mult))(), reads=[Bz[c], r.B_rstd], writes=[Bz[c]])
                op("act", (lambda c=c: lambda e: e.activation(out=zc[c], in_=zc[c], func=AF.Identity, scale=col(f"lng{l}_{j}", c), bias=col(f"lnb{l}_{j}", c)))(),
                   reads=[Bz[c], B_cv], writes=[Bz[c]])

        def tile_src(t, b, t0, T):
            return t[b].rearrange("(c p) s -> p c s", p=128)[:, :, t0:t0 + T]

        def run_tiles(tiles, LOAD, PRE, MAIN, POSTA=None, POSTB=None):
            n = len(tiles)
            LOAD(0, *tiles[0])
            PRE(0, *tiles[0])
            for i in range(n):
                if i + 1 < n:
                    LOAD(i + 1, *tiles[i + 1])
                MAIN(i, *tiles[i])
                if i + 1 < n:
                    PRE(i + 1, *tiles[i + 1])
                if POSTB is not None and i > 0:
                    POSTB(i - 1, *tiles[i - 1])
                if POSTA is not None:
                    POSTA(i, *tiles[i])
            if POSTB is not None:
                POSTB(n - 1, *tiles[n - 1])

        def ffn_up_phase(l, src):
            T = 512 if S >= 512 else S
            NT = S // T
            HG = GC // 2
            with ExitStack() as es:
                wup = sbt(es, "wup", [128, NC8, 2 * FF], BF16)
                xin = [sbt(es, f"xin{i}", [128, NC8, T], F32) for i in range(2)]
                ub = [sbt(es, f"ub{i}", [128, NC8, T], BF16) for i in range(2)]
                gb = [sbt(es, f"gb{i}", [128, HG, T], BF16) for i in range(2)]
                NAB = 8
                ab = [sbt(es, f"ab{i}", [128, T], F32) for i in range(NAB)]
                sg = [sbt(es, f"sg{i}", [128, T], F32) for i in range(2)]
                tb = [sbt(es, f"tb{i}", [128, T], F32) for i in range(2)]
                B_tb = bufs(2, "tb")
                tails = [sbt(es, f"tails{i}", [128, HC, 2], F32) for i in range(2)]
                NWU = 4
                B_wup = bufs(NWU, "wup")
                B_xin = [bufs(NC8, f"xin{i}_") for i in range(2)]
                B_ub = [bufs(NC8, f"ub{i}_") for i in range(2)]
                B_gb = [Buf("gb0"), Buf("gb1")]
                B_ab = bufs(NAB, "ab")
                B_sg = bufs(2, "sg")
                B_tails = [bufs(HC, "tails0_"), bufs(HC, "tails1_")]
                wsrc = ffn_w_up[l].rearrange("(kc p) n -> p kc n", p=128)
                cw = 2 * FF // NWU
                for i in range(NWU):
                    op("pool", (lambda i=i: lambda e: e.dma_start(out=wup[:, :, i * cw:(i + 1) * cw], in_=wsrc[:, :, i * cw:(i + 1) * cw]))(), writes=[B_wup[i]], semkey=f"wA{i}")
                tiles = [(b, ti) for b in range(nseq) for ti in range(NT)]

                def LOAD(i, b, ti):
                    s = i % 2
                    op("sp", lambda e: e.dma_start(out=xin[s][:], in_=tile_src(src, b, ti * T, T)), writes=B_xin[s], semkey=f"ld{s}")

                def PRE(i, b, ti):
                    s = i % 2
                    for c in range(NC8):
                        op("dve", (lambda c=c: lambda e: e.tensor_scalar(out=ub[s][:, c, :], in0=xin[s][:, c, :], scalar1=mcol(l, 4, b, c), scalar2=mcol(l, 3, b, c), op0=ALU.mult, op1=ALU.add))(),
                           reads=[B_xin[s][c], B_modv], writes=[B_ub[s][c]])

                def MAIN(i, b, ti):
                    s = i % 2
                    par = i % 2
                    t0 = ti * T
                    if ti == 0:
                        op("pool", lambda e: e.memset(tails[1 - par][:], 0.0), writes=B_tails[1 - par])
                    pend = None

                    def glu(p, av, ag):
                        ss = p % 2
                        hf = p // HG
                        op("act", lambda e: e.activation(out=sg[ss][:], in_=ab[ag][:], func=AF.Silu), reads=[B_ab[ag]], writes=[B_sg[ss]])
                        op("pool", lambda e: e.tensor_tensor(out=gb[hf][:, p - hf * HG, :], in0=ab[av][:], in1=sg[ss][:], op=ALU.mult), reads=[B_ab[av], B_sg[ss]], writes=[B_gb[hf]])
                        if p % HG == HG - 1:
                            op("sp", lambda e: e.dma_start(out=g_d[b].rearrange("(c p) s -> p c s", p=128)[:, hf * HG:(hf + 1) * HG, t0:t0 + T], in_=gb[hf][:]), reads=[B_gb[hf]], semkey=f"sg{hf}")

                    for p in range(GC):
                        slots = []
                        for half in range(2):
                            c = p + half * GC
                            pb = (2 * p + half) % 6
                            a = (2 * p + half) % NAB
                            slots.append(a)

                            def mm(e, c=c, pb=pb):
                                ins = None
                                for kc in range(NC8):
                                    ins = e.matmul(bank(pb, T), lhsT=wup[:, kc, c * 128:(c + 1) * 128], rhs=ub[s][:, kc, :], start=(kc == 0), stop=(kc == NC8 - 1))
                                return ins
                            op("pe", mm, reads=[B_wup[c * 128 // cw]] + B_ub[s], writes=[PB[pb]])
                            op("act", (lambda c=c, pb=pb, a=a: lambda e: e.activation(out=ab[a][:], in_=bank(pb, T), func=AF.Identity, scale=col(f"fcw{l}_2", c), bias=col(f"fcb{l}", c)))(),
                               reads=[PB[pb], B_cv], writes=[B_ab[a]])
                            op("act", (lambda c=c, pb=pb: lambda e: e.activation(out=tails[par][:, c, :], in_=bank(pb, T)[:, T - 2:T], func=AF.Copy))(), reads=[PB[pb]], writes=[B_tails[par][c]])
                            if half == 0:
                                tt_ = p % 2
                                op("act", (lambda c=c, pb=pb, tt_=tt_: lambda e: e.activation(out=tb[tt_][:, 1:T], in_=bank(pb, T)[:, 0:T - 1], func=AF.Identity, scale=col(f"fcw{l}_1", c)))(),
                                   reads=[PB[pb], B_cv], writes=[B_tb[tt_]])
                                op("act", (lambda c=c, tt_=tt_: lambda e: e.activation(out=tb[tt_][:, 0:1], in_=tails[1 - par][:, c, 1:2], func=AF.Identity, scale=col(f"fcw{l}_1", c)))(),
                                   reads=[B_tails[1 - par][c], B_cv], writes=[B_tb[tt_]])
                                op("dve", (lambda c=c, pb=pb, a=a: lambda e: e.scalar_tensor_tensor(out=ab[a][:, 2:T], in0=bank(pb, T)[:, 0:T - 2], scalar=col(f"fcw{l}_0", c), in1=ab[a][:, 2:T], op0=ALU.mult, op1=ALU.add))(),
                                   reads=[PB[pb], B_ab[a], B_cv, B_tb[tt_]], writes=[B_ab[a]])
                                op("dve", (lambda c=c, a=a: lambda e: e.scalar_tensor_tensor(out=ab[a][:, 0:2], in0=tails[1 - par][:, c, 0:2], scalar=col(f"fcw{l}_0", c), in1=ab[a][:, 0:2], op0=ALU.mult, op1=ALU.add))(),
                                   reads=[B_tails[1 - par][c], B_ab[a], B_cv], writes=[B_ab[a]])
                                op("pool", (lambda a=a, tt_=tt_: lambda e: e.tensor_tensor(out=ab[a][:], in0=ab[a][:], in1=tb[tt_][:], op=ALU.add))(), reads=[B_ab[a], B_tb[tt_]], writes=[B_ab[a]])
                            else:
                                op("dve", (lambda c=c, pb=pb, a=a: lambda e: e.scalar_tensor_tensor(out=ab[a][:, 1:T], in0=bank(pb, T)[:, 0:T - 1], scalar=col(f"fcw{l}_1", c), in1=ab[a][:, 1:T], op0=ALU.mult, op1=ALU.add))(),
                                   reads=[PB[pb], B_ab[a], B_cv, B_tails[par][c]], writes=[B_ab[a]])
                                op("dve", (lambda c=c, pb=pb, a=a: lambda e: e.scalar_tensor_tensor(out=ab[a][:, 2:T], in0=bank(pb, T)[:, 0:T - 2], scalar=col(f"fcw{l}_0", c), in1=ab[a][:, 2:T], op0=ALU.mult, op1=ALU.add))(),
                                   reads=[PB[pb], B_ab[a], B_cv], writes=[B_ab[a]])
                                op("dve", (lambda c=c, a=a: lambda e: e.scalar_tensor_tensor(out=ab[a][:, 0:1], in0=tails[1 - par][:, c, 1:2], scalar=col(f"fcw{l}_1", c), in1=ab[a][:, 0:1], op0=ALU.mult, op1=ALU.add))(),
                                   reads=[B_tails[1 - par][c], B_ab[a], B_cv], writes=[B_ab[a]])
                                op("dve", (lambda c=c, a=a: lambda e: e.scalar_tensor_tensor(out=ab[a][:, 0:2], in0=tails[1 - par][:, c, 0:2], scalar=col(f"fcw{l}_0", c), in1=ab[a][:, 0:2], op0=ALU.mult, op1=ALU.add))(),
                                   reads=[B_tails[1 - par][c], B_ab[a], B_cv], writes=[B_ab[a]])
                        if pend is not None:
                            glu(*pend)
                        pend = (p, slots[0], slots[1])
                    glu(*pend)

                run_tiles(tiles, LOAD, PRE, MAIN)
                S_.flush()

        def ffn_down_phase(l, src, dst):
            T = 512 if S >= 512 else S
            NT = S // T
            with ExitStack() as es:
                wdn = sbt(es, "wdn", [128, GC, D], BF16)
                zb = [sbt(es, f"zb{i}", [128, NC8, T], F32) for i in range(3)]
                gt = [sbt(es, f"gt{i}", [128, GC, T], BF16) for i in range(2)]
                lnr = ln_alloc(es, T)
                B_wdn = Buf("wdn")
                B_zb = [bufs(NC8, f"zb{i}_") for i in range(3)]
                B_gt = [Buf("gt0"), Buf("gt1")]
                op("pool", lambda e: e.dma_start(out=wdn[:], in_=ffn_w_down[l].rearrange("(kc p) n -> p kc n", p=128)), writes=[B_wdn], semkey="wB0")
                tiles = [(b, ti) for b in range(nseq) for ti in range(NT)]

                def LOAD(i, b, ti):
                    s = i % 2
                    sx = i % 3
                    op("sp", lambda e: e.dma_start(out=gt[s][:], in_=tile_src(g_d, b, ti * T, T)), writes=[B_gt[s]], semkey=f"lo{s}")
                    op("sp", lambda e: e.dma_start(out=zb[sx][:], in_=tile_src(src, b, ti * T, T)), writes=B_zb[sx], semkey=f"ld{sx}")

                def PRE(i, b, ti):
                    s = i % 2
                    sx = i % 3
                    for c in range(NC8):
                        op("act", (lambda c=c: lambda e: e.activation(out=zb[sx][:, c, :], in_=zb[sx][:, c, :], func=AF.Identity, scale=ALPHA))(), reads=[B_zb[sx][c]], writes=[B_zb[sx][c]])

                def MAIN(i, b, ti):
                    s = i % 2
                    sx = i % 3
                    for oc in range(NC8):
                        pb = oc % 4

                        def mm2(e, oc=oc, pb=pb):
                            ins = None
                            for c in range(GC):
                                ins = e.matmul(bank(pb, T), lhsT=wdn[:, c, oc * 128:(oc + 1) * 128], rhs=gt[s][:, c, :], start=(c == 0), stop=(c == GC - 1))
                            return ins
                        op("pe", mm2, reads=[B_wdn, B_gt[s]], writes=[PB[pb]])
                        op("dve", (lambda oc=oc, pb=pb: lambda e: e.scalar_tensor_tensor(out=zb[sx][:, oc, :], in0=bank(pb, T), scalar=mcol(l, 5, b, oc), in1=zb[sx][:, oc, :], op0=ALU.mult, op1=ALU.add))(),
                           reads=[PB[pb], B_zb[sx][oc], B_modv], writes=[B_zb[sx][oc]])

                def POSTA(i, b, ti):
                    sx = i % 3
                    ln_reduce(lnr, [zb[sx][:, c, :] for c in range(NC8)], B_zb[sx])

                def POSTB(i, b, ti):
                    sx = i % 3
                    ln_finish(lnr, [zb[sx][:, c, :] for c in range(NC8)], B_zb[sx], l, 1, 6, 7)
                    op("sp", lambda e: e.dma_start(out=tile_src(dst, b, ti * T, T), in_=zb[sx][:]), reads=B_zb[sx], semkey=f"st{sx}")

                run_tiles(tiles, LOAD, PRE, MAIN, POSTA, POSTB)
                S_.flush()

        def pool_phase(l, src, dst):
            T = 512 if S >= 512 else S
            NT = S // T
            H = 16
            E = H + T
            pj = l // 3
            with ExitStack() as es:
                pw = sbt(es, "pw", [128, 4, 2, 256], BF16)
                xw = [sbt(es, f"xw{i}", [128, NC8, H + T], F32) for i in range(3)]
                uw = sbt(es, "uw", [128, NC8, H + T], F32)
                scr = [sbt(es, f"pscr{i}", [128, 2, H + T], F32) for i in range(2)]
                pl = [sbt(es, f"pl{i}", [128, NC8, T], BF16) for i in range(2)]
                lnr = ln_alloc(es, T)
                B_pw = Buf("pw")
                B_xw = [bufs(NC8, f"xw{i}_") for i in range(3)]
                B_uw = bufs(NC8, "uw")
                B_scr = bufs(2, "pscr")
                B_pl = [bufs(NC8, f"pl{i}_") for i in range(2)]
                op("pool", lambda e: e.dma_start(out=pw[:], in_=pool_w[pj].rearrange("g (kc p) n -> p g kc n", p=128)), writes=[B_pw], semkey="wA0")
                tiles = [(b, ti) for b in range(nseq) for ti in range(NT)]

                def LOAD(i, b, ti):
                    s = i % 2
                    sx = i % 3
                    t0 = ti * T
                    if ti == 0:
                        op("sp", lambda e: e.dma_start(out=xw[sx][:, :, H:H + T], in_=tile_src(src, b, t0, T)), writes=B_xw[sx], semkey=f"ld{sx}")
                    else:
                        op("sp", lambda e: e.dma_start(out=xw[sx][:], in_=tile_src(src, b, t0 - H, T + H)), writes=B_xw[sx], semkey=f"ld{sx}")

                def PRE(i, b, ti):
                    s = i % 2
                    sx = i % 3
                    lo = H if ti == 0 else 0
                    for c in range(NC8):
                        if ti == 0:
                            op("pool", (lambda c=c: lambda e: e.memset(uw[:, c, 0:H], 0.0))(), writes=[B_uw[c]])
                        op("act", (lambda c=c: lambda e: e.activation(out=uw[:, c, lo:E], in_=xw[sx][:, c, lo:E], func=AF.Identity, scale=mcol(l, 1, b, c), bias=mcol(l, 0, b, c)))(),
                           reads=[B_xw[sx][c], B_modv], writes=[B_uw[c]])
                    for c in range(NC8):
                        op("act", (lambda c=c: lambda e: e.activation(out=xw[sx][:, c, H:E], in_=xw[sx][:, c, H:E], func=AF.Identity, scale=ALPHA))(), reads=[B_xw[sx][c]], writes=[B_xw[sx][c]])
                    for g in range(4):
                        w = 2 << g
                        cs = slice(2 * g, 2 * g + 2)
                        Bu = [B_uw[2 * g], B_uw[2 * g + 1]]
                        starts = {0: [16], 1: [14, 16], 2: [10, 12, 16], 3: [2, 4, 8, 16]}[g]
                        cur = None
                        for lvl, st in enumerate(starts):
                            sh = 1 << lvl
                            dsti = lvl % 2

                            def lv(e, cur=cur, dsti=dsti, st=st, sh=sh, cs=cs):
                                srcT = uw[:, cs, :] if cur is None else scr[cur][:, :, :]
                                return e.tensor_tensor(out=scr[dsti][:, :, st:E], in0=srcT[:, :, st:E], in1=srcT[:, :, st - sh:E - sh], op=ALU.add)
                            op("dve", lv, reads=(Bu if cur is None else [B_scr[cur]]), writes=[B_scr[dsti]])
                            cur = dsti
                        if ti == 0:
                            def cr(e, cur=cur, g=g):
                                cc = cv[:, coff["corr"] + g * 16: coff["corr"] + (g + 1) * 16]
                                ins = None
                                for k in range(2):
                                    ins = e.tensor_tensor(out=scr[cur][:, k, H:H + 16], in0=scr[cur][:, k, H:H + 16], in1=cc, op=ALU.mult)
                                return ins
                            op("dve", cr, reads=[B_scr[cur], B_cv], writes=[B_scr[cur]])
                        op("dve", (lambda cur=cur, cs=cs, w=w: lambda e: e.scalar_tensor_tensor(out=pl[s][:, cs, :], in0=scr[cur][:, :, H:E], scalar=1.0 / w, in1=uw[:, cs, H:E], op0=ALU.mult, op1=ALU.subtract))(),
                           reads=[B_scr[cur]] + Bu, writes=[B_pl[s][2 * g], B_pl[s][2 * g + 1]])

                def MAIN(i, b, ti):
                    s = i % 2
                    sx = i % 3
                    for g in range(4):
                        for oc in range(2):
                            c = 2 * g + oc
                            pb = c % 4

                            def mm(e, g=g, oc=oc, pb=pb):
                                ins = None
                                for kc in range(2):
                                    ins = e.matmul(bank(pb, T), lhsT=pw[:, g, kc, oc * 128:(oc + 1) * 128], rhs=pl[s][:, 2 * g + kc, :], start=(kc == 0), stop=(kc == 1))
                                return ins
                            op("pe", mm, reads=[B_pw, B_pl[s][2 * g], B_pl[s][2 * g + 1]], writes=[PB[pb]])
                            op("dve", (lambda c=c, pb=pb: lambda e: e.scalar_tensor_tensor(out=xw[sx][:, c, H:E], in0=bank(pb, T), scalar=mcol(l, 2, b, c), in1=xw[sx][:, c, H:E], op0=ALU.mult, op1=ALU.add))(),
                               reads=[PB[pb], B_xw[sx][c], B_modv], writes=[B_xw[sx][c]])

                def POSTA(i, b, ti):
                    sx = i % 3
                    ln_reduce(lnr, [xw[sx][:, c, H:E] for c in range(NC8)], B_xw[sx])

                def POSTB(i, b, ti):
                    sx = i % 3
                    ln_finish(lnr, [xw[sx][:, c, H:E] for c in range(NC8)], B_xw[sx], l, 0, 6, 7)
                    op("sp", lambda e: e.dma_start(out=tile_src(dst, b, ti * T, T), in_=xw[sx][:, :, H:E]), reads=B_xw[sx], semkey=f"st{sx}")

                run_tiles(tiles, LOAD, PRE, MAIN, POSTA, POSTB)
                S_.flush()

        def sconv_phase(l, src, dst):
            T = 512 if S >= 512 else S
            NT = S // T
            with ExitStack() as es:
                win = sbt(es, "win", [128, NC8, 3 * D], BF16)
                wout = sbt(es, "wout", [128, NC8, D], BF16)
                xw = [sbt(es, f"sxw{i}", [128, NC8, T], F32) for i in range(3)]
                ub = [sbt(es, f"sub{i}", [128, NC8, T], BF16) for i in range(2)]
                qb = sbt(es, "sqb", [128, NC8, T], BF16)
                t1 = [sbt(es, f"st1{i}", [128, T], F32) for i in range(2)]
                pbuf = [sbt(es, f"spb{i}", [128, T + 2], F32) for i in range(2)]
                ab = [sbt(es, f"sab{i}", [128, T], F32) for i in range(2)]
                ptl = sbt(es, "sptl", [128, NC8, 2], F32)
                lnr = ln_alloc(es, T)
                B_win = bufs(3, "win")
                B_wout = Buf("wout")
                B_xw = [bufs(NC8, f"sxw{i}_") for i in range(3)]
                B_ub = [bufs(NC8, f"sub{i}_") for i in range(2)]
                B_qb = bufs(NC8, "sqb")
                B_t1 = bufs(2, "st1")
                B_pb = bufs(2, "spb")
                B_ab = bufs(2, "sab")
                B_ptl = bufs(NC8, "sptl")
                wsrc = sc_w_in.rearrange("(kc p) n -> p kc n", p=128)
                for i in range(3):
                    op("pool", (lambda i=i: lambda e: e.dma_start(out=win[:, :, i * D:(i + 1) * D], in_=wsrc[:, :, i * D:(i + 1) * D]))(), writes=[B_win[i]], semkey=f"wA{i}")
                op("pool", lambda e: e.dma_start(out=wout[:], in_=sc_w_out.rearrange("(kc p) n -> p kc n", p=128)), writes=[B_wout], semkey="wB0")
                tiles = [(b, ti) for b in range(nseq) for ti in range(NT)]

                def LOAD(i, b, ti):
                    s = i % 2
                    sx = i % 3
                    op("sp", lambda e: e.dma_start(out=xw[sx][:], in_=tile_src(src, b, ti * T, T)), writes=B_xw[sx], semkey=f"ld{sx}")

                def PRE(i, b, ti):
                    s = i % 2
                    sx = i % 3
                    for c in range(NC8):
                        op("dve", (lambda c=c: lambda e: e.tensor_scalar(out=ub[s][:, c, :], in0=xw[sx][:, c, :], scalar1=mcol(l, 1, b, c), scalar2=mcol(l, 0, b, c), op0=ALU.mult, op1=ALU.add))(),
                           reads=[B_xw[sx][c], B_modv], writes=[B_ub[s][c]])
                    for c in range(NC8):
                        op("act", (lambda c=c: lambda e: e.activation(out=xw[sx][:, c, :], in_=xw[sx][:, c, :], func=AF.Identity, scale=ALPHA))(), reads=[B_xw[sx][c]], writes=[B_xw[sx][c]])

                def MAIN(i, b, ti):
                    s = i % 2
                    sx = i % 3
                    if ti == 0:
                        op("pool", lambda e: e.memset(ptl[:], 0.0), writes=B_ptl)
                    for c in range(NC8):
                        k2 = c % 2
                        banks3 = [0 + 3 * k2, 1 + 3 * k2, 2 + 3 * k2]
                        for which in range(3):
                            def mm(e, which=which, c=c, pbk=banks3[which]):
                                ins = None
                                for kc in range(NC8):
                                    ins = e.matmul(bank(pbk, T), lhsT=win[:, kc, which * D + c * 128: which * D + (c + 1) * 128], rhs=ub[s][:, kc, :], start=(kc == 0), stop=(kc == NC8 - 1))
                                return ins
                            op("pe", mm, reads=[B_win[which]] + B_ub[s], writes=[PB[banks3[which]]])
                        op("act", (lambda k2=k2, pbk=banks3[1]: lambda e: e.activation(out=t1[k2][:], in_=bank(pbk, T), func=AF.Copy))(), reads=[PB[banks3[1]]], writes=[B_t1[k2]])
                        op("pool", (lambda k2=k2, c=c: lambda e: e.tensor_copy(out=pbuf[k2][:, 0:2], in_=ptl[:, c, :]))(), reads=[B_ptl[c]], writes=[B_pb[k2]])
                        op("dve", (lambda k2=k2, pbk=banks3[2]: lambda e: e.tensor_tensor(out=pbuf[k2][:, 2:T + 2], in0=bank(pbk, T), in1=t1[k2][:], op=ALU.mult))(),
                           reads=[PB[banks3[2]], B_t1[k2]], writes=[B_pb[k2]])
                        op("pool", (lambda k2=k2, c=c: lambda e: e.tensor_copy(out=ptl[:, c, :], in_=pbuf[k2][:, T:T + 2]))(), reads=[B_pb[k2]], writes=[B_ptl[c]])
                        op("dve", (lambda k2=k2, c=c: lambda e: e.tensor_scalar(out=ab[k2][:], in0=pbuf[k2][:, 2:T + 2], scalar1=col("scw2", c), scalar2=None, op0=ALU.mult))(),
                           reads=[B_pb[k2], B_cv], writes=[B_ab[k2]])
                        op("dve", (lambda k2=k2, c=c: lambda e: e.scalar_tensor_tensor(out=ab[k2][:], in0=pbuf[k2][:, 1:T + 1], scalar=col("scw1", c), in1=ab[k2][:], op0=ALU.mult, op1=ALU.add))(),
                           reads=[B_pb[k2], B_ab[k2], B_cv], writes=[B_ab[k2]])
                        op("dve", (lambda k2=k2, c=c: lambda e: e.scalar_tensor_tensor(out=ab[k2][:], in0=pbuf[k2][:, 0:T], scalar=col("scw0", c), in1=ab[k2][:], op0=ALU.mult, op1=ALU.add))(),
                           reads=[B_pb[k2], B_ab[k2], B_cv], writes=[B_ab[k2]])
                        op("dve", (lambda k2=k2, c=c, pbk=banks3[0]: lambda e: e.tensor_tensor(out=qb[:, c, :], in0=bank(pbk, T), in1=ab[k2][:], op=ALU.mult))(),
                           reads=[PB[banks3[0]], B_ab[k2]], writes=[B_qb[c]])
                    for oc in range(NC8):
                        pb = 6 + oc % 2

                        def mm2(e, oc=oc, pb=pb):
                            ins = None
                            for kc in range(NC8):
                                ins = e.matmul(bank(pb, T), lhsT=wout[:, kc, oc * 128:(oc + 1) * 128], rhs=qb[:, kc, :], start=(kc == 0), stop=(kc == NC8 - 1))
                            return ins
                        op("pe", mm2, reads=[B_wout] + B_qb, writes=[PB[pb]])
                        op("dve", (lambda oc=oc, pb=pb: lambda e: e.scalar_tensor_tensor(out=xw[sx][:, oc, :], in0=bank(pb, T), scalar=mcol(l, 2, b, oc), in1=xw[sx][:, oc, :], op0=ALU.mult, op1=ALU.add))(),
                           reads=[PB[pb], B_xw[sx][oc], B_modv], writes=[B_xw[sx][oc]])

                def POSTA(i, b, ti):
                    sx = i % 3
                    ln_reduce(lnr, [xw[sx][:, c, :] for c in range(NC8)], B_xw[sx])

                def POSTB(i, b, ti):
                    sx = i % 3
                    ln_finish(lnr, [xw[sx][:, c, :] for c in range(NC8)], B_xw[sx], l, 0, 6, 7)
                    op("sp", lambda e: e.dma_start(out=tile_src(dst, b, ti * T, T), in_=xw[sx][:]), reads=B_xw[sx], semkey=f"st{sx}")

                run_tiles(tiles, LOAD, PRE, MAIN, POSTA, POSTB)
                S_.flush()

        def mla_a1(l, src):
            T = 256 if S >= 256 else S
            NT = S // T
            with ExitStack() as es:
                wa = sbt(es, "wa", [128, NC8, 1056 + 32], BF16)
                wqp = sbt(es, "wqp", [128, 6, 1024], BF16)
                xw = [sbt(es, f"axw{i}", [128, NC8, T], F32) for i in range(2)]
                ub = [sbt(es, f"aub{i}", [128, NC8, T], BF16) for i in range(2)]
                cqf = sbt(es, "cqf", [128, 8, T], F32)
                sq = [sbt(es, f"asq{i}", [128, T], F32) for i in range(2)]
                rs = [sbt(es, f"ars{i}", [128, T], F32) for i in range(2)]
                cqn = [sbt(es, f"acqn{i}", [128, 8, T], BF16) for i in range(2)]
                cosT = sbt(es, "cosT", [128, S], F32)
                sinT = sbt(es, "sinT", [128, S], F32)
                tscr = sbt(es, "tscr", [128, S], F32)
                posi = sbt(es, "posi", [128, S], I32)
                tki = posi
                rt = [sbt(es, f"art{i}", [128, T], F32) for i in range(4)]
                rpe = [sbt(es, f"arpe{i}", [128, 5, T], BF16) for i in range(2)]
                B_wa = Buf("wa")
                B_wqp = Buf("wqp")
                B_xw = [bufs(NC8, f"axw{i}_") for i in range(2)]
                B_ub = [bufs(NC8, f"aub{i}_") for i in range(2)]
                B_cqf = bufs(8, "cqf")
                B_sq = bufs(2, "asq")
                B_rs = bufs(2, "ars")
                B_cqn = [bufs(8, f"acqn{i}_") for i in range(2)]
                B_tab = Buf("tab")
                B_tscr = Buf("tscr")
                B_rt = bufs(4, "art")
                B_rpe = [bufs(5, f"arpe{i}_") for i in range(2)]
                op("pool", lambda e: e.dma_start(out=wa[:, :, 0:1056], in_=w_a.rearrange("(kc p) n -> p kc n", p=128)), writes=[B_wa], semkey="wA0")
                op("pool", lambda e: e.dma_start(out=wa[:, :, 1056:1088], in_=w_a_sw.rearrange("(kc p) n -> p kc n", p=128)), writes=[B_wa], semkey="wA0")
                op("pool", lambda e: e.dma_start(out=wqp[:, :, 0:512], in_=w_uq_pe.rearrange("(kc p) n -> p kc n", p=128)), writes=[B_wqp], semkey="wA1")
                op("pool", lambda e: e.dma_start(out=wqp[:, :, 512:1024], in_=w_uq_pesw.rearrange("(kc p) n -> p kc n", p=128)), writes=[B_wqp], semkey="wA1")
                C1 = 6.28125
                C2 = float(2 * np.pi - 6.28125)
                PI = float(np.pi)

                def tables(b):
                    op("sp", lambda e: e.dma_start(out=posi[:], in_=pos[b:b + 1, :].partition_broadcast(128)), writes=[B_tscr], semkey="ld2")
                    op("dve", lambda e: e.tensor_copy(out=sinT[:], in_=posi[:]), reads=[B_tscr], writes=[B_tab])
                    op("dve", lambda e: e.tensor_scalar(out=sinT[:], in0=sinT[:], scalar1=col("invf"), scalar2=None, op0=ALU.mult), reads=[B_tab, B_cv], writes=[B_tab])
                    op("dve", lambda e: e.tensor_scalar(out=cosT[:], in0=sinT[:], scalar1=PI / 2, scalar2=None, op0=ALU.add), reads=[B_tab], writes=[B_tab])
                    for tb in (sinT, cosT):
                        op("dve", (lambda tb=tb: lambda e: e.tensor_scalar(out=tscr[:], in0=tb[:], scalar1=float(1 / (2 * np.pi)), scalar2=None, op0=ALU.mult))(), reads=[B_tab], writes=[B_tscr])
                        op("dve", lambda e: e.tensor_copy(out=tki[:], in_=tscr[:]), reads=[B_tscr], writes=[B_tscr])
                        op("dve", lambda e: e.tensor_copy(out=tscr[:], in_=tki[:]), reads=[B_tscr], writes=[B_tscr])
                        op("dve", (lambda tb=tb: lambda e: e.scalar_tensor_tensor(out=tb[:], in0=tscr[:], scalar=-C1, in1=tb[:], op0=ALU.mult, op1=ALU.add))(), reads=[B_tscr, B_tab], writes=[B_tab])
                        op("dve", (lambda tb=tb: lambda e: e.scalar_tensor_tensor(out=tb[:], in0=tscr[:], scalar=-C2, in1=tb[:], op0=ALU.mult, op1=ALU.add))(), reads=[B_tscr, B_tab], writes=[B_tab])
                        op("dve", (lambda tb=tb: lambda e: e.tensor_scalar(out=tscr[:], in0=tb[:], scalar1=PI, scalar2=float(-2 * np.pi), op0=ALU.is_gt, op1=ALU.mult))(), reads=[B_tab], writes=[B_tscr])
                        op("dve", (lambda tb=tb: lambda e: e.tensor_tensor(out=tb[:], in0=tb[:], in1=tscr[:], op=ALU.add))(), reads=[B_tscr, B_tab], writes=[B_tab])
                        op("dve", (lambda tb=tb: lambda e: e.tensor_scalar(out=tscr[:], in0=tb[:], scalar1=-PI, scalar2=float(2 * np.pi), op0=ALU.is_lt, op1=ALU.mult))(), reads=[B_tab], writes=[B_tscr])
                        op("dve", (lambda tb=tb: lambda e: e.tensor_tensor(out=tb[:], in0=tb[:], in1=tscr[:], op=ALU.add))(), reads=[B_tscr, B_tab], writes=[B_tab])
                        op("dve", (lambda tb=tb: lambda e: e.tensor_scalar(out=tb[:], in0=tb[:], scalar1=PI, scalar2=-PI, op0=ALU.min, op1=ALU.max))(), reads=[B_tab], writes=[B_tab])
                        op("act", (lambda tb=tb: lambda e: e.activation(out=tb[:], in_=tb[:], func=AF.Sin))(), reads=[B_tab], writes=[B_tab])
                    op("dve", lambda e: e.tensor_scalar(out=sinT[:], in0=sinT[:], scalar1=col("sgn"), scalar2=None, op0=ALU.mult), reads=[B_tab, B_cv], writes=[B_tab])

                tiles = [(b, ti) for b in range(nseq) for ti in range(NT)]

                def LOAD(i, b, ti):
                    s = i % 2
                    op("sp", lambda e: e.dma_start(out=xw[s][:], in_=tile_src(src, b, ti * T, T)), writes=B_xw[s], semkey=f"ld{s}")

                def PRE(i, b, ti):
                    s = i % 2
                    for c in range(NC8):
                        op("dve", (lambda c=c: lambda e: e.tensor_scalar(out=ub[s][:, c, :], in0=xw[s][:, c, :], scalar1=mcol(l, 1, b, c), scalar2=mcol(l, 0, b, c), op0=ALU.mult, op1=ALU.add))(),
                           reads=[B_xw[s][c], B_modv], writes=[B_ub[s][c]])

                def MAIN(i, b, ti):
                    s = i % 2
                    t0 = ti * T
                    if ti == 0:
                        tables(b)
                    for c in range(8):
                        pb = c % 3
                        grp = 0 if c < 6 else 1

                        def mm(e, c=c, pb=pb):
                            ins = None
                            for kc in range(NC8):
                                ins = e.matmul(bank(pb, T), lhsT=wa[:, kc, c * 128:(c + 1) * 128], rhs=ub[s][:, kc, :], start=(kc == 0), stop=(kc == NC8 - 1))
                            return ins
                        op("pe", mm, reads=[B_wa] + B_ub[s], writes=[PB[pb]])
                        op("act", (lambda c=c, pb=pb: lambda e: e.activation(out=cqf[:, c, :], in_=bank(pb, T), func=AF.Copy))(), reads=[PB[pb]], writes=[B_cqf[c]])
                        op("act", (lambda c=c, pb=pb: lambda e: e.activation(out=sq[c % 2][:], in_=bank(pb, T), func=AF.Square))(), reads=[PB[pb]], writes=[B_sq[c % 2]])
                        first = c in (0, 6)
                        last = c in (5, 7)
                        op("pe", (lambda c=c, grp=grp, first=first, last=last: lambda e: e.matmul(bank(6 + grp, T), lhsT=(ones_q if grp == 0 else ones_kv)[:], rhs=sq[c % 2][:], start=first, stop=last))(),
                           reads=[B_sq[c % 2], B_ones], writes=[PB[6 + grp]])
                    for grp in range(2):
                        op("dve", (lambda grp=grp: lambda e: e.tensor_scalar(out=rs[grp][:], in0=bank(6 + grp, T), scalar1=0.0, scalar2=None, op0=ALU.max))(), reads=[PB[6 + grp]], writes=[B_rs[grp]])
                        op("act", (lambda grp=grp: lambda e: e.activation(out=rs[grp][:], in_=rs[grp][:], func=AF.Sqrt, bias=col("eps_rms")))(), reads=[B_rs[grp], B_cv], writes=[B_rs[grp]])
                        op("dve", (lambda grp=grp: lambda e: e.reciprocal(out=rs[grp][:], in_=rs[grp][:]))(), reads=[B_rs[grp]], writes=[B_rs[grp]])
                    for c in range(8):
                        grp = 0 if c < 6 else 1
                        nm = col("qnorm", c) if c < 6 else col("kvnorm", c - 6)
                        op("dve", (lambda c=c, grp=grp, nm=nm: lambda e: e.scalar_tensor_tensor(out=cqn[s][:, c, :], in0=cqf[:, c, :], scalar=nm, in1=rs[grp][:], op0=ALU.mult, op1=ALU.mult))(),
                           reads=[B_cqf[c], B_rs[grp], B_cv], writes=[B_cqn[s][c]])
                    op("sp", lambda e: e.dma_start(out=cqn_d[b].rearrange("(c p) s -> p c s", p=128)[:, :, t0:t0 + T], in_=cqn[s][:, 0:6, :]), reads=B_cqn[s][0:6], semkey=f"st{s}")
                    op("sp", lambda e: e.dma_start(out=ckvn_d[b].rearrange("(c p) s -> p c s", p=128)[:, :, t0:t0 + T], in_=cqn[s][:, 6:8, :]), reads=B_cqn[s][6:8], semkey=f"st{s}")
                    for c in range(5):
                        pa = 3 if c % 2 == 0 else 0
                        pbk = 4 if c % 2 == 0 else 5
                        M = 128 if c < 4 else 32

                        def mmA(e, c=c, pa=pa):
                            ins = None
                            if c < 4:
                                for kc in range(6):
                                    ins = e.matmul(bank(pa, T), lhsT=wqp[:, kc, c * 128:(c + 1) * 128], rhs=cqn[s][:, kc, :], start=(kc == 0), stop=(kc == 5))
                            else:
                                for kc in range(NC8):
                                    ins = e.matmul(bank(pa, T)[0:32, :], lhsT=wa[:, kc, 1024:1056], rhs=ub[s][:, kc, :], start=(kc == 0), stop=(kc == NC8 - 1))
                            return ins

                        def mmB(e, c=c, pbk=pbk):
                            ins = None
                            if c < 4:
                                for kc in range(6):
                                    ins = e.matmul(bank(pbk, T), lhsT=wqp[:, kc, 512 + c * 128:512 + (c + 1) * 128], rhs=cqn[s][:, kc, :], start=(kc == 0), stop=(kc == 5))
                            else:
                                for kc in range(NC8):
                                    ins = e.matmul(bank(pbk, T)[0:32, :], lhsT=wa[:, kc, 1056:1088], rhs=ub[s][:, kc, :], start=(kc == 0), stop=(kc == NC8 - 1))
                            return ins
                        rd = (B_cqn[s][0:6] + [B_wqp]) if c < 4 else (B_ub[s] + [B_wa])
                        op("pe", mmA, reads=rd, writes=[PB[pa]])
                        op("pe", mmB, reads=rd, writes=[PB[pbk]])
                        r0 = (c % 2) * 2
                        op("dve", (lambda pa=pa, M=M, r0=r0: lambda e: e.tensor_tensor(out=rt[r0][0:M, :], in0=bank(pa, T)[0:M, :], in1=cosT[0:M, t0:t0 + T], op=ALU.mult))(),
                           reads=[PB[pa], B_tab], writes=[B_rt[r0]])
                        op("dve", (lambda pbk=pbk, M=M, r0=r0: lambda e: e.tensor_tensor(out=rt[r0 + 1][0:M, :], in0=bank(pbk, T)[0:M, :], in1=sinT[0:M, t0:t0 + T], op=ALU.mult))(),
                           reads=[PB[pbk], B_tab], writes=[B_rt[r0 + 1]])
                        op("pool", (lambda c=c, M=M, r0=r0: lambda e: e.tensor_tensor(out=rpe[s][0:M, c, :], in0=rt[r0][0:M, :], in1=rt[r0 + 1][0:M, :], op=ALU.add))(),
                           reads=[B_rt[r0], B_rt[r0 + 1]], writes=[B_rpe[s][c]])
                    op("sp", lambda e: e.dma_start(out=qpe_d[b].rearrange("(c p) s -> p c s", p=128)[:, :, t0:t0 + T], in_=rpe[s][:, 0:4, :]), reads=B_rpe[s][0:4], semkey=f"st{s}")
                    op("sp", lambda e: e.dma_start(out=kpe_d[b][:, t0:t0 + T], in_=rpe[s][0:32, 4, :]), reads=[B_rpe[s][4]], semkey=f"st{s}")

                run_tiles(tiles, LOAD, PRE, MAIN)
                S_.flush()

        def mla_a2():
            QT = 512 if S >= 512 else S
            NQ = S // QT
            NKT = S // 128
            KPQ = QT // 128
            with ExitStack() as es:
                wq = sbt(es, "wq", [128, 6, 1024], BF16)
                wk = sbt(es, "wk", [128, 2, 1024], BF16)
                wv = sbt(es, "wv", [128, 2, 1024], BF16)
                msk = sbt(es, "msk", [128, 4, 512], BF16)
                cq = sbt(es, "cq", [128, 6, S], BF16)
                ckv = sbt(es, "ckv", [128, 2, S], BF16)
                KTb = [sbt(es, f"KT{i}", [128, S], BF16) for i in range(2)]
                QTb = [sbt(es, f"QT{i}", [128, S], BF16) for i in range(2)]
                Vb = [sbt(es, f"V{i}", [128, NKT, 128], BF16) for i in range(2)]
                NPT = 4
                PT = [sbt(es, f"PT{i}", [128, QT], BF16) for i in range(NPT)]
                rden = sbt(es, "rden", [128, QT], F32)
                rden0 = sbt(es, "rden0", [128, QT], F32)
                ost = [sbt(es, f"ost{i}", [128, QT], BF16) for i in range(2)]
                B_w = Buf("a2w")
                B_msk = Buf("msk")
                B_cq = Buf("cq")
                B_ckv = Buf("ckv")
                B_KT = [Buf("KTn0"), Buf("KTn1")]
                B_KTp = [Buf("KTp0"), Buf("KTp1")]
                B_QT = [bufs(NQ, "QTn0_"), bufs(NQ, "QTn1_")]
                B_QTp = [Buf("QTp0"), Buf("QTp1")]
                B_V = [Buf("V0"), Buf("V1")]
                B_Vones = [Buf("Vo0"), Buf("Vo1")]
                B_PT = bufs(NPT, "PT")
                B_rden = Buf("rden")
                B_rden0 = Buf("rden0")
                B_ost = bufs(2, "ost")
                op("pool", lambda e: e.dma_start(out=wq[:], in_=w_uq_nope.rearrange("(kc p) n -> p kc n", p=128)), writes=[B_w], semkey="wA0")
                op("pool", lambda e: e.dma_start(out=wk[:], in_=w_uk.rearrange("(kc p) n -> p kc n", p=128)), writes=[B_w], semkey="wA0")
                op("pool", lambda e: e.dma_start(out=wv[:], in_=w_uv.rearrange("(kc p) n -> p kc n", p=128)), writes=[B_w], semkey="wA0")
                op("sp", lambda e: e.dma_start(out=msk[:], in_=masks_d), writes=[B_msk], semkey="ld2")
                for i in range(2):
                    op("pool", (lambda i=i: lambda e: e.memset(Vb[i][:, :, 64:128], 1.0))(), writes=[B_Vones[i]])

                def proj(b, h, sl):
                    if h == 0:
                        op("sp", lambda e: e.dma_start(out=cq[:], in_=cqn_d[b].rearrange("(c p) s -> p c s", p=128)), writes=[B_cq], semkey="ld0")
                        op("sp", lambda e: e.dma_start(out=ckv[:], in_=ckvn_d[b].rearrange("(c p) s -> p c s", p=128)), writes=[B_ckv], semkey="ld1")
                    op("sp", lambda e: e.dma_start(out=KTb[sl][64:96, :], in_=kpe_d[b]), writes=[B_KTp[sl]], semkey=f"kp{sl}")
                    op("sp", lambda e: e.dma_start(out=QTb[sl][64:96, :], in_=qpe_d[b][h * 32:(h + 1) * 32, :]), writes=[B_QTp[sl]], semkey=f"qp{sl}")
                    for qi in range(NQ):
                        pb = qi % 2

                        def mmk(e, qi=qi, pb=pb):
                            ins = None
                            for kc in range(2):
                                ins = e.matmul(bank(pb, QT)[0:64, :], lhsT=wk[:, kc, h * 64:(h + 1) * 64], rhs=ckv[:, kc, qi * QT:(qi + 1) * QT], start=(kc == 0), stop=(kc == 1))
                            return ins
                        op("pe", mmk, reads=[B_w, B_ckv], writes=[PB[pb]])
                        op("dve", (lambda qi=qi, pb=pb: lambda e: e.tensor_copy(out=KTb[sl][0:64, qi * QT:(qi + 1) * QT], in_=bank(pb, QT)[0:64, :]))(), reads=[PB[pb]], writes=[B_KT[sl]])
                    for qi in range(NQ):
                        pb = qi % 2

                        def mmq(e, qi=qi, pb=pb):
                            ins = None
                            for kc in range(6):
                                ins = e.matmul(bank(pb, QT)[0:64, :], lhsT=wq[:, kc, h * 64:(h + 1) * 64], rhs=cq[:, kc, qi * QT:(qi + 1) * QT], start=(kc == 0), stop=(kc == 5))
                            return ins
                        op("pe", mmq, reads=[B_w, B_cq], writes=[PB[pb]])
                        op("dve", (lambda qi=qi, pb=pb: lambda e: e.tensor_copy(out=QTb[sl][0:64, qi * QT:(qi + 1) * QT], in_=bank(pb, QT)[0:64, :]))(), reads=[PB[pb]], writes=[B_QT[sl][qi]])
                    for t8 in range(NKT // 8 if NKT >= 8 else 1):
                        nt8 = min(8, NKT)
                        pb = 2

                        def mmv(e, t8=t8, pb=pb, nt8=nt8):
                            ins = None
                            for k in range(nt8):
                                tc = t8 * 8 + k
                                for kc in range(2):
                                    ins = e.matmul(bank(pb)[:, k * 64:(k + 1) * 64], lhsT=ckv[:, kc, tc * 128:(tc + 1) * 128], rhs=wv[:, kc, h * 64:(h + 1) * 64], start=(kc == 0), stop=(kc == 1))
                            return ins
                        op("pe", mmv, reads=[B_w, B_ckv], writes=[PB[pb]])
                        op("dve", (lambda t8=t8, pb=pb, nt8=nt8: lambda e: e.tensor_copy(out=Vb[sl][:, t8 * 8:t8 * 8 + nt8, 0:64], in_=bank(pb)[:, 0:nt8 * 64].rearrange("p (k d) -> p k d", d=64)))(),
                           reads=[PB[pb]], writes=[B_V[sl]])

                cnt = [0]

                def emitS(b, h, sl, qi, ki):
                    k = cnt[0]
                    cnt[0] += 1
                    pbs = 3 + k % 3
                    pts = k % NPT
                    dg = ki - KPQ * qi
                    c0 = dg * 128 if dg > 0 else 0
                    op("pe", lambda e: e.matmul(bank(pbs, QT)[:, c0:QT], lhsT=KTb[sl][0:96, ki * 128:(ki + 1) * 128], rhs=QTb[sl][0:96, qi * QT + c0:(qi + 1) * QT], start=True, stop=True),
                       reads=[B_KT[sl], B_KTp[sl], B_QT[sl][qi], B_QTp[sl]], writes=[PB[pbs]])
                    op("act", lambda e: e.activation(out=PT[pts][:, c0:QT], in_=bank(pbs, QT)[:, c0:QT], func=AF.Exp, scale=SM_SCALE), reads=[PB[pbs]], writes=[B_PT[pts]])
                    if dg >= 0:
                        op("pool", lambda e: e.tensor_tensor(out=PT[pts][:, c0:c0 + 128], in0=PT[pts][:, c0:c0 + 128], in1=msk[:, 0, 0:128], op=ALU.mult), reads=[B_PT[pts], B_msk], writes=[B_PT[pts]])
                    return (pts, c0)

                def emitPV(b, h, sl, qi, ki, nk, st):
                    pts, c0 = st
                    po = 6 + qi % 2
                    op("pe", lambda e: e.matmul(bank(po, QT)[:, c0:QT], lhsT=Vb[sl][:, ki, :], rhs=PT[pts][:, c0:QT], start=(ki == 0), stop=(ki == nk - 1)),
                       reads=[B_V[sl], B_Vones[sl], B_PT[pts]], writes=[PB[po]])
                    if ki == nk - 1:
                        os_ = qi % 2
                        op("dve", lambda e: e.reciprocal(out=rden[64:128, :], in_=bank(po, QT)[64:128, :]), reads=[PB[po]], writes=[B_rden])
                        op("dve", lambda e: e.tensor_copy(out=rden0[0:64, :], in_=rden[64:128, :]), reads=[B_rden], writes=[B_rden0])
                        op("dve", lambda e: e.tensor_tensor(out=ost[os_][0:64, :], in0=bank(po, QT)[0:64, :], in1=rden0[0:64, :], op=ALU.mult), reads=[PB[po], B_rden0], writes=[B_ost[os_]])
                        op("sp", lambda e: e.dma_start(out=oT_d[b][h * 64:(h + 1) * 64, qi * QT:(qi + 1) * QT], in_=ost[os_][0:64, :]), reads=[B_ost[os_]], semkey=f"st{os_}")

                heads = [(b, h) for b in range(nseq) for h in range(HEADS)]
                LOOK = 2
                proj(heads[0][0], heads[0][1], 0)
                for gi, (b, h) in enumerate(heads):
                    sl = gi % 2
                    pairs = [(qi, ki) for qi in range(NQ) for ki in range(KPQ * (qi + 1))]
                    mid = len(pairs) // 2
                    states = {}
                    for j in range(min(LOOK, len(pairs))):
                        states[j] = emitS(b, h, sl, *pairs[j])
                    for j, (qi, ki) in enumerate(pairs):
                        if j + LOOK < len(pairs):
                            states[j + LOOK] = emitS(b, h, sl, *pairs[j + LOOK])
                        emitPV(b, h, sl, qi, ki, KPQ * (qi + 1), states.pop(j))
                        if j == mid and gi + 1 < len(heads):
                            proj(heads[gi + 1][0], heads[gi + 1][1], (gi + 1) % 2)
                S_.flush()

        def mla_a3(l, src, dst):
            T = 512 if S >= 512 else S
            NT = S // T
            with ExitStack() as es:
                wo = sbt(es, "wo", [128, NC8, D], BF16)
                xw = [sbt(es, f"oxw{i}", [128, NC8, T], F32) for i in range(3)]
                ob = [sbt(es, f"oob{i}", [128, NC8, T], BF16) for i in range(2)]
                lnr = ln_alloc(es, T)
                B_wo = Buf("wo")
                B_xw = [bufs(NC8, f"oxw{i}_") for i in range(3)]
                B_ob = [Buf("oob0"), Buf("oob1")]
                op("pool", lambda e: e.dma_start(out=wo[:], in_=w_o.rearrange("(kc p) n -> p kc n", p=128)), writes=[B_wo], semkey="wA0")
                tiles = [(b, ti) for b in range(nseq) for ti in range(NT)]

                def LOAD(i, b, ti):
                    s = i % 2
                    sx = i % 3
                    op("sp", lambda e: e.dma_start(out=xw[sx][:], in_=tile_src(src, b, ti * T, T)), writes=B_xw[sx], semkey=f"ld{sx}")
                    op("sp", lambda e: e.dma_start(out=ob[s][:], in_=tile_src(oT_d, b, ti * T, T)), writes=[B_ob[s]], semkey=f"lo{s}")

                def PRE(i, b, ti):
                    s = i % 2
                    sx = i % 3
                    for c in range(NC8):
                        op("act", (lambda c=c: lambda e: e.activation(out=xw[sx][:, c, :], in_=xw[sx][:, c, :], func=AF.Identity, scale=ALPHA))(), reads=[B_xw[sx][c]], writes=[B_xw[sx][c]])

                def MAIN(i, b, ti):
                    s = i % 2
                    sx = i % 3
                    for oc in range(NC8):
                        pb = oc % 4

                        def mm(e, oc=oc, pb=pb):
                            ins = None
                            for kc in range(NC8):
                                ins = e.matmul(bank(pb, T), lhsT=wo[:, kc, oc * 128:(oc + 1) * 128], rhs=ob[s][:, kc, :], start=(kc == 0), stop=(kc == NC8 - 1))
                            return ins
                        op("pe", mm, reads=[B_wo, B_ob[s]], writes=[PB[pb]])
                        op("dve", (lambda oc=oc, pb=pb: lambda e: e.scalar_tensor_tensor(out=xw[sx][:, oc, :], in0=bank(pb, T), scalar=mcol(l, 2, b, oc), in1=xw[sx][:, oc, :], op0=ALU.mult, op1=ALU.add))(),
                           reads=[PB[pb], B_xw[sx][oc], B_modv], writes=[B_xw[sx][oc]])

                def POSTA(i, b, ti):
                    sx = i % 3
                    ln_reduce(lnr, [xw[sx][:, c, :] for c in range(NC8)], B_xw[sx])

                def POSTB(i, b, ti):
                    sx = i % 3
                    ln_finish(lnr, [xw[sx][:, c, :] for c in range(NC8)], B_xw[sx], l, 0, 6, 7)
                    op("sp", lambda e: e.dma_start(out=tile_src(dst, b, ti * T, T), in_=xw[sx][:]), reads=B_xw[sx], semkey=f"st{sx}")

                run_tiles(tiles, LOAD, PRE, MAIN, POSTA, POSTB)
                S_.flush()

        phases = []
        for l in layers:
            phases.append(("mix", l))
            phases.append(("ffn", l))
        if stop_after is not None:
            phases = phases[:stop_after]
        cur = xT
        pp = [sA, sB]
        for pi_, (kind, l) in enumerate(phases):
            dst = yT if pi_ == len(phases) - 1 else pp[pi_ % 2]
            if kind == "ffn":
                ffn_up_phase(l, cur)
                ffn_down_phase(l, cur, dst)
            elif l % 3 == 0:
                pool_phase(l, cur, dst)
            elif l % 3 == 1:
                mla_a1(l, cur)
                mla_a2()
                mla_a3(l, cur, dst)
            else:
                sconv_phase(l, cur, dst)
            cur = dst
    return nc


def host_weights(inp):
    w = {}
    f32 = lambda a: np.ascontiguousarray(np.asarray(a, dtype=np.float32))
    w["cvec"] = cvec_layout(inp).build()
    w["masks"] = make_masks()
    w["mod_w"] = f32(inp["mod_w"])
    w["pool_w"] = f32(inp["pool_w"])
    wa = np.asarray(inp["mla_w_a"][0], np.float32)
    w["w_a"] = f32(wa)
    w["w_a_sw"] = f32(np.concatenate([wa[:, 1040:1056], wa[:, 1024:1040]], axis=1))
    wuq = np.asarray(inp["mla_w_uq"][0], np.float32).reshape(768, HEADS, 96)
    w["w_uq_nope"] = f32(wuq[:, :, 0:64].reshape(768, 1024))
    w["w_uq_pe"] = f32(wuq[:, :, 64:96].reshape(768, 512))
    w["w_uq_pesw"] = f32(np.concatenate([wuq[:, :, 80:96], wuq[:, :, 64:80]], axis=2).reshape(768, 512))
    wukv = np.asarray(inp["mla_w_ukv"][0], np.float32).reshape(256, HEADS, 128)
    w["w_uk"] = f32(wukv[:, :, 0:64].reshape(256, 1024))
    w["w_uv"] = f32(wukv[:, :, 64:128].reshape(256, 1024))
    w["w_o"] = f32(inp["mla_w_o"][0])
    w["sc_w_in"] = f32(inp["sc_w_in"][0])
    w["sc_w_out"] = f32(inp["sc_w_out"][0])
    w["ffn_w_up"] = f32(inp["ffn_w_up"])
    w["ffn_w_down"] = f32(inp["ffn_w_down"])
    return w


def core_inputs(inp, w, b0, nseq, S):
    x = np.asarray(inp["x"], np.float32)[b0:b0 + nseq, :S]
    m = dict(w)
    m["xT"] = np.ascontiguousarray(x.transpose(0, 2, 1))
    c = np.asarray(inp["c"], np.float32)[b0:b0 + nseq]
    m["cT"] = np.ascontiguousarray(c.reshape(nseq, NC8, 128).transpose(2, 1, 0).reshape(128, NC8 * nseq))
    m["pos"] = np.ascontiguousarray(np.asarray(inp["positions"], np.int32)[b0:b0 + nseq, :S])
    return m


_PROG_CACHE = {}


def kernel(**inputs):
    B, S, _ = inputs["x"].shape
    ncores = 8
    nseq = B // ncores
    key = (nseq, S)
    if key not in _PROG_CACHE:
        _PROG_CACHE[key] = build_program(nseq, S)
    nc = _PROG_CACHE[key]
    w = host_weights(inputs)
    in_maps = [core_inputs(inputs, w, i * nseq, nseq, S) for i in range(ncores)]
    res = run_bass_kernel_spmd(nc, in_maps, core_ids=list(range(ncores)))
    out = np.empty((B, S, D), np.float32)
    for i in range(ncores):
        out[i * nseq:(i + 1) * nseq] = res.results[i]["yT"].transpose(0, 2, 1)
    return out
```

```python
import numpy as np
import ml_dtypes
import concourse.bass as bass
import concourse.mybir as mybir
from concourse.bass_utils import run_bass_kernel_spmd
from contextlib import ExitStack

F32 = mybir.dt.float32
BF16 = mybir.dt.bfloat16
I32 = mybir.dt.int32
AF = mybir.ActivationFunctionType
ALU = mybir.AluOpType

D = 1024
DEPTH = 4
NC8 = 8
FF = 2816
HC = 44
GC = 22
HEADS = 16
ALPHA = float((2 * DEPTH) ** 0.25)
LN_EPS = 1e-5
RMS_EPS = 1e-6
SM_SCALE = float(96 ** -0.5)


class Buf:
    __slots__ = ("name", "w", "r")

    def __init__(self, name=""):
        self.name = name
        self.w = None
        self.r = []


def bufs(n, name=""):
    return [Buf(f"{name}{i}") for i in range(n)]


class Op:
    __slots__ = ("eng", "fn", "deps", "dwaits", "sig", "idx", "semkey", "phase")


class Sched:
    ENGS = ("pe", "act", "dve", "pool", "sp")

    def __init__(self, nc, top, same_engine_sync=False):
        self.nc = nc
        self.top = top
        self.ops = {e: [] for e in self.ENGS}
        self.last_real = {e: None for e in self.ENGS}
        self.dma_cnt = {}
        self.same_engine_sync = same_engine_sync
        self.phase = 0
        self.esem = None
        self.dsem = {}
        self.sigbase = {e: 0 for e in self.ENGS}
        self.waited = {e: {} for e in self.ENGS}
        self.nops = 0

    def op(self, eng, fn, reads=(), writes=(), semkey=None):
        o = Op()
        o.eng = eng
        o.fn = fn
        o.deps = set()
        o.dwaits = {}
        o.sig = False
        o.idx = 0
        o.semkey = semkey
        o.phase = self.phase
        for b in reads:
            if b.w is not None:
                self._dep(o, b.w)
        for b in writes:
            if b.w is not None:
                self._dep(o, b.w)
            for r in b.r:
                self._dep(o, r)
        for b in reads:
            b.r.append(o)
        for b in writes:
            b.w = o
            b.r = []
        if semkey is not None:
            self.dma_cnt[semkey] = self.dma_cnt.get(semkey, 0) + 1
        self.ops[eng].append(o)
        self.last_real[eng] = o
        self.nops += 1
        return o

    def _dep(self, o, d):
        if d is o:
            return
        if d.phase != self.phase:
            return
        if d.semkey is not None:
            k = d.semkey
            o.dwaits[k] = max(o.dwaits.get(k, 0), self.dma_cnt[k])
            return
        if d.eng == o.eng and (d.eng == "pe" or not self.same_engine_sync):
            return
        d.sig = True
        o.deps.add(d)

    def barrier(self):
        lasts = [self.last_real[e] for e in self.ENGS if self.last_real[e] is not None and self.last_real[e].phase == self.phase]
        for e in self.ENGS:
            o = Op()
            o.eng = e
            o.fn = None
            o.deps = set()
            o.dwaits = dict(self.dma_cnt)
            o.sig = False
            o.idx = 0
            o.semkey = None
            o.phase = self.phase
            for d in lasts:
                if d.semkey is not None or d.eng == e:
                    continue
                d.sig = True
                o.deps.add(d)
            self.ops[e].append(o)

    def flush(self):
        nc = self.nc
        self.barrier()
        if self.esem is None:
            self.esem = {e: self.top.enter_context(nc.semaphore("s_" + e)) for e in self.ENGS if e != "sp"}
        for k in self.dma_cnt:
            if k not in self.dsem:
                self.dsem[k] = self.top.enter_context(nc.semaphore("d_" + str(k)))
        esem, dsem = self.esem, self.dsem
        for e in self.ENGS:
            c = self.sigbase[e]
            for o in self.ops[e]:
                if o.sig:
                    c += 1
                    o.idx = c
            self.sigbase[e] = c
        ops = self.ops
        waited_all = self.waited

        def run(engname):
            def body(eng):
                waited = waited_all[engname]
                for o in ops[engname]:
                    ws = {}
                    for d in o.deps:
                        s = esem[d.eng]
                        if ws.get(s, 0) < d.idx:
                            ws[s] = d.idx
                    for k, v in o.dwaits.items():
                        s = dsem[k]
                        if ws.get(s, 0) < 16 * v:
                            ws[s] = 16 * v
                    for s, v in ws.items():
                        if waited.get(s, 0) < v:
                            eng.wait_ge(s, v)
                            waited[s] = v
                    if o.fn is None:
                        continue
                    ins = o.fn(eng)
                    if o.semkey is not None:
                        ins.then_inc(dsem[o.semkey], 16)
                    elif o.sig:
                        ins.then_inc(esem[engname], 1)
            return body

        with nc.Block() as block:
            block.tensor(run("pe"))
            block.scalar(run("act"))
            block.vector(run("dve"))
            block.gpsimd(run("pool"))
            block.sync(run("sp"))
        self.ops = {e: [] for e in self.ENGS}
        self.phase += 1


def colpack(vec):
    v = np.asarray(vec, dtype=np.float32).reshape(-1, 128)
    return np.ascontiguousarray(v.T)


class ColTable:
    def __init__(self):
        self.blocks = []
        self.off = {}
        self.n = 0

    def add(self, name, arr):
        arr = np.asarray(arr, dtype=np.float32)
        assert arr.shape[0] == 128
        self.off[name] = self.n
        self.blocks.append(arr)
        self.n += arr.shape[1]

    def build(self):
        return np.ascontiguousarray(np.concatenate(self.blocks, axis=1))


def cvec_layout(inp=None):
    ct = ColTable()
    z = lambda n: np.zeros((n,), np.float32)
    g = (lambda k, n: inp[k]) if inp is not None else None
    for l in range(DEPTH):
        for j in range(2):
            ct.add(f"lng{l}_{j}", colpack(inp["ln_g"][l, j] if inp else z(D)))
            ct.add(f"lnb{l}_{j}", colpack(inp["ln_b"][l, j] if inp else z(D)))
        ct.add(f"modb{l}", colpack(inp["mod_b"][l] if inp else z(6 * D)))
        for k in range(3):
            ct.add(f"fcw{l}_{k}", colpack(inp["ffn_conv"][l, k] if inp else z(2 * FF)))
        ct.add(f"fcb{l}", colpack(inp["ffn_conv_b"][l] if inp else z(2 * FF)))
    for j in range(2):
        ct.add(f"pscale{j}", colpack(inp["pool_scale"][j] if inp else z(D)))
    ct.add("qnorm", colpack(inp["mla_q_norm"][0] if inp else z(768)))
    ct.add("kvnorm", colpack(inp["mla_kv_norm"][0] if inp else z(256)))
    for k in range(3):
        ct.add(f"scw{k}", colpack(inp["sc_conv"][0, k] if inp else z(D)))
    p = np.arange(128)
    invf = (10000.0 ** (-np.arange(0, 32, 2, dtype=np.float32) / 32)).astype(np.float32)
    ct.add("invf", invf[p % 16].reshape(128, 1))
    ct.add("sgn", np.where((p % 32) < 16, -1.0, 1.0).astype(np.float32).reshape(128, 1))
    corr = np.zeros((128, 64), np.float32)
    for gi, w in enumerate((2, 4, 8, 16)):
        t = np.arange(16)
        corr[:, gi * 16:(gi + 1) * 16] = (w / np.minimum(t + 1, w)).astype(np.float32)[None, :]
    ct.add("corr", corr)
    ct.add("eps_ln", np.full((128, 1), LN_EPS, np.float32))
    ct.add("eps_rms", np.full((128, 1), RMS_EPS, np.float32))
    return ct


def make_masks():
    k = np.arange(128)[:, None]
    q = np.arange(512)[None, :]
    m = np.stack([((d * 128 + k) <= q) for d in range(4)], axis=1)
    return np.ascontiguousarray(m.astype(np.float32).astype(ml_dtypes.bfloat16))


def build_program(nseq, S, layers=(0, 1, 2, 3), stop_after=None):
    nc = bass.Bass("TRN2", target_bir_lowering=False)
    NCV = cvec_layout().n
    coff = cvec_layout().off

    def din(name, shape, dt=F32):
        return nc.dram_tensor(name, list(shape), dt, kind="ExternalInput").ap()

    def dscr(name, shape, dt=F32):
        return nc.dram_tensor(name, list(shape), dt).ap()

    xT = din("xT", [nseq, D, S])
    cT = din("cT", [128, NC8 * nseq])
    pos = din("pos", [nseq, S], I32)
    cvec_d = din("cvec", [128, NCV])
    masks_d = din("masks", [128, 4, 512], BF16)
    mod_w = din("mod_w", [DEPTH, D, 6 * D])
    pool_w = din("pool_w", [2, 4, 256, 256])
    w_a = din("w_a", [D, 1056])
    w_a_sw = din("w_a_sw", [D, 32])
    w_uq_nope = din("w_uq_nope", [768, 1024])
    w_uq_pe = din("w_uq_pe", [768, 512])
    w_uq_pesw = din("w_uq_pesw", [768, 512])
    w_uk = din("w_uk", [256, 1024])
    w_uv = din("w_uv", [256, 1024])
    w_o = din("w_o", [D, D])
    sc_w_in = din("sc_w_in", [D, 3 * D])
    sc_w_out = din("sc_w_out", [D, D])
    ffn_w_up = din("ffn_w_up", [DEPTH, D, 2 * FF])
    ffn_w_down = din("ffn_w_down", [DEPTH, FF, D])
    yT = nc.dram_tensor("yT", [nseq, D, S], F32, kind="ExternalOutput").ap()
    sA = dscr("sA", [nseq, D, S])
    sB = dscr("sB", [nseq, D, S])
    cqn_d = dscr("cqn_d", [nseq, 768, S], BF16)
    ckvn_d = dscr("ckvn_d", [nseq, 256, S], BF16)
    kpe_d = dscr("kpe_d", [nseq, 32, S], BF16)
    qpe_d = dscr("qpe_d", [nseq, 512, S], BF16)
    oT_d = dscr("oT_d", [nseq, D, S], BF16)
    g_d = dscr("g_d", [nseq, FF, S], BF16)

    top = ExitStack()
    S_ = Sched(nc, top)
    op = S_.op
    with top:
        uid = [0]

        def sbt(es, name, shape, dt):
            uid[0] += 1
            return es.enter_context(nc.sbuf_tensor(f"{name}_u{uid[0]}", list(shape), dt))

        cv = sbt(top, "cv", [128, NCV], F32)
        modv = sbt(top, "modv", [128, DEPTH, 6, nseq, NC8], F32)
        ones_ln = sbt(top, "ones_ln", [128, 128], F32)
        ones_q = sbt(top, "ones_q", [128, 128], F32)
        ones_kv = sbt(top, "ones_kv", [128, 128], F32)
        psum = top.enter_context(nc.psum_tensor("psum", [128, 8 * 512], F32))
        PB = bufs(8, "psb")
        B_cv = Buf("cv")
        B_modv = {l_: Buf(f"modv{l_}") for l_ in range(DEPTH)}
        B_ones = Buf("ones")

        def bank(b, n=512):
            return psum[:, b * 512:b * 512 + n]

        def col(name, i=0):
            o_ = coff[name] + i
            return cv[:, o_:o_ + 1]

        op("sp", lambda e: e.dma_start(out=cv[:], in_=cvec_d), writes=[B_cv], semkey="cv")
        op("pool", lambda e: e.memset(ones_ln[:], 1.0 / D), writes=[B_ones])
        op("pool", lambda e: e.memset(ones_q[:], 1.0 / 768), writes=[B_ones])
        op("pool", lambda e: e.memset(ones_kv[:], 1.0 / 256), writes=[B_ones])

        cond = sbt(top, "cond", [128, NC8 * nseq], F32)
        B_cond = Buf("cond")
        op("sp", lambda e: e.dma_start(out=cond[:], in_=cT), writes=[B_cond], semkey="cond")
        op("act", lambda e: e.activation(out=cond[:], in_=cond[:], func=AF.Silu), reads=[B_cond], writes=[B_cond])
        mod_state = {"pi": 0}

        def mod_piece(l, j, mw, B_mw, pb0):
            pi = mod_state["pi"]
            mod_state["pi"] += 1
            s = pi % 2
            pb = pb0 + pi % 2
            src = mod_w[l].rearrange("(kc p) n -> p kc n", p=128)[:, :, j * D:(j + 1) * D]
            op("sp", lambda e: e.dma_start(out=mw[s][:], in_=src), writes=[B_mw[s]], semkey=f"mw{s}")
            for oc in range(NC8):
                def mm(e, oc=oc):
                    ins = None
                    for kc in range(NC8):
                        ins = e.matmul(bank(pb)[:, oc * nseq:(oc + 1) * nseq], lhsT=mw[s][:, kc, oc * 128:(oc + 1) * 128],
                                       rhs=cond[:, kc * nseq:(kc + 1) * nseq], start=(kc == 0), stop=(kc == NC8 - 1))
                    return ins
                op("pe", mm, reads=[B_mw[s], B_cond], writes=[PB[pb]])
            for b in range(nseq):
                def ev(e, b=b):
                    src_ = bank(pb)[:, 0:NC8 * nseq].rearrange("p (o b) -> p o b", b=nseq)[:, :, b]
                    mb = cv[:, coff[f"modb{l}"] + j * NC8: coff[f"modb{l}"] + (j + 1) * NC8]
                    return e.tensor_tensor(out=modv[:, l, j, b, :], in0=src_, in1=mb, op=ALU.add)
                op("dve", ev, reads=[PB[pb], B_cv], writes=[B_modv[l]])
            if j == 5:
                for jj in (1, 2, 4, 5):
                    for b in range(nseq):
                        op("dve", (lambda jj=jj, b=b: lambda e: e.tensor_scalar(out=modv[:, l, jj, b, :], in0=modv[:, l, jj, b, :], scalar1=1.0, scalar2=None, op0=ALU.add))(),
                           reads=[B_modv[l]], writes=[B_modv[l]])
                if l % 3 == 0:
                    pj = l // 3
                    for b in range(nseq):
                        def gs(e, b=b, pj=pj):
                            ps_ = cv[:, coff[f"pscale{pj}"]: coff[f"pscale{pj}"] + NC8]
                            return e.tensor_tensor(out=modv[:, l, 2, b, :], in0=modv[:, l, 2, b, :], in1=ps_, op=ALU.mult)
                        op("dve", gs, reads=[B_modv[l], B_cv], writes=[B_modv[l]])

        overlap_mod = (layers[0] % 3 == 0 and len(layers) > 1)
        pre_layers = [layers[0]] if overlap_mod else list(layers)
        deferred = [(l_, j_) for l_ in layers[1:] for j_ in range(6)] if overlap_mod else []
        with ExitStack() as es:
            mw = [sbt(es, f"mw{i}", [128, NC8, D], F32) for i in range(2)]
            B_mw = bufs(2, "mw")
            for l_ in pre_layers:
                for j_ in range(6):
                    mod_piece(l_, j_, mw, B_mw, 0)
            S_.flush()

        def mcol(l, j, b, c):
            return modv[:, l, j, b, c:c + 1]

        class LNRes:
            pass

        def ln_alloc(es, T):
            r = LNRes()
            r.sq = [sbt(es, f"ln_sq{i}", [128, T], F32) for i in range(2)]
            r.zs = sbt(es, "ln_zs", [128, T], F32)
            r.zq = sbt(es, "ln_zq", [128, T], F32)
            r.mean = sbt(es, "ln_mean", [128, T], F32)
            r.rstd = sbt(es, "ln_rstd", [128, T], F32)
            r.B_sq = bufs(2, "lnsq")
            r.B_zs = Buf("lnzs")
            r.B_zq = Buf("lnzq")
            r.B_mean = Buf("lnmean")
            r.B_rstd = Buf("lnrstd")
            r.T = T
            return r

        def ln_reduce(r, zc, Bz):
            op("pool", lambda e: e.tensor_tensor(out=r.zs[:], in0=zc[0], in1=zc[1], op=ALU.add), reads=[Bz[0], Bz[1]], writes=[r.B_zs])
            for c in range(2, NC8):
                op("pool", (lambda c=c: lambda e: e.tensor_tensor(out=r.zs[:], in0=r.zs[:], in1=zc[c], op=ALU.add))(), reads=[Bz[c], r.B_zs], writes=[r.B_zs])
            for c in range(NC8):
                s = c % 2
                op("act", (lambda c=c, s=s: lambda e: e.activation(out=r.sq[s][:], in_=zc[c], func=AF.Square))(), reads=[Bz[c]], writes=[r.B_sq[s]])
                if c == 1:
                    op("pool", lambda e: e.tensor_tensor(out=r.zq[:], in0=r.sq[0][:], in1=r.sq[1][:], op=ALU.add), reads=[r.B_sq[0], r.B_sq[1]], writes=[r.B_zq])
                elif c >= 2:
                    op("pool", (lambda s=s: lambda e: e.tensor_tensor(out=r.zq[:], in0=r.zq[:], in1=r.sq[s][:], op=ALU.add))(), reads=[r.B_sq[s], r.B_zq], writes=[r.B_zq])

        def ln_finish(r, zc, Bz, l, j, pb_m, pb_q):
            T = r.T
            op("pe", lambda e: e.matmul(bank(pb_m, T), lhsT=ones_ln[:], rhs=r.zs[:], start=True, stop=True), reads=[r.B_zs, B_ones], writes=[PB[pb_m]])
            op("pe", lambda e: e.matmul(bank(pb_q, T), lhsT=ones_ln[:], rhs=r.zq[:], start=True, stop=True), reads=[r.B_zq, B_ones], writes=[PB[pb_q]])
            op("act", lambda e: e.activation(out=r.mean[:], in_=bank(pb_m, T), func=AF.Copy), reads=[PB[pb_m]], writes=[r.B_mean])
            op("dve", lambda e: e.tensor_tensor(out=r.rstd[:], in0=r.mean[:], in1=r.mean[:], op=ALU.mult), reads=[r.B_mean], writes=[r.B_rstd])
            op("dve", lambda e: e.tensor_tensor(out=r.rstd[:], in0=bank(pb_q, T), in1=r.rstd[:], op=ALU.subtract), reads=[PB[pb_q], r.B_rstd], writes=[r.B_rstd])
            op("dve", lambda e: e.tensor_scalar(out=r.rstd[:], in0=r.rstd[:], scalar1=0.0, scalar2=None, op0=ALU.max), reads=[r.B_rstd], writes=[r.B_rstd])
            op("act", lambda e: e.activation(out=r.rstd[:], in_=r.rstd[:], func=AF.Sqrt, bias=col("eps_ln")), reads=[r.B_rstd, B_cv], writes=[r.B_rstd])
            op("dve", lambda e: e.reciprocal(out=r.rstd[:], in_=r.rstd[:]), reads=[r.B_rstd], writes=[r.B_rstd])
            for c in range(NC8):
                op("dve", (lambda c=c: lambda e: e.tensor_tensor(out=zc[c], in0=zc[c], in1=r.mean[:], op=ALU.subtract))(), reads=[Bz[c], r.B_mean], writes=[Bz[c]])
                op("dve", (lambda c=c: lambda e: e.tensor_tensor(out=zc[c], in0=zc[c], in1=r.rstd[:], op=ALU.mult))(), reads=[Bz[c], r.B_rstd], writes=[Bz[c]])
                op("act", (lambda c=c: lambda e: e.activation(out=zc[c], in_=zc[c], func=AF.Identity, scale=col(f"lng{l}_{j}", c), bias=col(f"lnb{l}_{j}", c)))(),
                   reads=[Bz[c], B_cv], writes=[Bz[c]])

        def tile_src(t, b, t0, T):
            return t[b].rearrange("(c p) s -> p c s", p=128)[:, :, t0:t0 + T]

        def run_tiles(tiles, LOAD, PRE, MAIN, POSTA=None, POSTB=None):
            n = len(tiles)
            LOAD(0, *tiles[0])
            PRE(0, *tiles[0])
            for i in range(n):
                if i + 1 < n:
                    LOAD(i + 1, *tiles[i + 1])
                done = [False]

                def pre_next(i=i, done=done):
                    if not done[0] and i + 1 < n:
                        PRE(i + 1, *tiles[i + 1])
                    done[0] = True
                MAIN(i, *tiles[i], pre_next)
                pre_next()
                if POSTB is not None and i > 0:
                    POSTB(i - 1, *tiles[i - 1])
                if POSTA is not None:
                    POSTA(i, *tiles[i])
            if POSTB is not None:
                POSTB(n - 1, *tiles[n - 1])

        def ffn_up_phase(l, src):
            T = 512 if S >= 512 else S
            NT = S // T
            HG = GC // 2
            with ExitStack() as es:
                wup = sbt(es, "wup", [128, NC8, 2 * FF], BF16)
                xin = [sbt(es, f"xin{i}", [128, NC8, T], F32) for i in range(2)]
                ub = [sbt(es, f"ub{i}", [128, NC8, T], BF16) for i in range(2)]
                gb = [sbt(es, f"gb{i}", [128, HG, T], BF16) for i in range(2)]
                NAB = 8
                ab = [sbt(es, f"ab{i}", [128, T], F32) for i in range(NAB)]
                sg = [sbt(es, f"sg{i}", [128, T], F32) for i in range(2)]
                tb = [sbt(es, f"tb{i}", [128, T], F32) for i in range(2)]
                B_tb = bufs(2, "tb")
                tails = [sbt(es, f"tails{i}", [128, HC, 2], F32) for i in range(2)]
                NWU = 4
                B_wup = bufs(NWU, "wup")
                B_xin = [bufs(NC8, f"xin{i}_") for i in range(2)]
                B_ub = [bufs(NC8, f"ub{i}_") for i in range(2)]
                B_gb = [Buf("gb0"), Buf("gb1")]
                B_ab = bufs(NAB, "ab")
                B_sg = bufs(2, "sg")
                B_tails = [bufs(HC, "tails0_"), bufs(HC, "tails1_")]
                wsrc = ffn_w_up[l].rearrange("(kc p) n -> p kc n", p=128)
                cw = 2 * FF // NWU
                for i in range(NWU):
                    op("pool", (lambda i=i: lambda e: e.dma_start(out=wup[:, :, i * cw:(i + 1) * cw], in_=wsrc[:, :, i * cw:(i + 1) * cw]))(), writes=[B_wup[i]], semkey=f"wA{i}")
                tiles = [(b, ti) for b in range(nseq) for ti in range(NT)]

                def LOAD(i, b, ti):
                    s = i % 2
                    op("sp", lambda e: e.dma_start(out=xin[s][:], in_=tile_src(src, b, ti * T, T)), writes=B_xin[s], semkey=f"ld{s}")

                def PRE(i, b, ti):
                    s = i % 2
                    for c in range(NC8):
                        op("dve", (lambda c=c: lambda e: e.tensor_scalar(out=ub[s][:, c, :], in0=xin[s][:, c, :], scalar1=mcol(l, 4, b, c), scalar2=mcol(l, 3, b, c), op0=ALU.mult, op1=ALU.add))(),
                           reads=[B_xin[s][c], B_modv[l]], writes=[B_ub[s][c]])

                def MAIN(i, b, ti, pre_next):
                    s = i % 2
                    par = i % 2
                    t0 = ti * T
                    if ti == 0:
                        op("pool", lambda e: e.memset(tails[1 - par][:], 0.0), writes=B_tails[1 - par])
                    pend = None

                    def glu(p, av, ag):
                        ss = p % 2
                        hf = p // HG
                        op("act", lambda e: e.activation(out=sg[ss][:], in_=ab[ag][:], func=AF.Silu), reads=[B_ab[ag]], writes=[B_sg[ss]])
                        op("pool", lambda e: e.tensor_tensor(out=gb[hf][:, p - hf * HG, :], in0=ab[av][:], in1=sg[ss][:], op=ALU.mult), reads=[B_ab[av], B_sg[ss]], writes=[B_gb[hf]])
                        if p % HG == HG - 1:
                            op("sp", lambda e: e.dma_start(out=g_d[b].rearrange("(c p) s -> p c s", p=128)[:, hf * HG:(hf + 1) * HG, t0:t0 + T], in_=gb[hf][:]), reads=[B_gb[hf]], semkey=f"sg{hf}")

                    for p in range(GC):
                        slots = []
                        for half in range(2):
                            c = p + half * GC
                            pb = (2 * p + half) % 6
                            a = (2 * p + half) % NAB
                            slots.append(a)

                            def mm(e, c=c, pb=pb):
                                ins = None
                                for kc in range(NC8):
                                    ins = e.matmul(bank(pb, T), lhsT=wup[:, kc, c * 128:(c + 1) * 128], rhs=ub[s][:, kc, :], start=(kc == 0), stop=(kc == NC8 - 1))
                                return ins
                            op("pe", mm, reads=[B_wup[c * 128 // cw]] + B_ub[s], writes=[PB[pb]])
                            op("act", (lambda c=c, pb=pb, a=a: lambda e: e.activation(out=ab[a][:], in_=bank(pb, T), func=AF.Identity, scale=col(f"fcw{l}_2", c), bias=col(f"fcb{l}", c)))(),
                               reads=[PB[pb], B_cv], writes=[B_ab[a]])
                            op("act", (lambda c=c, pb=pb: lambda e: e.activation(out=tails[par][:, c, :], in_=bank(pb, T)[:, T - 2:T], func=AF.Copy))(), reads=[PB[pb]], writes=[B_tails[par][c]])
                            if half == 0:
                                tt_ = p % 2
                                op("act", (lambda c=c, pb=pb, tt_=tt_: lambda e: e.activation(out=tb[tt_][:, 1:T], in_=bank(pb, T)[:, 0:T - 1], func=AF.Identity, scale=col(f"fcw{l}_1", c)))(),
                                   reads=[PB[pb], B_cv], writes=[B_tb[tt_]])
                                op("act", (lambda c=c, tt_=tt_: lambda e: e.activation(out=tb[tt_][:, 0:1], in_=tails[1 - par][:, c, 1:2], func=AF.Identity, scale=col(f"fcw{l}_1", c)))(),
                                   reads=[B_tails[1 - par][c], B_cv], writes=[B_tb[tt_]])
                                op("dve", (lambda c=c, pb=pb, a=a: lambda e: e.scalar_tensor_tensor(out=ab[a][:, 2:T], in0=bank(pb, T)[:, 0:T - 2], scalar=col(f"fcw{l}_0", c), in1=ab[a][:, 2:T], op0=ALU.mult, op1=ALU.add))(),
                                   reads=[PB[pb], B_ab[a], B_cv, B_tb[tt_]], writes=[B_ab[a]])
                                op("dve", (lambda c=c, a=a: lambda e: e.scalar_tensor_tensor(out=ab[a][:, 0:2], in0=tails[1 - par][:, c, 0:2], scalar=col(f"fcw{l}_0", c), in1=ab[a][:, 0:2], op0=ALU.mult, op1=ALU.add))(),
                                   reads=[B_tails[1 - par][c], B_ab[a], B_cv], writes=[B_ab[a]])
                                op("pool", (lambda a=a, tt_=tt_: lambda e: e.tensor_tensor(out=ab[a][:], in0=ab[a][:], in1=tb[tt_][:], op=ALU.add))(), reads=[B_ab[a], B_tb[tt_]], writes=[B_ab[a]])
                            else:
                                op("dve", (lambda c=c, pb=pb, a=a: lambda e: e.scalar_tensor_tensor(out=ab[a][:, 1:T], in0=bank(pb, T)[:, 0:T - 1], scalar=col(f"fcw{l}_1", c), in1=ab[a][:, 1:T], op0=ALU.mult, op1=ALU.add))(),
                                   reads=[PB[pb], B_ab[a], B_cv, B_tails[par][c]], writes=[B_ab[a]])
                                op("dve", (lambda c=c, pb=pb, a=a: lambda e: e.scalar_tensor_tensor(out=ab[a][:, 2:T], in0=bank(pb, T)[:, 0:T - 2], scalar=col(f"fcw{l}_0", c), in1=ab[a][:, 2:T], op0=ALU.mult, op1=ALU.add))(),
                                   reads=[PB[pb], B_ab[a], B_cv], writes=[B_ab[a]])
                                op("dve", (lambda c=c, a=a: lambda e: e.scalar_tensor_tensor(out=ab[a][:, 0:1], in0=tails[1 - par][:, c, 1:2], scalar=col(f"fcw{l}_1", c), in1=ab[a][:, 0:1], op0=ALU.mult, op1=ALU.add))(),
                                   reads=[B_tails[1 - par][c], B_ab[a], B_cv], writes=[B_ab[a]])
                                op("dve", (lambda c=c, a=a: lambda e: e.scalar_tensor_tensor(out=ab[a][:, 0:2], in0=tails[1 - par][:, c, 0:2], scalar=col(f"fcw{l}_0", c), in1=ab[a][:, 0:2], op0=ALU.mult, op1=ALU.add))(),
                                   reads=[B_tails[1 - par][c], B_ab[a], B_cv], writes=[B_ab[a]])
                        if pend is not None:
                            glu(*pend)
                        pend = (p, slots[0], slots[1])
                        if p == 14:
                            pre_next()
                    glu(*pend)

                run_tiles(tiles, LOAD, PRE, MAIN)
                S_.flush()

        def ffn_down_phase(l, src, dst):
            T = 512 if S >= 512 else S
            NT = S // T
            with ExitStack() as es:
                wdn = sbt(es, "wdn", [128, GC, D], BF16)
                zb = [sbt(es, f"zb{i}", [128, NC8, T], F32) for i in range(3)]
                gt = [sbt(es, f"gt{i}", [128, GC, T], BF16) for i in range(2)]
                lnr = ln_alloc(es, T)
                B_wdn = Buf("wdn")
                B_zb = [bufs(NC8, f"zb{i}_") for i in range(3)]
                B_gt = [Buf("gt0"), Buf("gt1")]
                op("pool", lambda e: e.dma_start(out=wdn[:], in_=ffn_w_down[l].rearrange("(kc p) n -> p kc n", p=128)), writes=[B_wdn], semkey="wB0")
                tiles = [(b, ti) for b in range(nseq) for ti in range(NT)]

                def LOAD(i, b, ti):
                    s = i % 2
                    sx = i % 3
                    op("sp", lambda e: e.dma_start(out=gt[s][:], in_=tile_src(g_d, b, ti * T, T)), writes=[B_gt[s]], semkey=f"lo{s}")
                    op("sp", lambda e: e.dma_start(out=zb[sx][:], in_=tile_src(src, b, ti * T, T)), writes=B_zb[sx], semkey=f"ld{sx}")

                def PRE(i, b, ti):
                    s = i % 2
                    sx = i % 3
                    for c in range(NC8):
                        op("act", (lambda c=c: lambda e: e.activation(out=zb[sx][:, c, :], in_=zb[sx][:, c, :], func=AF.Identity, scale=ALPHA))(), reads=[B_zb[sx][c]], writes=[B_zb[sx][c]])

                def MAIN(i, b, ti, pre_next):
                    s = i % 2
                    sx = i % 3
                    for oc in range(NC8):
                        pb = oc % 4

                        def mm2(e, oc=oc, pb=pb):
                            ins = None
                            for c in range(GC):
                                ins = e.matmul(bank(pb, T), lhsT=wdn[:, c, oc * 128:(oc + 1) * 128], rhs=gt[s][:, c, :], start=(c == 0), stop=(c == GC - 1))
                            return ins
                        op("pe", mm2, reads=[B_wdn, B_gt[s]], writes=[PB[pb]])
                        op("dve", (lambda oc=oc, pb=pb: lambda e: e.scalar_tensor_tensor(out=zb[sx][:, oc, :], in0=bank(pb, T), scalar=mcol(l, 5, b, oc), in1=zb[sx][:, oc, :], op0=ALU.mult, op1=ALU.add))(),
                           reads=[PB[pb], B_zb[sx][oc], B_modv[l]], writes=[B_zb[sx][oc]])

                def POSTA(i, b, ti):
                    sx = i % 3
                    ln_reduce(lnr, [zb[sx][:, c, :] for c in range(NC8)], B_zb[sx])

                def POSTB(i, b, ti):
                    sx = i % 3
                    ln_finish(lnr, [zb[sx][:, c, :] for c in range(NC8)], B_zb[sx], l, 1, 6, 7)
                    op("sp", lambda e: e.dma_start(out=tile_src(dst, b, ti * T, T), in_=zb[sx][:]), reads=B_zb[sx], semkey=f"st{sx}")

                run_tiles(tiles, LOAD, PRE, MAIN, POSTA, POSTB)
                S_.flush()

        def pool_phase(l, src, dst):
            T = 512 if S >= 512 else S
            NT = S // T
            H = 16
            E = H + T
            pj = l // 3
            with ExitStack() as es:
                pw = sbt(es, "pw", [128, 4, 2, 256], BF16)
                xw = [sbt(es, f"xw{i}", [128, NC8, H + T], F32) for i in range(3)]
                uw = sbt(es, "uw", [128, NC8, H + T], F32)
                scr = [sbt(es, f"pscr{i}", [128, 2, H + T], F32) for i in range(2)]
                scrP = [sbt(es, f"pscrP{i}", [128, 2, H + T], F32) for i in range(2)]
                B_scrP = bufs(2, "pscrP")
                pl = [sbt(es, f"pl{i}", [128, NC8, T], BF16) for i in range(2)]
                lnr = ln_alloc(es, T)
                if deferred:
                    mwp = [sbt(es, f"mwp{i}", [128, NC8, D], F32) for i in range(2)]
                    B_mwp = bufs(2, "mwp")
                B_pw = Buf("pw")
                B_xw = [bufs(NC8, f"xw{i}_") for i in range(3)]
                B_uw = bufs(NC8, "uw")
                B_scr = bufs(2, "pscr")
                B_pl = [bufs(NC8, f"pl{i}_") for i in range(2)]
                op("pool", lambda e: e.dma_start(out=pw[:], in_=pool_w[pj].rearrange("g (kc p) n -> p g kc n", p=128)), writes=[B_pw], semkey="wA0")
                tiles = [(b, ti) for b in range(nseq) for ti in range(NT)]

                def LOAD(i, b, ti):
                    s = i % 2
                    sx = i % 3
                    t0 = ti * T
                    if ti == 0:
                        op("sp", lambda e: e.dma_start(out=xw[sx][:, :, H:H + T], in_=tile_src(src, b, t0, T)), writes=B_xw[sx], semkey=f"ld{sx}")
                    else:
                        op("sp", lambda e: e.dma_start(out=xw[sx][:], in_=tile_src(src, b, t0 - H, T + H)), writes=B_xw[sx], semkey=f"ld{sx}")

                def PRE(i, b, ti):
                    s = i % 2
                    sx = i % 3
                    lo = H if ti == 0 else 0
                    for c in range(NC8):
                        if ti == 0:
                            op("pool", (lambda c=c: lambda e: e.memset(uw[:, c, 0:H], 0.0))(), writes=[B_uw[c]])
                        op("act", (lambda c=c: lambda e: e.activation(out=uw[:, c, lo:E], in_=xw[sx][:, c, lo:E], func=AF.Identity, scale=mcol(l, 1, b, c), bias=mcol(l, 0, b, c)))(),
                           reads=[B_xw[sx][c], B_modv[l]], writes=[B_uw[c]])
                    for c in range(NC8):
                        op("act", (lambda c=c: lambda e: e.activation(out=xw[sx][:, c, H:E], in_=xw[sx][:, c, H:E], func=AF.Identity, scale=ALPHA))(), reads=[B_xw[sx][c]], writes=[B_xw[sx][c]])
                    for g in (3, 0, 1, 2):
                        w = 2 << g
                        cs = slice(2 * g, 2 * g + 2)
                        Bu = [B_uw[2 * g], B_uw[2 * g + 1]]
                        starts = {0: [16], 1: [14, 16], 2: [10, 12, 16], 3: [2, 4, 8, 16]}[g]
                        cur = None
                        SC, BSC, weng = (scrP, B_scrP, "pool") if g == 3 else (scr, B_scr, "dve")
                        for lvl, st in enumerate(starts):
                            sh = 1 << lvl
                            dsti = lvl % 2

                            def lv(e, cur=cur, dsti=dsti, st=st, sh=sh, cs=cs, SC=SC):
                                srcT = uw[:, cs, :] if cur is None else SC[cur][:, :, :]
                                return e.tensor_tensor(out=SC[dsti][:, :, st:E], in0=srcT[:, :, st:E], in1=srcT[:, :, st - sh:E - sh], op=ALU.add)
                            op(weng, lv, reads=(Bu if cur is None else [BSC[cur]]), writes=[BSC[dsti]])
                            cur = dsti
                        if ti == 0:
                            def cr(e, cur=cur, g=g, SC=SC):
                                cc = cv[:, coff["corr"] + g * 16: coff["corr"] + (g + 1) * 16]
                                ins = None
                                for k in range(2):
                                    ins = e.tensor_tensor(out=SC[cur][:, k, H:H + 16], in0=SC[cur][:, k, H:H + 16], in1=cc, op=ALU.mult)
                                return ins
                            op("dve", cr, reads=[BSC[cur], B_cv], writes=[BSC[cur]])
                        op("dve", (lambda cur=cur, cs=cs, w=w, SC=SC: lambda e: e.scalar_tensor_tensor(out=pl[s][:, cs, :], in0=SC[cur][:, :, H:E], scalar=1.0 / w, in1=uw[:, cs, H:E], op0=ALU.mult, op1=ALU.subtract))(),
                           reads=[BSC[cur]] + Bu, writes=[B_pl[s][2 * g], B_pl[s][2 * g + 1]])

                def MAIN(i, b, ti, pre_next):
                    s = i % 2
                    sx = i % 3
                    for _ in range(2):
                        if deferred:
                            mod_piece(*deferred.pop(0), mwp, B_mwp, 4)
                    for g in range(4):
                        for oc in range(2):
                            c = 2 * g + oc
                            pb = c % 4

                            def mm(e, g=g, oc=oc, pb=pb):
                                ins = None
                                for kc in range(2):
                                    ins = e.matmul(bank(pb, T), lhsT=pw[:, g, kc, oc * 128:(oc + 1) * 128], rhs=pl[s][:, 2 * g + kc, :], start=(kc == 0), stop=(kc == 1))
                                return ins
                            op("pe", mm, reads=[B_pw, B_pl[s][2 * g], B_pl[s][2 * g + 1]], writes=[PB[pb]])
                            op("dve", (lambda c=c, pb=pb: lambda e: e.scalar_tensor_tensor(out=xw[sx][:, c, H:E], in0=bank(pb, T), scalar=mcol(l, 2, b, c), in1=xw[sx][:, c, H:E], op0=ALU.mult, op1=ALU.add))(),
                               reads=[PB[pb], B_xw[sx][c], B_modv[l]], writes=[B_xw[sx][c]])

                def POSTA(i, b, ti):
                    sx = i % 3
                    ln_reduce(lnr, [xw[sx][:, c, H:E] for c in range(NC8)], B_xw[sx])

                def POSTB(i, b, ti):
                    sx = i % 3
                    ln_finish(lnr, [xw[sx][:, c, H:E] for c in range(NC8)], B_xw[sx], l, 0, 6, 7)
                    op("sp", lambda e: e.dma_start(out=tile_src(dst, b, ti * T, T), in_=xw[sx][:, :, H:E]), reads=B_xw[sx], semkey=f"st{sx}")

                run_tiles(tiles, LOAD, PRE, MAIN, POSTA, POSTB)
                while deferred:
                    mod_piece(*deferred.pop(0), mwp, B_mwp, 4)
                S_.flush()

        def sconv_phase(l, src, dst):
            T = 512 if S >= 512 else S
            NT = S // T
            with ExitStack() as es:
                win = sbt(es, "win", [128, NC8, 3 * D], BF16)
                wout = sbt(es, "wout", [128, NC8, D], BF16)
                xw = [sbt(es, f"sxw{i}", [128, NC8, T], F32) for i in range(3)]
                ub = [sbt(es, f"sub{i}", [128, NC8, T], BF16) for i in range(2)]
                qb = sbt(es, "sqb", [128, NC8, T], BF16)
                t1 = [sbt(es, f"st1{i}", [128, T], F32) for i in range(2)]
                pbuf = [sbt(es, f"spb{i}", [128, T + 2], F32) for i in range(2)]
                ab = [sbt(es, f"sab{i}", [128, T], F32) for i in range(2)]
                ptl = sbt(es, "sptl", [128, NC8, 2], F32)
                lnr = ln_alloc(es, T)
                B_win = bufs(3, "win")
                B_wout = Buf("wout")
                B_xw = [bufs(NC8, f"sxw{i}_") for i in range(3)]
                B_ub = [bufs(NC8, f"sub{i}_") for i in range(2)]
                B_qb = bufs(NC8, "sqb")
                B_t1 = bufs(2, "st1")
                B_pb = bufs(2, "spb")
                B_ab = bufs(2, "sab")
                B_ptl = bufs(NC8, "sptl")
                wsrc = sc_w_in.rearrange("(kc p) n -> p kc n", p=128)
                for i in range(3):
                    op("pool", (lambda i=i: lambda e: e.dma_start(out=win[:, :, i * D:(i + 1) * D], in_=wsrc[:, :, i * D:(i + 1) * D]))(), writes=[B_win[i]], semkey=f"wA{i}")
                op("pool", lambda e: e.dma_start(out=wout[:], in_=sc_w_out.rearrange("(kc p) n -> p kc n", p=128)), writes=[B_wout], semkey="wB0")
                tiles = [(b, ti) for b in range(nseq) for ti in range(NT)]

                def LOAD(i, b, ti):
                    s = i % 2
                    sx = i % 3
                    op("sp", lambda e: e.dma_start(out=xw[sx][:], in_=tile_src(src, b, ti * T, T)), writes=B_xw[sx], semkey=f"ld{sx}")

                def PRE(i, b, ti):
                    s = i % 2
                    sx = i % 3
                    for c in range(NC8):
                        op("dve", (lambda c=c: lambda e: e.tensor_scalar(out=ub[s][:, c, :], in0=xw[sx][:, c, :], scalar1=mcol(l, 1, b, c), scalar2=mcol(l, 0, b, c), op0=ALU.mult, op1=ALU.add))(),
                           reads=[B_xw[sx][c], B_modv[l]], writes=[B_ub[s][c]])
                    for c in range(NC8):
                        op("act", (lambda c=c: lambda e: e.activation(out=xw[sx][:, c, :], in_=xw[sx][:, c, :], func=AF.Identity, scale=ALPHA))(), reads=[B_xw[sx][c]], writes=[B_xw[sx][c]])

                def MAIN(i, b, ti, pre_next):
                    s = i % 2
                    sx = i % 3
                    if ti == 0:
                        op("pool", lambda e: e.memset(ptl[:], 0.0), writes=B_ptl)
                    for c in range(NC8):
                        k2 = c % 2
                        banks3 = [0 + 3 * k2, 1 + 3 * k2, 2 + 3 * k2]
                        for which in range(3):
                            def mm(e, which=which, c=c, pbk=banks3[which]):
                                ins = None
                                for kc in range(NC8):
                                    ins = e.matmul(bank(pbk, T), lhsT=win[:, kc, which * D + c * 128: which * D + (c + 1) * 128], rhs=ub[s][:, kc, :], start=(kc == 0), stop=(kc == NC8 - 1))
                                return ins
                            op("pe", mm, reads=[B_win[which]] + B_ub[s], writes=[PB[banks3[which]]])
                        op("act", (lambda k2=k2, pbk=banks3[1]: lambda e: e.activation(out=t1[k2][:], in_=bank(pbk, T), func=AF.Copy))(), reads=[PB[banks3[1]]], writes=[B_t1[k2]])
                        op("pool", (lambda k2=k2, c=c: lambda e: e.tensor_copy(out=pbuf[k2][:, 0:2], in_=ptl[:, c, :]))(), reads=[B_ptl[c]], writes=[B_pb[k2]])
                        op("dve", (lambda k2=k2, pbk=banks3[2]: lambda e: e.tensor_tensor(out=pbuf[k2][:, 2:T + 2], in0=bank(pbk, T), in1=t1[k2][:], op=ALU.mult))(),
                           reads=[PB[banks3[2]], B_t1[k2]], writes=[B_pb[k2]])
                        op("pool", (lambda k2=k2, c=c: lambda e: e.tensor_copy(out=ptl[:, c, :], in_=pbuf[k2][:, T:T + 2]))(), reads=[B_pb[k2]], writes=[B_ptl[c]])
                        op("dve", (lambda k2=k2, c=c: lambda e: e.tensor_scalar(out=ab[k2][:], in0=pbuf[k2][:, 2:T + 2], scalar1=col("scw2", c), scalar2=None, op0=ALU.mult))(),
                           reads=[B_pb[k2], B_cv], writes=[B_ab[k2]])
                        op("dve", (lambda k2=k2, c=c: lambda e: e.scalar_tensor_tensor(out=ab[k2][:], in0=pbuf[k2][:, 1:T + 1], scalar=col("scw1", c), in1=ab[k2][:], op0=ALU.mult, op1=ALU.add))(),
                           reads=[B_pb[k2], B_ab[k2], B_cv], writes=[B_ab[k2]])
                        op("dve", (lambda k2=k2, c=c: lambda e: e.scalar_tensor_tensor(out=ab[k2][:], in0=pbuf[k2][:, 0:T], scalar=col("scw0", c), in1=ab[k2][:], op0=ALU.mult, op1=ALU.add))(),
                           reads=[B_pb[k2], B_ab[k2], B_cv], writes=[B_ab[k2]])
                        op("dve", (lambda k2=k2, c=c, pbk=banks3[0]: lambda e: e.tensor_tensor(out=qb[:, c, :], in0=bank(pbk, T), in1=ab[k2][:], op=ALU.mult))(),
                           reads=[PB[banks3[0]], B_ab[k2]], writes=[B_qb[c]])
                    pre_next()
                    for oc in range(NC8):
                        pb = 6 + oc % 2

                        def mm2(e, oc=oc, pb=pb):
                            ins = None
                            for kc in range(NC8):
                                ins = e.matmul(bank(pb, T), lhsT=wout[:, kc, oc * 128:(oc + 1) * 128], rhs=qb[:, kc, :], start=(kc == 0), stop=(kc == NC8 - 1))
                            return ins
                        op("pe", mm2, reads=[B_wout] + B_qb, writes=[PB[pb]])
                        op("dve", (lambda oc=oc, pb=pb: lambda e: e.scalar_tensor_tensor(out=xw[sx][:, oc, :], in0=bank(pb, T), scalar=mcol(l, 2, b, oc), in1=xw[sx][:, oc, :], op0=ALU.mult, op1=ALU.add))(),
                           reads=[PB[pb], B_xw[sx][oc], B_modv[l]], writes=[B_xw[sx][oc]])

                def POSTA(i, b, ti):
                    sx = i % 3
                    ln_reduce(lnr, [xw[sx][:, c, :] for c in range(NC8)], B_xw[sx])

                def POSTB(i, b, ti):
                    sx = i % 3
                    ln_finish(lnr, [xw[sx][:, c, :] for c in range(NC8)], B_xw[sx], l, 0, 6, 7)
                    op("sp", lambda e: e.dma_start(out=tile_src(dst, b, ti * T, T), in_=xw[sx][:]), reads=B_xw[sx], semkey=f"st{sx}")

                run_tiles(tiles, LOAD, PRE, MAIN, POSTA, POSTB)
                S_.flush()

        def mla_a1(l, src):
            T = 256 if S >= 256 else S
            NT = S // T
            with ExitStack() as es:
                wa = sbt(es, "wa", [128, NC8, 1056 + 32], BF16)
                wqp = sbt(es, "wqp", [128, 6, 1024], BF16)
                xw = [sbt(es, f"axw{i}", [128, NC8, T], F32) for i in range(2)]
                ub = [sbt(es, f"aub{i}", [128, NC8, T], BF16) for i in range(2)]
                cqf = sbt(es, "cqf", [128, 8, T], F32)
                sq = [sbt(es, f"asq{i}", [128, T], F32) for i in range(2)]
                rs = [sbt(es, f"ars{i}", [128, T], F32) for i in range(2)]
                cqn = [sbt(es, f"acqn{i}", [128, 8, T], BF16) for i in range(2)]
                cosT = sbt(es, "cosT", [128, S], F32)
                sinT = sbt(es, "sinT", [128, S], F32)
                tscr = sbt(es, "tscr", [128, S], F32)
                posi = sbt(es, "posi", [128, S], I32)
                tki = posi
                rt = [sbt(es, f"art{i}", [128, T], F32) for i in range(4)]
                rpe = [sbt(es, f"arpe{i}", [128, 5, T], BF16) for i in range(2)]
                B_wa = Buf("wa")
                B_wqp = Buf("wqp")
                B_xw = [bufs(NC8, f"axw{i}_") for i in range(2)]
                B_ub = [bufs(NC8, f"aub{i}_") for i in range(2)]
                B_cqf = bufs(8, "cqf")
                B_sq = bufs(2, "asq")
                B_rs = bufs(2, "ars")
                B_cqn = [bufs(8, f"acqn{i}_") for i in range(2)]
                B_tab = Buf("tab")
                B_tscr = Buf("tscr")
                B_rt = bufs(4, "art")
                B_rpe = [bufs(5, f"arpe{i}_") for i in range(2)]
                op("pool", lambda e: e.dma_start(out=wa[:, :, 0:1056], in_=w_a.rearrange("(kc p) n -> p kc n", p=128)), writes=[B_wa], semkey="wA0")
                op("pool", lambda e: e.dma_start(out=wa[:, :, 1056:1088], in_=w_a_sw.rearrange("(kc p) n -> p kc n", p=128)), writes=[B_wa], semkey="wA0")
                op("pool", lambda e: e.dma_start(out=wqp[:, :, 0:512], in_=w_uq_pe.rearrange("(kc p) n -> p kc n", p=128)), writes=[B_wqp], semkey="wA1")
                op("pool", lambda e: e.dma_start(out=wqp[:, :, 512:1024], in_=w_uq_pesw.rearrange("(kc p) n -> p kc n", p=128)), writes=[B_wqp], semkey="wA1")
                C1 = 6.28125
                C2 = float(2 * np.pi - 6.28125)
                PI = float(np.pi)

                def tables(b):
                    op("sp", lambda e: e.dma_start(out=posi[:], in_=pos[b:b + 1, :].partition_broadcast(128)), writes=[B_tscr], semkey="ld2")
                    op("dve", lambda e: e.tensor_copy(out=sinT[:], in_=posi[:]), reads=[B_tscr], writes=[B_tab])
                    op("dve", lambda e: e.tensor_scalar(out=sinT[:], in0=sinT[:], scalar1=col("invf"), scalar2=None, op0=ALU.mult), reads=[B_tab, B_cv], writes=[B_tab])
                    op("dve", lambda e: e.tensor_scalar(out=cosT[:], in0=sinT[:], scalar1=PI / 2, scalar2=None, op0=ALU.add), reads=[B_tab], writes=[B_tab])
                    for tb in (sinT, cosT):
                        op("dve", (lambda tb=tb: lambda e: e.tensor_scalar(out=tscr[:], in0=tb[:], scalar1=float(1 / (2 * np.pi)), scalar2=None, op0=ALU.mult))(), reads=[B_tab], writes=[B_tscr])
                        op("dve", lambda e: e.tensor_copy(out=tki[:], in_=tscr[:]), reads=[B_tscr], writes=[B_tscr])
                        op("dve", lambda e: e.tensor_copy(out=tscr[:], in_=tki[:]), reads=[B_tscr], writes=[B_tscr])
                        op("dve", (lambda tb=tb: lambda e: e.scalar_tensor_tensor(out=tb[:], in0=tscr[:], scalar=-C1, in1=tb[:], op0=ALU.mult, op1=ALU.add))(), reads=[B_tscr, B_tab], writes=[B_tab])
                        op("dve", (lambda tb=tb: lambda e: e.scalar_tensor_tensor(out=tb[:], in0=tscr[:], scalar=-C2, in1=tb[:], op0=ALU.mult, op1=ALU.add))(), reads=[B_tscr, B_tab], writes=[B_tab])
                        op("dve", (lambda tb=tb: lambda e: e.tensor_scalar(out=tscr[:], in0=tb[:], scalar1=PI, scalar2=float(-2 * np.pi), op0=ALU.is_gt, op1=ALU.mult))(), reads=[B_tab], writes=[B_tscr])
                        op("dve", (lambda tb=tb: lambda e: e.tensor_tensor(out=tb[:], in0=tb[:], in1=tscr[:], op=ALU.add))(), reads=[B_tscr, B_tab], writes=[B_tab])
                        op("dve", (lambda tb=tb: lambda e: e.tensor_scalar(out=tscr[:], in0=tb[:], scalar1=-PI, scalar2=float(2 * np.pi), op0=ALU.is_lt, op1=ALU.mult))(), reads=[B_tab], writes=[B_tscr])
                        op("dve", (lambda tb=tb: lambda e: e.tensor_tensor(out=tb[:], in0=tb[:], in1=tscr[:], op=ALU.add))(), reads=[B_tscr, B_tab], writes=[B_tab])
                        op("dve", (lambda tb=tb: lambda e: e.tensor_scalar(out=tb[:], in0=tb[:], scalar1=PI, scalar2=-PI, op0=ALU.min, op1=ALU.max))(), reads=[B_tab], writes=[B_tab])
                        op("act", (lambda tb=tb: lambda e: e.activation(out=tb[:], in_=tb[:], func=AF.Sin))(), reads=[B_tab], writes=[B_tab])
                    op("dve", lambda e: e.tensor_scalar(out=sinT[:], in0=sinT[:], scalar1=col("sgn"), scalar2=None, op0=ALU.mult), reads=[B_tab, B_cv], writes=[B_tab])

                tiles = [(b, ti) for b in range(nseq) for ti in range(NT)]

                def LOAD(i, b, ti):
                    s = i % 2
                    op("sp", lambda e: e.dma_start(out=xw[s][:], in_=tile_src(src, b, ti * T, T)), writes=B_xw[s], semkey=f"ld{s}")

                def PRE(i, b, ti):
                    s = i % 2
                    for c in range(NC8):
                        op("dve", (lambda c=c: lambda e: e.tensor_scalar(out=ub[s][:, c, :], in0=xw[s][:, c, :], scalar1=mcol(l, 1, b, c), scalar2=mcol(l, 0, b, c), op0=ALU.mult, op1=ALU.add))(),
                           reads=[B_xw[s][c], B_modv[l]], writes=[B_ub[s][c]])

                def MAIN(i, b, ti, pre_next):
                    s = i % 2
                    t0 = ti * T
                    if ti == 0:
                        tables(b)
                    for c in range(8):
                        pb = c % 3
                        grp = 0 if c < 6 else 1

                        def mm(e, c=c, pb=pb):
                            ins = None
                            for kc in range(NC8):
                                ins = e.matmul(bank(pb, T), lhsT=wa[:, kc, c * 128:(c + 1) * 128], rhs=ub[s][:, kc, :], start=(kc == 0), stop=(kc == NC8 - 1))
                            return ins
                        op("pe", mm, reads=[B_wa] + B_ub[s], writes=[PB[pb]])
                        op("act", (lambda c=c, pb=pb: lambda e: e.activation(out=cqf[:, c, :], in_=bank(pb, T), func=AF.Copy))(), reads=[PB[pb]], writes=[B_cqf[c]])
                        op("act", (lambda c=c, pb=pb: lambda e: e.activation(out=sq[c % 2][:], in_=bank(pb, T), func=AF.Square))(), reads=[PB[pb]], writes=[B_sq[c % 2]])
                        first = c in (0, 6)
                        last = c in (5, 7)
                        op("pe", (lambda c=c, grp=grp, first=first, last=last: lambda e: e.matmul(bank(6 + grp, T), lhsT=(ones_q if grp == 0 else ones_kv)[:], rhs=sq[c % 2][:], start=first, stop=last))(),
                           reads=[B_sq[c % 2], B_ones], writes=[PB[6 + grp]])
                    for grp in range(2):
                        op("dve", (lambda grp=grp: lambda e: e.tensor_scalar(out=rs[grp][:], in0=bank(6 + grp, T), scalar1=0.0, scalar2=None, op0=ALU.max))(), reads=[PB[6 + grp]], writes=[B_rs[grp]])
                        op("act", (lambda grp=grp: lambda e: e.activation(out=rs[grp][:], in_=rs[grp][:], func=AF.Sqrt, bias=col("eps_rms")))(), reads=[B_rs[grp], B_cv], writes=[B_rs[grp]])
                        op("dve", (lambda grp=grp: lambda e: e.reciprocal(out=rs[grp][:], in_=rs[grp][:]))(), reads=[B_rs[grp]], writes=[B_rs[grp]])
                    for c in range(8):
                        grp = 0 if c < 6 else 1
                        nm = col("qnorm", c) if c < 6 else col("kvnorm", c - 6)
                        op("dve", (lambda c=c, grp=grp, nm=nm: lambda e: e.scalar_tensor_tensor(out=cqn[s][:, c, :], in0=cqf[:, c, :], scalar=nm, in1=rs[grp][:], op0=ALU.mult, op1=ALU.mult))(),
                           reads=[B_cqf[c], B_rs[grp], B_cv], writes=[B_cqn[s][c]])
                    op("sp", lambda e: e.dma_start(out=cqn_d[b].rearrange("(c p) s -> p c s", p=128)[:, :, t0:t0 + T], in_=cqn[s][:, 0:6, :]), reads=B_cqn[s][0:6], semkey=f"st{s}")
                    op("sp", lambda e: e.dma_start(out=ckvn_d[b].rearrange("(c p) s -> p c s", p=128)[:, :, t0:t0 + T], in_=cqn[s][:, 6:8, :]), reads=B_cqn[s][6:8], semkey=f"st{s}")
                    pre_next()
                    for c in range(5):
                        pa = 3 if c % 2 == 0 else 0
                        pbk = 4 if c % 2 == 0 else 5
                        M = 128 if c < 4 else 32

                        def mmA(e, c=c, pa=pa):
                            ins = None
                            if c < 4:
                                for kc in range(6):
                                    ins = e.matmul(bank(pa, T), lhsT=wqp[:, kc, c * 128:(c + 1) * 128], rhs=cqn[s][:, kc, :], start=(kc == 0), stop=(kc == 5))
                            else:
                                for kc in range(NC8):
                                    ins = e.matmul(bank(pa, T)[0:32, :], lhsT=wa[:, kc, 1024:1056], rhs=ub[s][:, kc, :], start=(kc == 0), stop=(kc == NC8 - 1))
                            return ins

                        def mmB(e, c=c, pbk=pbk):
                            ins = None
                            if c < 4:
                                for kc in range(6):
                                    ins = e.matmul(bank(pbk, T), lhsT=wqp[:, kc, 512 + c * 128:512 + (c + 1) * 128], rhs=cqn[s][:, kc, :], start=(kc == 0), stop=(kc == 5))
                            else:
                                for kc in range(NC8):
                                    ins = e.matmul(bank(pbk, T)[0:32, :], lhsT=wa[:, kc, 1056:1088], rhs=ub[s][:, kc, :], start=(kc == 0), stop=(kc == NC8 - 1))
                            return ins
                        rd = (B_cqn[s][0:6] + [B_wqp]) if c < 4 else (B_ub[s] + [B_wa])
                        op("pe", mmA, reads=rd, writes=[PB[pa]])
                        op("pe", mmB, reads=rd, writes=[PB[pbk]])
                        r0 = (c % 2) * 2
                        op("dve", (lambda pa=pa, M=M, r0=r0: lambda e: e.tensor_tensor(out=rt[r0][0:M, :], in0=bank(pa, T)[0:M, :], in1=cosT[0:M, t0:t0 + T], op=ALU.mult))(),
                           reads=[PB[pa], B_tab], writes=[B_rt[r0]])
                        op("dve", (lambda pbk=pbk, M=M, r0=r0: lambda e: e.tensor_tensor(out=rt[r0 + 1][0:M, :], in0=bank(pbk, T)[0:M, :], in1=sinT[0:M, t0:t0 + T], op=ALU.mult))(),
                           reads=[PB[pbk], B_tab], writes=[B_rt[r0 + 1]])
                        op("pool", (lambda c=c, M=M, r0=r0: lambda e: e.tensor_tensor(out=rpe[s][0:M, c, :], in0=rt[r0][0:M, :], in1=rt[r0 + 1][0:M, :], op=ALU.add))(),
                           reads=[B_rt[r0], B_rt[r0 + 1]], writes=[B_rpe[s][c]])
                    op("sp", lambda e: e.dma_start(out=qpe_d[b].rearrange("(c p) s -> p c s", p=128)[:, :, t0:t0 + T], in_=rpe[s][:, 0:4, :]), reads=B_rpe[s][0:4], semkey=f"st{s}")
                    op("sp", lambda e: e.dma_start(out=kpe_d[b][:, t0:t0 + T], in_=rpe[s][0:32, 4, :]), reads=[B_rpe[s][4]], semkey=f"st{s}")

                run_tiles(tiles, LOAD, PRE, MAIN)
                S_.flush()

        def mla_a2():
            QT = 512 if S >= 512 else S
            NQ = S // QT
            NKT = S // 128
            KPQ = QT // 128
            with ExitStack() as es:
                wq = sbt(es, "wq", [128, 6, 1024], BF16)
                wk = sbt(es, "wk", [128, 2, 1024], BF16)
                wv = sbt(es, "wv", [128, 2, 1024], BF16)
                msk = sbt(es, "msk", [128, 4, 512], BF16)
                cq = sbt(es, "cq", [128, 6, S], BF16)
                ckv = sbt(es, "ckv", [128, 2, S], BF16)
                KTb = [sbt(es, f"KT{i}", [128, S], BF16) for i in range(2)]
                QTb = [sbt(es, f"QT{i}", [128, S], BF16) for i in range(2)]
                Vb = [sbt(es, f"V{i}", [128, NKT, 128], BF16) for i in range(2)]
                NPT = 4
                PT = [sbt(es, f"PT{i}", [128, QT], BF16) for i in range(NPT)]
                rden = sbt(es, "rden", [128, QT], F32)
                rden0 = sbt(es, "rden0", [128, QT], F32)
                ost = [sbt(es, f"ost{i}", [128, QT], BF16) for i in range(2)]
                B_w = Buf("a2w")
                B_msk = Buf("msk")
                B_cq = Buf("cq")
                B_ckv = Buf("ckv")
                B_KT = [Buf("KTn0"), Buf("KTn1")]
                B_KTp = [Buf("KTp0"), Buf("KTp1")]
                B_QT = [bufs(NQ, "QTn0_"), bufs(NQ, "QTn1_")]
                B_QTp = [Buf("QTp0"), Buf("QTp1")]
                B_V = [Buf("V0"), Buf("V1")]
                B_Vones = [Buf("Vo0"), Buf("Vo1")]
                B_PT = bufs(NPT, "PT")
                B_rden = Buf("rden")
                B_rden0 = Buf("rden0")
                B_ost = bufs(2, "ost")
                op("pool", lambda e: e.dma_start(out=wq[:], in_=w_uq_nope.rearrange("(kc p) n -> p kc n", p=128)), writes=[B_w], semkey="wA0")
                op("pool", lambda e: e.dma_start(out=wk[:], in_=w_uk.rearrange("(kc p) n -> p kc n", p=128)), writes=[B_w], semkey="wA0")
                op("pool", lambda e: e.dma_start(out=wv[:], in_=w_uv.rearrange("(kc p) n -> p kc n", p=128)), writes=[B_w], semkey="wA0")
                op("sp", lambda e: e.dma_start(out=msk[:], in_=masks_d), writes=[B_msk], semkey="ld2")
                for i in range(2):
                    op("pool", (lambda i=i: lambda e: e.memset(Vb[i][:, :, 64:128], 1.0))(), writes=[B_Vones[i]])

                def proj(b, h, sl):
                    if h == 0:
                        op("sp", lambda e: e.dma_start(out=cq[:], in_=cqn_d[b].rearrange("(c p) s -> p c s", p=128)), writes=[B_cq], semkey="ld0")
                        op("sp", lambda e: e.dma_start(out=ckv[:], in_=ckvn_d[b].rearrange("(c p) s -> p c s", p=128)), writes=[B_ckv], semkey="ld1")
                    op("sp", lambda e: e.dma_start(out=KTb[sl][64:96, :], in_=kpe_d[b]), writes=[B_KTp[sl]], semkey=f"kp{sl}")
                    op("sp", lambda e: e.dma_start(out=QTb[sl][64:96, :], in_=qpe_d[b][h * 32:(h + 1) * 32, :]), writes=[B_QTp[sl]], semkey=f"qp{sl}")
                    for qi in range(NQ):
                        pb = qi % 2

                        def mmk(e, qi=qi, pb=pb):
                            ins = None
                            for kc in range(2):
                                ins = e.matmul(bank(pb, QT)[0:64, :], lhsT=wk[:, kc, h * 64:(h + 1) * 64], rhs=ckv[:, kc, qi * QT:(qi + 1) * QT], start=(kc == 0), stop=(kc == 1))
                            return ins
                        op("pe", mmk, reads=[B_w, B_ckv], writes=[PB[pb]])
                        op("dve", (lambda qi=qi, pb=pb: lambda e: e.tensor_copy(out=KTb[sl][0:64, qi * QT:(qi + 1) * QT], in_=bank(pb, QT)[0:64, :]))(), reads=[PB[pb]], writes=[B_KT[sl]])
                    for qi in range(NQ):
                        pb = qi % 2

                        def mmq(e, qi=qi, pb=pb):
                            ins = None
                            for kc in range(6):
                                ins = e.matmul(bank(pb, QT)[0:64, :], lhsT=wq[:, kc, h * 64:(h + 1) * 64], rhs=cq[:, kc, qi * QT:(qi + 1) * QT], start=(kc == 0), stop=(kc == 5))
                            return ins
                        op("pe", mmq, reads=[B_w, B_cq], writes=[PB[pb]])
                        op("dve", (lambda qi=qi, pb=pb: lambda e: e.tensor_copy(out=QTb[sl][0:64, qi * QT:(qi + 1) * QT], in_=bank(pb, QT)[0:64, :]))(), reads=[PB[pb]], writes=[B_QT[sl][qi]])
                    for t8 in range(NKT // 8 if NKT >= 8 else 1):
                        nt8 = min(8, NKT)
                        pb = 2

                        def mmv(e, t8=t8, pb=pb, nt8=nt8):
                            ins = None
                            for k in range(nt8):
                                tc = t8 * 8 + k
                                for kc in range(2):
                                    ins = e.matmul(bank(pb)[:, k * 64:(k + 1) * 64], lhsT=ckv[:, kc, tc * 128:(tc + 1) * 128], rhs=wv[:, kc, h * 64:(h + 1) * 64], start=(kc == 0), stop=(kc == 1))
                            return ins
                        op("pe", mmv, reads=[B_w, B_ckv], writes=[PB[pb]])
                        op("dve", (lambda t8=t8, pb=pb, nt8=nt8: lambda e: e.tensor_copy(out=Vb[sl][:, t8 * 8:t8 * 8 + nt8, 0:64], in_=bank(pb)[:, 0:nt8 * 64].rearrange("p (k d) -> p k d", d=64)))(),
                           reads=[PB[pb]], writes=[B_V[sl]])

                cnt = [0]

                def emitS(b, h, sl, qi, ki):
                    k = cnt[0]
                    cnt[0] += 1
                    pbs = 3 + k % 3
                    pts = k % NPT
                    dg = ki - KPQ * qi
                    c0 = dg * 128 if dg > 0 else 0
                    op("pe", lambda e: e.matmul(bank(pbs, QT)[:, c0:QT], lhsT=KTb[sl][0:96, ki * 128:(ki + 1) * 128], rhs=QTb[sl][0:96, qi * QT + c0:(qi + 1) * QT], start=True, stop=True),
                       reads=[B_KT[sl], B_KTp[sl], B_QT[sl][qi], B_QTp[sl]], writes=[PB[pbs]])
                    op("act", lambda e: e.activation(out=PT[pts][:, c0:QT], in_=bank(pbs, QT)[:, c0:QT], func=AF.Exp, scale=SM_SCALE), reads=[PB[pbs]], writes=[B_PT[pts]])
                    if dg >= 0:
                        op("pool", lambda e: e.tensor_tensor(out=PT[pts][:, c0:c0 + 128], in0=PT[pts][:, c0:c0 + 128], in1=msk[:, 0, 0:128], op=ALU.mult), reads=[B_PT[pts], B_msk], writes=[B_PT[pts]])
                    return (pts, c0)

                def emitPV(b, h, sl, qi, ki, nk, st):
                    pts, c0 = st
                    po = 6 + qi % 2
                    op("pe", lambda e: e.matmul(bank(po, QT)[:, c0:QT], lhsT=Vb[sl][:, ki, :], rhs=PT[pts][:, c0:QT], start=(ki == 0), stop=(ki == nk - 1)),
                       reads=[B_V[sl], B_Vones[sl], B_PT[pts]], writes=[PB[po]])
                    if ki == nk - 1:
                        os_ = qi % 2
                        op("dve", lambda e: e.reciprocal(out=rden[64:128, :], in_=bank(po, QT)[64:128, :]), reads=[PB[po]], writes=[B_rden])
                        op("dve", lambda e: e.tensor_copy(out=rden0[0:64, :], in_=rden[64:128, :]), reads=[B_rden], writes=[B_rden0])
                        op("dve", lambda e: e.tensor_tensor(out=ost[os_][0:64, :], in0=bank(po, QT)[0:64, :], in1=rden0[0:64, :], op=ALU.mult), reads=[PB[po], B_rden0], writes=[B_ost[os_]])
                        op("sp", lambda e: e.dma_start(out=oT_d[b][h * 64:(h + 1) * 64, qi * QT:(qi + 1) * QT], in_=ost[os_][0:64, :]), reads=[B_ost[os_]], semkey=f"st{os_}")

                heads = [(b, h) for b in range(nseq) for h in range(HEADS)]
                LOOK = 2
                proj(heads[0][0], heads[0][1], 0)
                for gi, (b, h) in enumerate(heads):
                    sl = gi % 2
                    pairs = [(qi, ki) for qi in range(NQ) for ki in range(KPQ * (qi + 1))]
                    mid = len(pairs) // 2
                    states = {}
                    for j in range(min(LOOK, len(pairs))):
                        states[j] = emitS(b, h, sl, *pairs[j])
                    for j, (qi, ki) in enumerate(pairs):
                        if j + LOOK < len(pairs):
                            states[j + LOOK] = emitS(b, h, sl, *pairs[j + LOOK])
                        emitPV(b, h, sl, qi, ki, KPQ * (qi + 1), states.pop(j))
                        if j == mid and gi + 1 < len(heads):
                            proj(heads[gi + 1][0], heads[gi + 1][1], (gi + 1) % 2)
                S_.flush()

        def mla_a3(l, src, dst):
            T = 512 if S >= 512 else S
            NT = S // T
            with ExitStack() as es:
                wo = sbt(es, "wo", [128, NC8, D], BF16)
                xw = [sbt(es, f"oxw{i}", [128, NC8, T], F32) for i in range(3)]
                ob = [sbt(es, f"oob{i}", [128, NC8, T], BF16) for i in range(2)]
                lnr = ln_alloc(es, T)
                B_wo = Buf("wo")
                B_xw = [bufs(NC8, f"oxw{i}_") for i in range(3)]
                B_ob = [Buf("oob0"), Buf("oob1")]
                op("pool", lambda e: e.dma_start(out=wo[:], in_=w_o.rearrange("(kc p) n -> p kc n", p=128)), writes=[B_wo], semkey="wA0")
                tiles = [(b, ti) for b in range(nseq) for ti in range(NT)]

                def LOAD(i, b, ti):
                    s = i % 2
                    sx = i % 3
                    op("sp", lambda e: e.dma_start(out=xw[sx][:], in_=tile_src(src, b, ti * T, T)), writes=B_xw[sx], semkey=f"ld{sx}")
                    op("sp", lambda e: e.dma_start(out=ob[s][:], in_=tile_src(oT_d, b, ti * T, T)), writes=[B_ob[s]], semkey=f"lo{s}")

                def PRE(i, b, ti):
                    s = i % 2
                    sx = i % 3
                    for c in range(NC8):
                        op("act", (lambda c=c: lambda e: e.activation(out=xw[sx][:, c, :], in_=xw[sx][:, c, :], func=AF.Identity, scale=ALPHA))(), reads=[B_xw[sx][c]], writes=[B_xw[sx][c]])

                def MAIN(i, b, ti, pre_next):
                    s = i % 2
                    sx = i % 3
                    for oc in range(NC8):
                        pb = oc % 4

                        def mm(e, oc=oc, pb=pb):
                            ins = None
                            for kc in range(NC8):
                                ins = e.matmul(bank(pb, T), lhsT=wo[:, kc, oc * 128:(oc + 1) * 128], rhs=ob[s][:, kc, :], start=(kc == 0), stop=(kc == NC8 - 1))
                            return ins
                        op("pe", mm, reads=[B_wo, B_ob[s]], writes=[PB[pb]])
                        op("dve", (lambda oc=oc, pb=pb: lambda e: e.scalar_tensor_tensor(out=xw[sx][:, oc, :], in0=bank(pb, T), scalar=mcol(l, 2, b, oc), in1=xw[sx][:, oc, :], op0=ALU.mult, op1=ALU.add))(),
                           reads=[PB[pb], B_xw[sx][oc], B_modv[l]], writes=[B_xw[sx][oc]])

                def POSTA(i, b, ti):
                    sx = i % 3
                    ln_reduce(lnr, [xw[sx][:, c, :] for c in range(NC8)], B_xw[sx])

                def POSTB(i, b, ti):
                    sx = i % 3
                    ln_finish(lnr, [xw[sx][:, c, :] for c in range(NC8)], B_xw[sx], l, 0, 6, 7)
                    op("sp", lambda e: e.dma_start(out=tile_src(dst, b, ti * T, T), in_=xw[sx][:]), reads=B_xw[sx], semkey=f"st{sx}")

                run_tiles(tiles, LOAD, PRE, MAIN, POSTA, POSTB)
                S_.flush()

        phases = []
        for l in layers:
            phases.append(("mix", l))
            phases.append(("ffn", l))
        if stop_after is not None:
            phases = phases[:stop_after]
        cur = xT
        pp = [sA, sB]
        for pi_, (kind, l) in enumerate(phases):
            dst = yT if pi_ == len(phases) - 1 else pp[pi_ % 2]
            if kind == "ffn":
                ffn_up_phase(l, cur)
                ffn_down_phase(l, cur, dst)
            elif l % 3 == 0:
                pool_phase(l, cur, dst)
            elif l % 3 == 1:
                mla_a1(l, cur)
                mla_a2()
                mla_a3(l, cur, dst)
            else:
                sconv_phase(l, cur, dst)
            cur = dst
    return nc


def host_weights(inp):
    w = {}
    f32 = lambda a: np.ascontiguousarray(np.asarray(a, dtype=np.float32))
    w["cvec"] = cvec_layout(inp).build()
    w["masks"] = make_masks()
    w["mod_w"] = f32(inp["mod_w"])
    w["pool_w"] = f32(inp["pool_w"])
    wa = np.asarray(inp["mla_w_a"][0], np.float32)
    w["w_a"] = f32(wa)
    w["w_a_sw"] = f32(np.concatenate([wa[:, 1040:1056], wa[:, 1024:1040]], axis=1))
    wuq = np.asarray(inp["mla_w_uq"][0], np.float32).reshape(768, HEADS, 96)
    w["w_uq_nope"] = f32(wuq[:, :, 0:64].reshape(768, 1024))
    w["w_uq_pe"] = f32(wuq[:, :, 64:96].reshape(768, 512))
    w["w_uq_pesw"] = f32(np.concatenate([wuq[:, :, 80:96], wuq[:, :, 64:80]], axis=2).reshape(768, 512))
    wukv = np.asarray(inp["mla_w_ukv"][0], np.float32).reshape(256, HEADS, 128)
    w["w_uk"] = f32(wukv[:, :, 0:64].reshape(256, 1024))
    w["w_uv"] = f32(wukv[:, :, 64:128].reshape(256, 1024))
    w["w_o"] = f32(inp["mla_w_o"][0])
    w["sc_w_in"] = f32(inp["sc_w_in"][0])
    w["sc_w_out"] = f32(inp["sc_w_out"][0])
    w["ffn_w_up"] = f32(inp["ffn_w_up"])
    w["ffn_w_down"] = f32(inp["ffn_w_down"])
    return w


def core_inputs(inp, w, b0, nseq, S):
    x = np.asarray(inp["x"], np.float32)[b0:b0 + nseq, :S]
    m = dict(w)
    m["xT"] = np.ascontiguousarray(x.transpose(0, 2, 1))
    c = np.asarray(inp["c"], np.float32)[b0:b0 + nseq]
    m["cT"] = np.ascontiguousarray(c.reshape(nseq, NC8, 128).transpose(2, 1, 0).reshape(128, NC8 * nseq))
    m["pos"] = np.ascontiguousarray(np.asarray(inp["positions"], np.int32)[b0:b0 + nseq, :S])
    return m


_PROG_CACHE = {}


def kernel(**inputs):
    B, S, _ = inputs["x"].shape
    ncores = 8
    nseq = B // ncores
    key = (nseq, S)
    if key not in _PROG_CACHE:
        _PROG_CACHE[key] = build_program(nseq, S)
    nc = _PROG_CACHE[key]
    w = host_weights(inputs)
    in_maps = [core_inputs(inputs, w, i * nseq, nseq, S) for i in range(ncores)]
    res = run_bass_kernel_spmd(nc, in_maps, core_ids=list(range(ncores)))
    out = np.empty((B, S, D), np.float32)
    for i in range(ncores):
        out[i * nseq:(i + 1) * nseq] = res.results[i]["yT"].transpose(0, 2, 1)
    return out
```

```python
import numpy as np
import ml_dtypes
import concourse.bass as bass
import concourse.mybir as mybir
from concourse.bass_utils import run_bass_kernel_spmd
from contextlib import ExitStack

F32 = mybir.dt.float32
BF16 = mybir.dt.bfloat16
I32 = mybir.dt.int32
AF = mybir.ActivationFunctionType
ALU = mybir.AluOpType

D = 1024
DEPTH = 4
NC8 = 8
FF = 2816
HC = 44
GC = 22
HEADS = 16
ALPHA = float((2 * DEPTH) ** 0.25)
LN_EPS = 1e-5
RMS_EPS = 1e-6
SM_SCALE = float(96 ** -0.5)


class Buf:
    __slots__ = ("name", "w", "r")

    def __init__(self, name=""):
        self.name = name
        self.w = None
        self.r = []


def bufs(n, name=""):
    return [Buf(f"{name}{i}") for i in range(n)]


class Op:
    __slots__ = ("eng", "fn", "deps", "dwaits", "sig", "idx", "semkey", "phase")


class Sched:
    ENGS = ("pe", "act", "dve", "pool", "sp")

    def __init__(self, nc, top, same_engine_sync=False):
        self.nc = nc
        self.top = top
        self.ops = {e: [] for e in self.ENGS}
        self.last_real = {e: None for e in self.ENGS}
        self.dma_cnt = {}
        self.same_engine_sync = same_engine_sync
        self.phase = 0
        self.esem = None
        self.dsem = {}
        self.sigbase = {e: 0 for e in self.ENGS}
        self.waited = {e: {} for e in self.ENGS}
        self.nops = 0

    def op(self, eng, fn, reads=(), writes=(), semkey=None):
        o = Op()
        o.eng = eng
        o.fn = fn
        o.deps = set()
        o.dwaits = {}
        o.sig = False
        o.idx = 0
        o.semkey = semkey
        o.phase = self.phase
        for b in reads:
            if b.w is not None:
                self._dep(o, b.w)
        for b in writes:
            if b.w is not None:
                self._dep(o, b.w)
            for r in b.r:
                self._dep(o, r)
        for b in reads:
            b.r.append(o)
        for b in writes:
            b.w = o
            b.r = []
        if semkey is not None:
            self.dma_cnt[semkey] = self.dma_cnt.get(semkey, 0) + 1
        self.ops[eng].append(o)
        self.last_real[eng] = o
        self.nops += 1
        return o

    def _dep(self, o, d):
        if d is o:
            return
        if d.phase != self.phase:
            return
        if d.semkey is not None:
            k = d.semkey
            o.dwaits[k] = max(o.dwaits.get(k, 0), self.dma_cnt[k])
            return
        if d.eng == o.eng and (d.eng == "pe" or not self.same_engine_sync):
            return
        d.sig = True
        o.deps.add(d)

    def barrier(self):
        lasts = [self.last_real[e] for e in self.ENGS if self.last_real[e] is not None and self.last_real[e].phase == self.phase]
        for e in self.ENGS:
            o = Op()
            o.eng = e
            o.fn = None
            o.deps = set()
            o.dwaits = dict(self.dma_cnt)
            o.sig = False
            o.idx = 0
            o.semkey = None
            o.phase = self.phase
            for d in lasts:
                if d.semkey is not None or d.eng == e:
                    continue
                d.sig = True
                o.deps.add(d)
            self.ops[e].append(o)

    def flush(self):
        nc = self.nc
        self.barrier()
        if self.esem is None:
            self.esem = {e: self.top.enter_context(nc.semaphore("s_" + e)) for e in self.ENGS if e != "sp"}
        for k in self.dma_cnt:
            if k not in self.dsem:
                self.dsem[k] = self.top.enter_context(nc.semaphore("d_" + str(k)))
        esem, dsem = self.esem, self.dsem
        for e in self.ENGS:
            c = self.sigbase[e]
            for o in self.ops[e]:
                if o.sig:
                    c += 1
                    o.idx = c
            self.sigbase[e] = c
        ops = self.ops
        waited_all = self.waited

        def run(engname):
            def body(eng):
                waited = waited_all[engname]
                for o in ops[engname]:
                    ws = {}
                    for d in o.deps:
                        s = esem[d.eng]
                        if ws.get(s, 0) < d.idx:
                            ws[s] = d.idx
                    for k, v in o.dwaits.items():
                        s = dsem[k]
                        if ws.get(s, 0) < 16 * v:
                            ws[s] = 16 * v
                    for s, v in ws.items():
                        if waited.get(s, 0) < v:
                            eng.wait_ge(s, v)
                            waited[s] = v
                    if o.fn is None:
                        continue
                    ins = o.fn(eng)
                    if o.semkey is not None:
                        ins.then_inc(dsem[o.semkey], 16)
                    elif o.sig:
                        ins.then_inc(esem[engname], 1)
            return body

        with nc.Block() as block:
            block.tensor(run("pe"))
            block.scalar(run("act"))
            block.vector(run("dve"))
            block.gpsimd(run("pool"))
            block.sync(run("sp"))
        self.ops = {e: [] for e in self.ENGS}
        self.phase += 1


def colpack(vec):
    v = np.asarray(vec, dtype=np.float32).reshape(-1, 128)
    return np.ascontiguousarray(v.T)


class ColTable:
    def __init__(self):
        self.blocks = []
        self.off = {}
        self.n = 0

    def add(self, name, arr):
        arr = np.asarray(arr, dtype=np.float32)
        assert arr.shape[0] == 128
        self.off[name] = self.n
        self.blocks.append(arr)
        self.n += arr.shape[1]

    def build(self):
        return np.ascontiguousarray(np.concatenate(self.blocks, axis=1))


def cvec_layout(inp=None):
    ct = ColTable()
    z = lambda n: np.zeros((n,), np.float32)
    g = (lambda k, n: inp[k]) if inp is not None else None
    for l in range(DEPTH):
        for j in range(2):
            ct.add(f"lng{l}_{j}", colpack(inp["ln_g"][l, j] if inp else z(D)))
            ct.add(f"lnb{l}_{j}", colpack(inp["ln_b"][l, j] if inp else z(D)))
        ct.add(f"modb{l}", colpack(inp["mod_b"][l] if inp else z(6 * D)))
        for k in range(3):
            ct.add(f"fcw{l}_{k}", colpack(inp["ffn_conv"][l, k] if inp else z(2 * FF)))
        ct.add(f"fcb{l}", colpack(inp["ffn_conv_b"][l] if inp else z(2 * FF)))
    for j in range(2):
        ct.add(f"pscale{j}", colpack(inp["pool_scale"][j] if inp else z(D)))
    ct.add("qnorm", colpack(inp["mla_q_norm"][0] if inp else z(768)))
    ct.add("kvnorm", colpack(inp["mla_kv_norm"][0] if inp else z(256)))
    for k in range(3):
        ct.add(f"scw{k}", colpack(inp["sc_conv"][0, k] if inp else z(D)))
    p = np.arange(128)
    invf = (10000.0 ** (-np.arange(0, 32, 2, dtype=np.float32) / 32)).astype(np.float32)
    ct.add("invf", invf[p % 16].reshape(128, 1))
    ct.add("sgn", np.where((p % 32) < 16, -1.0, 1.0).astype(np.float32).reshape(128, 1))
    corr = np.zeros((128, 64), np.float32)
    for gi, w in enumerate((2, 4, 8, 16)):
        t = np.arange(16)
        corr[:, gi * 16:(gi + 1) * 16] = (w / np.minimum(t + 1, w)).astype(np.float32)[None, :]
    ct.add("corr", corr)
    ct.add("eps_ln", np.full((128, 1), LN_EPS, np.float32))
    ct.add("eps_rms", np.full((128, 1), RMS_EPS, np.float32))
    return ct


def make_masks():
    k = np.arange(128)[:, None]
    q = np.arange(512)[None, :]
    m = np.stack([((d * 128 + k) <= q) for d in range(4)], axis=1)
    return np.ascontiguousarray(m.astype(np.float32).astype(ml_dtypes.bfloat16))


def build_program(nseq, S, layers=(0, 1, 2, 3), stop_after=None):
    nc = bass.Bass("TRN2", target_bir_lowering=False)
    NCV = cvec_layout().n
    coff = cvec_layout().off

    def din(name, shape, dt=F32):
        return nc.dram_tensor(name, list(shape), dt, kind="ExternalInput").ap()

    def dscr(name, shape, dt=F32):
        return nc.dram_tensor(name, list(shape), dt).ap()

    xT = din("xT", [nseq, D, S])
    cT = din("cT", [128, NC8 * nseq])
    pos = din("pos", [nseq, S], I32)
    cvec_d = din("cvec", [128, NCV])
    masks_d = din("masks", [128, 4, 512], BF16)
    mod_w = din("mod_w", [DEPTH, D, 6 * D])
    pool_w = din("pool_w", [2, 4, 256, 256])
    w_a = din("w_a", [D, 1056])
    w_a_sw = din("w_a_sw", [D, 32])
    w_uq_nope = din("w_uq_nope", [768, 1024])
    w_uq_pe = din("w_uq_pe", [768, 512])
    w_uq_pesw = din("w_uq_pesw", [768, 512])
    w_uk = din("w_uk", [256, 1024])
    w_uv = din("w_uv", [256, 1024])
    w_o = din("w_o", [D, D])
    sc_w_in = din("sc_w_in", [D, 3 * D])
    sc_w_out = din("sc_w_out", [D, D])
    ffn_w_up = din("ffn_w_up", [DEPTH, D, 2 * FF])
    ffn_w_down = din("ffn_w_down", [DEPTH, FF, D])
    yT = nc.dram_tensor("yT", [nseq, D, S], F32, kind="ExternalOutput").ap()
    sA = dscr("sA", [nseq, D, S])
    sB = dscr("sB", [nseq, D, S])
    cqn_d = dscr("cqn_d", [nseq, 768, S], BF16)
    ckvn_d = dscr("ckvn_d", [nseq, 256, S], BF16)
    kpe_d = dscr("kpe_d", [nseq, 32, S], BF16)
    qpe_d = dscr("qpe_d", [nseq, 512, S], BF16)
    oT_d = dscr("oT_d", [nseq, D, S], BF16)
    g_d = dscr("g_d", [nseq, FF, S], BF16)

    top = ExitStack()
    S_ = Sched(nc, top)
    op = S_.op
    with top:
        uid = [0]

        def sbt(es, name, shape, dt):
            uid[0] += 1
            return es.enter_context(nc.sbuf_tensor(f"{name}_u{uid[0]}", list(shape), dt))

        cv = sbt(top, "cv", [128, NCV], F32)
        modv = sbt(top, "modv", [128, DEPTH, 6, nseq, NC8], F32)
        ones_ln = sbt(top, "ones_ln", [128, 128], F32)
        ones_q = sbt(top, "ones_q", [128, 128], F32)
        ones_kv = sbt(top, "ones_kv", [128, 128], F32)
        psum = top.enter_context(nc.psum_tensor("psum", [128, 8 * 512], F32))
        PB = bufs(8, "psb")
        B_cv = Buf("cv")
        B_modv = {l_: Buf(f"modv{l_}") for l_ in range(DEPTH)}
        B_ones = Buf("ones")

        def bank(b, n=512):
            return psum[:, b * 512:b * 512 + n]

        def col(name, i=0):
            o_ = coff[name] + i
            return cv[:, o_:o_ + 1]

        op("sp", lambda e: e.dma_start(out=cv[:], in_=cvec_d), writes=[B_cv], semkey="cv")
        op("pool", lambda e: e.memset(ones_ln[:], 1.0 / D), writes=[B_ones])
        op("pool", lambda e: e.memset(ones_q[:], 1.0 / 768), writes=[B_ones])
        op("pool", lambda e: e.memset(ones_kv[:], 1.0 / 256), writes=[B_ones])

        cond = sbt(top, "cond", [128, NC8 * nseq], F32)
        B_cond = Buf("cond")
        op("sp", lambda e: e.dma_start(out=cond[:], in_=cT), writes=[B_cond], semkey="cond")
        op("act", lambda e: e.activation(out=cond[:], in_=cond[:], func=AF.Silu), reads=[B_cond], writes=[B_cond])
        mod_state = {"pi": 0}

        def mod_dma(l, j, mw, B_mw):
            pi = mod_state["pi"]
            mod_state["pi"] += 1
            s = pi % 2
            src = mod_w[l].rearrange("(kc p) n -> p kc n", p=128)[:, :, j * D:(j + 1) * D]
            op("sp", lambda e: e.dma_start(out=mw[s][:], in_=src), writes=[B_mw[s]], semkey=f"mw{s}")
            return (l, j, pi)

        def mod_compute(tok, mw, B_mw, pb0):
            l, j, pi = tok
            s = pi % 2
            pb = pb0 + pi % 2
            for oc in range(NC8):
                def mm(e, oc=oc):
                    ins = None
                    for kc in range(NC8):
                        ins = e.matmul(bank(pb)[:, oc * nseq:(oc + 1) * nseq], lhsT=mw[s][:, kc, oc * 128:(oc + 1) * 128],
                                       rhs=cond[:, kc * nseq:(kc + 1) * nseq], start=(kc == 0), stop=(kc == NC8 - 1))
                    return ins
                op("pe", mm, reads=[B_mw[s], B_cond], writes=[PB[pb]])
            for b in range(nseq):
                def ev(e, b=b):
                    src_ = bank(pb)[:, 0:NC8 * nseq].rearrange("p (o b) -> p o b", b=nseq)[:, :, b]
                    mb = cv[:, coff[f"modb{l}"] + j * NC8: coff[f"modb{l}"] + (j + 1) * NC8]
                    return e.tensor_tensor(out=modv[:, l, j, b, :], in0=src_, in1=mb, op=ALU.add)
                op("dve", ev, reads=[PB[pb], B_cv], writes=[B_modv[l]])
            if j == 5:
                for jj in (1, 2, 4, 5):
                    for b in range(nseq):
                        op("dve", (lambda jj=jj, b=b: lambda e: e.tensor_scalar(out=modv[:, l, jj, b, :], in0=modv[:, l, jj, b, :], scalar1=1.0, scalar2=None, op0=ALU.add))(),
                           reads=[B_modv[l]], writes=[B_modv[l]])
                if l % 3 == 0:
                    pj = l // 3
                    for b in range(nseq):
                        def gs(e, b=b, pj=pj):
                            ps_ = cv[:, coff[f"pscale{pj}"]: coff[f"pscale{pj}"] + NC8]
                            return e.tensor_tensor(out=modv[:, l, 2, b, :], in0=modv[:, l, 2, b, :], in1=ps_, op=ALU.mult)
                        op("dve", gs, reads=[B_modv[l], B_cv], writes=[B_modv[l]])

        def mod_piece(l, j, mw, B_mw, pb0):
            mod_compute(mod_dma(l, j, mw, B_mw), mw, B_mw, pb0)

        overlap_mod = (layers[0] % 3 == 0 and len(layers) > 1)
        pre_layers = [layers[0]] if overlap_mod else list(layers)
        deferred = [(l_, j_) for l_ in layers[1:] for j_ in range(6)] if overlap_mod else []
        with ExitStack() as es:
            mw = [sbt(es, f"mw{i}", [128, NC8, D], F32) for i in range(2)]
            B_mw = bufs(2, "mw")
            for l_ in pre_layers:
                for j_ in range(6):
                    mod_piece(l_, j_, mw, B_mw, 0)
            S_.flush()

        def mcol(l, j, b, c):
            return modv[:, l, j, b, c:c + 1]

        class LNRes:
            pass

        def ln_alloc(es, T):
            r = LNRes()
            r.sq = [sbt(es, f"ln_sq{i}", [128, T], F32) for i in range(2)]
            r.zs = sbt(es, "ln_zs", [128, T], F32)
            r.zq = sbt(es, "ln_zq", [128, T], F32)
            r.mean = sbt(es, "ln_mean", [128, T], F32)
            r.rstd = sbt(es, "ln_rstd", [128, T], F32)
            r.B_sq = bufs(2, "lnsq")
            r.B_zs = Buf("lnzs")
            r.B_zq = Buf("lnzq")
            r.B_mean = Buf("lnmean")
            r.B_rstd = Buf("lnrstd")
            r.T = T
            return r

        def ln_reduce(r, zc, Bz):
            op("pool", lambda e: e.tensor_tensor(out=r.zs[:], in0=zc[0], in1=zc[1], op=ALU.add), reads=[Bz[0], Bz[1]], writes=[r.B_zs])
            for c in range(2, NC8):
                op("pool", (lambda c=c: lambda e: e.tensor_tensor(out=r.zs[:], in0=r.zs[:], in1=zc[c], op=ALU.add))(), reads=[Bz[c], r.B_zs], writes=[r.B_zs])
            for c in range(NC8):
                s = c % 2
                op("act", (lambda c=c, s=s: lambda e: e.activation(out=r.sq[s][:], in_=zc[c], func=AF.Square))(), reads=[Bz[c]], writes=[r.B_sq[s]])
                if c == 1:
                    op("pool", lambda e: e.tensor_tensor(out=r.zq[:], in0=r.sq[0][:], in1=r.sq[1][:], op=ALU.add), reads=[r.B_sq[0], r.B_sq[1]], writes=[r.B_zq])
                elif c >= 2:
                    op("pool", (lambda s=s: lambda e: e.tensor_tensor(out=r.zq[:], in0=r.zq[:], in1=r.sq[s][:], op=ALU.add))(), reads=[r.B_sq[s], r.B_zq], writes=[r.B_zq])

        def ln_finish(r, zc, Bz, l, j, pb_m, pb_q):
            T = r.T
            op("pe", lambda e: e.matmul(bank(pb_m, T), lhsT=ones_ln[:], rhs=r.zs[:], start=True, stop=True), reads=[r.B_zs, B_ones], writes=[PB[pb_m]])
            op("pe", lambda e: e.matmul(bank(pb_q, T), lhsT=ones_ln[:], rhs=r.zq[:], start=True, stop=True), reads=[r.B_zq, B_ones], writes=[PB[pb_q]])
            op("act", lambda e: e.activation(out=r.mean[:], in_=bank(pb_m, T), func=AF.Copy), reads=[PB[pb_m]], writes=[r.B_mean])
            op("dve", lambda e: e.tensor_tensor(out=r.rstd[:], in0=r.mean[:], in1=r.mean[:], op=ALU.mult), reads=[r.B_mean], writes=[r.B_rstd])
            op("dve", lambda e: e.tensor_tensor(out=r.rstd[:], in0=bank(pb_q, T), in1=r.rstd[:], op=ALU.subtract), reads=[PB[pb_q], r.B_rstd], writes=[r.B_rstd])
            op("dve", lambda e: e.tensor_scalar(out=r.rstd[:], in0=r.rstd[:], scalar1=0.0, scalar2=None, op0=ALU.max), reads=[r.B_rstd], writes=[r.B_rstd])
            op("act", lambda e: e.activation(out=r.rstd[:], in_=r.rstd[:], func=AF.Sqrt, bias=col("eps_ln")), reads=[r.B_rstd, B_cv], writes=[r.B_rstd])
            op("dve", lambda e: e.reciprocal(out=r.rstd[:], in_=r.rstd[:]), reads=[r.B_rstd], writes=[r.B_rstd])
            for c in range(NC8):
                op("dve", (lambda c=c: lambda e: e.tensor_tensor(out=zc[c], in0=zc[c], in1=r.mean[:], op=ALU.subtract))(), reads=[Bz[c], r.B_mean], writes=[Bz[c]])
                op("dve", (lambda c=c: lambda e: e.tensor_tensor(out=zc[c], in0=zc[c], in1=r.rstd[:], op=ALU.mult))(), reads=[Bz[c], r.B_rstd], writes=[Bz[c]])
                op("act", (lambda c=c: lambda e: e.activation(out=zc[c], in_=zc[c], func=AF.Identity, scale=col(f"lng{l}_{j}", c), bias=col(f"lnb{l}_{j}", c)))(),
                   reads=[Bz[c], B_cv], writes=[Bz[c]])

        def tile_src(t, b, t0, T):
            return t[b].rearrange("(c p) s -> p c s", p=128)[:, :, t0:t0 + T]

        def run_tiles(tiles, LOAD, PRE, MAIN, POSTA=None, POSTB=None):
            n = len(tiles)
            LOAD(0, *tiles[0])
            PRE(0, *tiles[0])
            for i in range(n):
                if i + 1 < n:
                    LOAD(i + 1, *tiles[i + 1])
                done = [False]

                def pre_next(i=i, done=done):
                    if not done[0] and i + 1 < n:
                        PRE(i + 1, *tiles[i + 1])
                    done[0] = True
                MAIN(i, *tiles[i], pre_next)
                pre_next()
                if POSTB is not None and i > 0:
                    POSTB(i - 1, *tiles[i - 1])
                if POSTA is not None:
                    POSTA(i, *tiles[i])
            if POSTB is not None:
                POSTB(n - 1, *tiles[n - 1])

        def ffn_up_phase(l, src):
            T = 512 if S >= 512 else S
            NT = S // T
            HG = GC // 2
            with ExitStack() as es:
                wup = sbt(es, "wup", [128, NC8, 2 * FF], BF16)
                xin = [sbt(es, f"xin{i}", [128, NC8, T], F32) for i in range(2)]
                ub = [sbt(es, f"ub{i}", [128, NC8, T], BF16) for i in range(2)]
                gb = [sbt(es, f"gb{i}", [128, HG, T], BF16) for i in range(2)]
                NAB = 8
                ab = [sbt(es, f"ab{i}", [128, T], F32) for i in range(NAB)]
                sg = [sbt(es, f"sg{i}", [128, T], F32) for i in range(2)]
                tb = [sbt(es, f"tb{i}", [128, T], F32) for i in range(2)]
                B_tb = bufs(2, "tb")
                tails = [sbt(es, f"tails{i}", [128, HC, 2], F32) for i in range(2)]
                NWU = 4
                B_wup = bufs(NWU, "wup")
                B_xin = [bufs(NC8, f"xin{i}_") for i in range(2)]
                B_ub = [bufs(NC8, f"ub{i}_") for i in range(2)]
                B_gb = [Buf("gb0"), Buf("gb1")]
                B_ab = bufs(NAB, "ab")
                B_sg = bufs(2, "sg")
                B_tails = [bufs(HC, "tails0_"), bufs(HC, "tails1_")]
                wsrc = ffn_w_up[l].rearrange("(kc p) n -> p kc n", p=128)
                cw = 2 * FF // NWU
                for i in range(NWU):
                    op("pool", (lambda i=i: lambda e: e.dma_start(out=wup[:, :, i * cw:(i + 1) * cw], in_=wsrc[:, :, i * cw:(i + 1) * cw]))(), writes=[B_wup[i]], semkey=f"wA{i}")
                tiles = [(b, ti) for b in range(nseq) for ti in range(NT)]

                def LOAD(i, b, ti):
                    s = i % 2
                    op("sp", lambda e: e.dma_start(out=xin[s][:], in_=tile_src(src, b, ti * T, T)), writes=B_xin[s], semkey=f"ld{s}")

                def PRE(i, b, ti):
                    s = i % 2
                    for c in range(NC8):
                        op("dve", (lambda c=c: lambda e: e.tensor_scalar(out=ub[s][:, c, :], in0=xin[s][:, c, :], scalar1=mcol(l, 4, b, c), scalar2=mcol(l, 3, b, c), op0=ALU.mult, op1=ALU.add))(),
                           reads=[B_xin[s][c], B_modv[l]], writes=[B_ub[s][c]])

                def MAIN(i, b, ti, pre_next):
                    s = i % 2
                    par = i % 2
                    t0 = ti * T
                    if ti == 0:
                        op("pool", lambda e: e.memset(tails[1 - par][:], 0.0), writes=B_tails[1 - par])
                    pend = None

                    def glu(p, av, ag):
                        ss = p % 2
                        hf = p // HG
                        op("act", lambda e: e.activation(out=sg[ss][:], in_=ab[ag][:], func=AF.Silu), reads=[B_ab[ag]], writes=[B_sg[ss]])
                        op("pool", lambda e: e.tensor_tensor(out=gb[hf][:, p - hf * HG, :], in0=ab[av][:], in1=sg[ss][:], op=ALU.mult), reads=[B_ab[av], B_sg[ss]], writes=[B_gb[hf]])
                        if p % HG == HG - 1:
                            op("sp", lambda e: e.dma_start(out=g_d[b].rearrange("(c p) s -> p c s", p=128)[:, hf * HG:(hf + 1) * HG, t0:t0 + T], in_=gb[hf][:]), reads=[B_gb[hf]], semkey=f"sg{hf}")

                    for p in range(GC):
                        slots = []
                        for half in range(2):
                            c = p + half * GC
                            pb = (2 * p + half) % 6
                            a = (2 * p + half) % NAB
                            slots.append(a)

                            def mm(e, c=c, pb=pb):
                                ins = None
                                for kc in range(NC8):
                                    ins = e.matmul(bank(pb, T), lhsT=wup[:, kc, c * 128:(c + 1) * 128], rhs=ub[s][:, kc, :], start=(kc == 0), stop=(kc == NC8 - 1))
                                return ins
                            op("pe", mm, reads=[B_wup[c * 128 // cw]] + B_ub[s], writes=[PB[pb]])
                            op("act", (lambda c=c, pb=pb, a=a: lambda e: e.activation(out=ab[a][:], in_=bank(pb, T), func=AF.Identity, scale=col(f"fcw{l}_2", c), bias=col(f"fcb{l}", c)))(),
                               reads=[PB[pb], B_cv], writes=[B_ab[a]])
                            op("act", (lambda c=c, pb=pb: lambda e: e.activation(out=tails[par][:, c, :], in_=bank(pb, T)[:, T - 2:T], func=AF.Copy))(), reads=[PB[pb]], writes=[B_tails[par][c]])
                            if half == 0:
                                tt_ = p % 2
                                op("act", (lambda c=c, pb=pb, tt_=tt_: lambda e: e.activation(out=tb[tt_][:, 1:T], in_=bank(pb, T)[:, 0:T - 1], func=AF.Identity, scale=col(f"fcw{l}_1", c)))(),
                                   reads=[PB[pb], B_cv], writes=[B_tb[tt_]])
                                op("act", (lambda c=c, tt_=tt_: lambda e: e.activation(out=tb[tt_][:, 0:1], in_=tails[1 - par][:, c, 1:2], func=AF.Identity, scale=col(f"fcw{l}_1", c)))(),
                                   reads=[B_tails[1 - par][c], B_cv], writes=[B_tb[tt_]])
                                op("dve", (lambda c=c, pb=pb, a=a: lambda e: e.scalar_tensor_tensor(out=ab[a][:, 2:T], in0=bank(pb, T)[:, 0:T - 2], scalar=col(f"fcw{l}_0", c), in1=ab[a][:, 2:T], op0=ALU.mult, op1=ALU.add))(),
                                   reads=[PB[pb], B_ab[a], B_cv, B_tb[tt_]], writes=[B_ab[a]])
                                op("dve", (lambda c=c, a=a: lambda e: e.scalar_tensor_tensor(out=ab[a][:, 0:2], in0=tails[1 - par][:, c, 0:2], scalar=col(f"fcw{l}_0", c), in1=ab[a][:, 0:2], op0=ALU.mult, op1=ALU.add))(),
                                   reads=[B_tails[1 - par][c], B_ab[a], B_cv], writes=[B_ab[a]])
                                op("pool", (lambda a=a, tt_=tt_: lambda e: e.tensor_tensor(out=ab[a][:], in0=ab[a][:], in1=tb[tt_][:], op=ALU.add))(), reads=[B_ab[a], B_tb[tt_]], writes=[B_ab[a]])
                            else:
                                op("dve", (lambda c=c, pb=pb, a=a: lambda e: e.scalar_tensor_tensor(out=ab[a][:, 1:T], in0=bank(pb, T)[:, 0:T - 1], scalar=col(f"fcw{l}_1", c), in1=ab[a][:, 1:T], op0=ALU.mult, op1=ALU.add))(),
                                   reads=[PB[pb], B_ab[a], B_cv, B_tails[par][c]], writes=[B_ab[a]])
                                op("dve", (lambda c=c, pb=pb, a=a: lambda e: e.scalar_tensor_tensor(out=ab[a][:, 2:T], in0=bank(pb, T)[:, 0:T - 2], scalar=col(f"fcw{l}_0", c), in1=ab[a][:, 2:T], op0=ALU.mult, op1=ALU.add))(),
                                   reads=[PB[pb], B_ab[a], B_cv], writes=[B_ab[a]])
                                op("dve", (lambda c=c, a=a: lambda e: e.scalar_tensor_tensor(out=ab[a][:, 0:1], in0=tails[1 - par][:, c, 1:2], scalar=col(f"fcw{l}_1", c), in1=ab[a][:, 0:1], op0=ALU.mult, op1=ALU.add))(),
                                   reads=[B_tails[1 - par][c], B_ab[a], B_cv], writes=[B_ab[a]])
                                op("dve", (lambda c=c, a=a: lambda e: e.scalar_tensor_tensor(out=ab[a][:, 0:2], in0=tails[1 - par][:, c, 0:2], scalar=col(f"fcw{l}_0", c), in1=ab[a][:, 0:2], op0=ALU.mult, op1=ALU.add))(),
                                   reads=[B_tails[1 - par][c], B_ab[a], B_cv], writes=[B_ab[a]])
                        if pend is not None:
                            glu(*pend)
                        pend = (p, slots[0], slots[1])
                        if p == 14:
                            pre_next()
                    glu(*pend)

                run_tiles(tiles, LOAD, PRE, MAIN)
                S_.flush()

        def ffn_down_phase(l, src, dst):
            T = 512 if S >= 512 else S
            NT = S // T
            with ExitStack() as es:
                wdn = sbt(es, "wdn", [128, GC, D], BF16)
                zb = [sbt(es, f"zb{i}", [128, NC8, T], F32) for i in range(3)]
                gt = [sbt(es, f"gt{i}", [128, GC, T], BF16) for i in range(2)]
                lnr = ln_alloc(es, T)
                B_wdn = Buf("wdn")
                B_zb = [bufs(NC8, f"zb{i}_") for i in range(3)]
                B_gt = [Buf("gt0"), Buf("gt1")]
                op("pool", lambda e: e.dma_start(out=wdn[:], in_=ffn_w_down[l].rearrange("(kc p) n -> p kc n", p=128)), writes=[B_wdn], semkey="wB0")
                tiles = [(b, ti) for b in range(nseq) for ti in range(NT)]

                def LOAD(i, b, ti):
                    s = i % 2
                    sx = i % 3
                    op("sp", lambda e: e.dma_start(out=gt[s][:], in_=tile_src(g_d, b, ti * T, T)), writes=[B_gt[s]], semkey=f"lo{s}")
                    op("sp", lambda e: e.dma_start(out=zb[sx][:], in_=tile_src(src, b, ti * T, T)), writes=B_zb[sx], semkey=f"ld{sx}")

                def PRE(i, b, ti):
                    s = i % 2
                    sx = i % 3
                    for c in range(NC8):
                        op("act", (lambda c=c: lambda e: e.activation(out=zb[sx][:, c, :], in_=zb[sx][:, c, :], func=AF.Identity, scale=ALPHA))(), reads=[B_zb[sx][c]], writes=[B_zb[sx][c]])

                def MAIN(i, b, ti, pre_next):
                    s = i % 2
                    sx = i % 3
                    for oc in range(NC8):
                        pb = oc % 4

                        def mm2(e, oc=oc, pb=pb):
                            ins = None
                            for c in range(GC):
                                ins = e.matmul(bank(pb, T), lhsT=wdn[:, c, oc * 128:(oc + 1) * 128], rhs=gt[s][:, c, :], start=(c == 0), stop=(c == GC - 1))
                            return ins
                        op("pe", mm2, reads=[B_wdn, B_gt[s]], writes=[PB[pb]])
                        op("dve", (lambda oc=oc, pb=pb: lambda e: e.scalar_tensor_tensor(out=zb[sx][:, oc, :], in0=bank(pb, T), scalar=mcol(l, 5, b, oc), in1=zb[sx][:, oc, :], op0=ALU.mult, op1=ALU.add))(),
                           reads=[PB[pb], B_zb[sx][oc], B_modv[l]], writes=[B_zb[sx][oc]])

                def POSTA(i, b, ti):
                    sx = i % 3
                    ln_reduce(lnr, [zb[sx][:, c, :] for c in range(NC8)], B_zb[sx])

                def POSTB(i, b, ti):
                    sx = i % 3
                    ln_finish(lnr, [zb[sx][:, c, :] for c in range(NC8)], B_zb[sx], l, 1, 6, 7)
                    op("sp", lambda e: e.dma_start(out=tile_src(dst, b, ti * T, T), in_=zb[sx][:]), reads=B_zb[sx], semkey=f"st{sx}")

                run_tiles(tiles, LOAD, PRE, MAIN, POSTA, POSTB)
                S_.flush()

        def pool_phase(l, src, dst):
            T = 512 if S >= 512 else S
            NT = S // T
            H = 16
            E = H + T
            pj = l // 3
            with ExitStack() as es:
                pw = sbt(es, "pw", [128, 4, 2, 256], BF16)
                xw = [sbt(es, f"xw{i}", [128, NC8, H + T], F32) for i in range(3)]
                uw = sbt(es, "uw", [128, NC8, H + T], F32)
                scr = [sbt(es, f"pscr{i}", [128, 2, H + T], F32) for i in range(2)]
                pl = [sbt(es, f"pl{i}", [128, NC8, T], BF16) for i in range(2)]
                lnr = ln_alloc(es, T)
                inflight = []
                if deferred:
                    mwp = [sbt(es, f"mwp{i}", [128, NC8, D], F32) for i in range(2)]
                    B_mwp = bufs(2, "mwp")
                B_pw = Buf("pw")
                B_xw = [bufs(NC8, f"xw{i}_") for i in range(3)]
                B_uw = bufs(NC8, "uw")
                B_scr = bufs(2, "pscr")
                B_pl = [bufs(NC8, f"pl{i}_") for i in range(2)]
                op("pool", lambda e: e.dma_start(out=pw[:], in_=pool_w[pj].rearrange("g (kc p) n -> p g kc n", p=128)), writes=[B_pw], semkey="wA0")
                tiles = [(b, ti) for b in range(nseq) for ti in range(NT)]

                def LOAD(i, b, ti):
                    s = i % 2
                    sx = i % 3
                    t0 = ti * T
                    if ti == 0:
                        op("sp", lambda e: e.dma_start(out=xw[sx][:, :, H:H + T], in_=tile_src(src, b, t0, T)), writes=B_xw[sx], semkey=f"ld{sx}")
                    else:
                        op("sp", lambda e: e.dma_start(out=xw[sx][:], in_=tile_src(src, b, t0 - H, T + H)), writes=B_xw[sx], semkey=f"ld{sx}")

                def PRE(i, b, ti):
                    s = i % 2
                    sx = i % 3
                    lo = H if ti == 0 else 0
                    for c in range(NC8):
                        if ti == 0:
                            op("pool", (lambda c=c: lambda e: e.memset(uw[:, c, 0:H], 0.0))(), writes=[B_uw[c]])
                        op("act", (lambda c=c: lambda e: e.activation(out=uw[:, c, lo:E], in_=xw[sx][:, c, lo:E], func=AF.Identity, scale=mcol(l, 1, b, c), bias=mcol(l, 0, b, c)))(),
                           reads=[B_xw[sx][c], B_modv[l]], writes=[B_uw[c]])
                    for c in range(NC8):
                        op("act", (lambda c=c: lambda e: e.activation(out=xw[sx][:, c, H:E], in_=xw[sx][:, c, H:E], func=AF.Identity, scale=ALPHA))(), reads=[B_xw[sx][c]], writes=[B_xw[sx][c]])
                    for g in range(4):
                        w = 2 << g
                        cs = slice(2 * g, 2 * g + 2)
                        Bu = [B_uw[2 * g], B_uw[2 * g + 1]]
                        starts = {0: [16], 1: [14, 16], 2: [10, 12, 16], 3: [2, 4, 8, 16]}[g]
                        cur = None
                        SC, BSC, weng = (scr, B_scr, "dve")
                        for lvl, st in enumerate(starts):
                            sh = 1 << lvl
                            dsti = lvl % 2

                            def lv(e, cur=cur, dsti=dsti, st=st, sh=sh, cs=cs, SC=SC):
                                srcT = uw[:, cs, :] if cur is None else SC[cur][:, :, :]
                                return e.tensor_tensor(out=SC[dsti][:, :, st:E], in0=srcT[:, :, st:E], in1=srcT[:, :, st - sh:E - sh], op=ALU.add)
                            op(weng, lv, reads=(Bu if cur is None else [BSC[cur]]), writes=[BSC[dsti]])
                            cur = dsti
                        if ti == 0:
                            def cr(e, cur=cur, g=g, SC=SC):
                                cc = cv[:, coff["corr"] + g * 16: coff["corr"] + (g + 1) * 16]
                                ins = None
                                for k in range(2):
                                    ins = e.tensor_tensor(out=SC[cur][:, k, H:H + 16], in0=SC[cur][:, k, H:H + 16], in1=cc, op=ALU.mult)
                                return ins
                            op("dve", cr, reads=[BSC[cur], B_cv], writes=[BSC[cur]])
                        op("dve", (lambda cur=cur, cs=cs, w=w, SC=SC: lambda e: e.scalar_tensor_tensor(out=pl[s][:, cs, :], in0=SC[cur][:, :, H:E], scalar=1.0 / w, in1=uw[:, cs, H:E], op0=ALU.mult, op1=ALU.subtract))(),
                           reads=[BSC[cur]] + Bu, writes=[B_pl[s][2 * g], B_pl[s][2 * g + 1]])

                def MAIN(i, b, ti, pre_next):
                    s = i % 2
                    sx = i % 3
                    if deferred or inflight:
                        if inflight:
                            mod_compute(inflight.pop(0), mwp, B_mwp, 4)
                        if deferred:
                            inflight.append(mod_dma(*deferred.pop(0), mwp, B_mwp))
                    for g in range(4):
                        for oc in range(2):
                            c = 2 * g + oc
                            pb = c % 4

                            def mm(e, g=g, oc=oc, pb=pb):
                                ins = None
                                for kc in range(2):
                                    ins = e.matmul(bank(pb, T), lhsT=pw[:, g, kc, oc * 128:(oc + 1) * 128], rhs=pl[s][:, 2 * g + kc, :], start=(kc == 0), stop=(kc == 1))
                                return ins
                            op("pe", mm, reads=[B_pw, B_pl[s][2 * g], B_pl[s][2 * g + 1]], writes=[PB[pb]])
                            op("dve", (lambda c=c, pb=pb: lambda e: e.scalar_tensor_tensor(out=xw[sx][:, c, H:E], in0=bank(pb, T), scalar=mcol(l, 2, b, c), in1=xw[sx][:, c, H:E], op0=ALU.mult, op1=ALU.add))(),
                               reads=[PB[pb], B_xw[sx][c], B_modv[l]], writes=[B_xw[sx][c]])

                def POSTA(i, b, ti):
                    sx = i % 3
                    ln_reduce(lnr, [xw[sx][:, c, H:E] for c in range(NC8)], B_xw[sx])

                def POSTB(i, b, ti):
                    sx = i % 3
                    ln_finish(lnr, [xw[sx][:, c, H:E] for c in range(NC8)], B_xw[sx], l, 0, 6, 7)
                    op("sp", lambda e: e.dma_start(out=tile_src(dst, b, ti * T, T), in_=xw[sx][:, :, H:E]), reads=B_xw[sx], semkey=f"st{sx}")

                run_tiles(tiles, LOAD, PRE, MAIN, POSTA, POSTB)
                while deferred or inflight:
                    if inflight:
                        mod_compute(inflight.pop(0), mwp, B_mwp, 4)
                    if deferred:
                        inflight.append(mod_dma(*deferred.pop(0), mwp, B_mwp))
                S_.flush()

        def sconv_phase(l, src, dst):
            T = 512 if S >= 512 else S
            NT = S // T
            with ExitStack() as es:
                win = sbt(es, "win", [128, NC8, 3 * D], BF16)
                wout = sbt(es, "wout", [128, NC8, D], BF16)
                xw = [sbt(es, f"sxw{i}", [128, NC8, T], F32) for i in range(3)]
                ub = [sbt(es, f"sub{i}", [128, NC8, T], BF16) for i in range(2)]
                qb = sbt(es, "sqb", [128, NC8, T], BF16)
                t1 = [sbt(es, f"st1{i}", [128, T], F32) for i in range(2)]
                pbuf = [sbt(es, f"spb{i}", [128, T + 2], F32) for i in range(2)]
                ab = [sbt(es, f"sab{i}", [128, T], F32) for i in range(2)]
                ptl = sbt(es, "sptl", [128, NC8, 2], F32)
                lnr = ln_alloc(es, T)
                B_win = bufs(3, "win")
                B_wout = Buf("wout")
                B_xw = [bufs(NC8, f"sxw{i}_") for i in range(3)]
                B_ub = [bufs(NC8, f"sub{i}_") for i in range(2)]
                B_qb = bufs(NC8, "sqb")
                B_t1 = bufs(2, "st1")
                B_pb = bufs(2, "spb")
                B_ab = bufs(2, "sab")
                B_ptl = bufs(NC8, "sptl")
                wsrc = sc_w_in.rearrange("(kc p) n -> p kc n", p=128)
                for i in range(3):
                    op("pool", (lambda i=i: lambda e: e.dma_start(out=win[:, :, i * D:(i + 1) * D], in_=wsrc[:, :, i * D:(i + 1) * D]))(), writes=[B_win[i]], semkey=f"wA{i}")
                op("pool", lambda e: e.dma_start(out=wout[:], in_=sc_w_out.rearrange("(kc p) n -> p kc n", p=128)), writes=[B_wout], semkey="wB0")
                tiles = [(b, ti) for b in range(nseq) for ti in range(NT)]

                def LOAD(i, b, ti):
                    s = i % 2
                    sx = i % 3
                    op("sp", lambda e: e.dma_start(out=xw[sx][:], in_=tile_src(src, b, ti * T, T)), writes=B_xw[sx], semkey=f"ld{sx}")

                def PRE(i, b, ti):
                    s = i % 2
                    sx = i % 3
                    for c in range(NC8):
                        op("dve", (lambda c=c: lambda e: e.tensor_scalar(out=ub[s][:, c, :], in0=xw[sx][:, c, :], scalar1=mcol(l, 1, b, c), scalar2=mcol(l, 0, b, c), op0=ALU.mult, op1=ALU.add))(),
                           reads=[B_xw[sx][c], B_modv[l]], writes=[B_ub[s][c]])
                    for c in range(NC8):
                        op("act", (lambda c=c: lambda e: e.activation(out=xw[sx][:, c, :], in_=xw[sx][:, c, :], func=AF.Identity, scale=ALPHA))(), reads=[B_xw[sx][c]], writes=[B_xw[sx][c]])

                def MAIN(i, b, ti, pre_next):
                    s = i % 2
                    sx = i % 3
                    if ti == 0:
                        op("pool", lambda e: e.memset(ptl[:], 0.0), writes=B_ptl)
                    for c in range(NC8):
                        k2 = c % 2
                        banks3 = [0 + 3 * k2, 1 + 3 * k2, 2 + 3 * k2]
                        for which in range(3):
                            def mm(e, which=which, c=c, pbk=banks3[which]):
                                ins = None
                                for kc in range(NC8):
                                    ins = e.matmul(bank(pbk, T), lhsT=win[:, kc, which * D + c * 128: which * D + (c + 1) * 128], rhs=ub[s][:, kc, :], start=(kc == 0), stop=(kc == NC8 - 1))
                                return ins
                            op("pe", mm, reads=[B_win[which]] + B_ub[s], writes=[PB[banks3[which]]])
                        op("act", (lambda k2=k2, pbk=banks3[1]: lambda e: e.activation(out=t1[k2][:], in_=bank(pbk, T), func=AF.Copy))(), reads=[PB[banks3[1]]], writes=[B_t1[k2]])
                        op("pool", (lambda k2=k2, c=c: lambda e: e.tensor_copy(out=pbuf[k2][:, 0:2], in_=ptl[:, c, :]))(), reads=[B_ptl[c]], writes=[B_pb[k2]])
                        op("dve", (lambda k2=k2, pbk=banks3[2]: lambda e: e.tensor_tensor(out=pbuf[k2][:, 2:T + 2], in0=bank(pbk, T), in1=t1[k2][:], op=ALU.mult))(),
                           reads=[PB[banks3[2]], B_t1[k2]], writes=[B_pb[k2]])
                        op("pool", (lambda k2=k2, c=c: lambda e: e.tensor_copy(out=ptl[:, c, :], in_=pbuf[k2][:, T:T + 2]))(), reads=[B_pb[k2]], writes=[B_ptl[c]])
                        op("dve", (lambda k2=k2, c=c: lambda e: e.tensor_scalar(out=ab[k2][:], in0=pbuf[k2][:, 2:T + 2], scalar1=col("scw2", c), scalar2=None, op0=ALU.mult))(),
                           reads=[B_pb[k2], B_cv], writes=[B_ab[k2]])
                        op("dve", (lambda k2=k2, c=c: lambda e: e.scalar_tensor_tensor(out=ab[k2][:], in0=pbuf[k2][:, 1:T + 1], scalar=col("scw1", c), in1=ab[k2][:], op0=ALU.mult, op1=ALU.add))(),
                           reads=[B_pb[k2], B_ab[k2], B_cv], writes=[B_ab[k2]])
                        op("dve", (lambda k2=k2, c=c: lambda e: e.scalar_tensor_tensor(out=ab[k2][:], in0=pbuf[k2][:, 0:T], scalar=col("scw0", c), in1=ab[k2][:], op0=ALU.mult, op1=ALU.add))(),
                           reads=[B_pb[k2], B_ab[k2], B_cv], writes=[B_ab[k2]])
                        op("dve", (lambda k2=k2, c=c, pbk=banks3[0]: lambda e: e.tensor_tensor(out=qb[:, c, :], in0=bank(pbk, T), in1=ab[k2][:], op=ALU.mult))(),
                           reads=[PB[banks3[0]], B_ab[k2]], writes=[B_qb[c]])
                    pre_next()
                    for oc in range(NC8):
                        pb = 6 + oc % 2

                        def mm2(e, oc=oc, pb=pb):
                            ins = None
                            for kc in range(NC8):
                                ins = e.matmul(bank(pb, T), lhsT=wout[:, kc, oc * 128:(oc + 1) * 128], rhs=qb[:, kc, :], start=(kc == 0), stop=(kc == NC8 - 1))
                            return ins
                        op("pe", mm2, reads=[B_wout] + B_qb, writes=[PB[pb]])
                        op("dve", (lambda oc=oc, pb=pb: lambda e: e.scalar_tensor_tensor(out=xw[sx][:, oc, :], in0=bank(pb, T), scalar=mcol(l, 2, b, oc), in1=xw[sx][:, oc, :], op0=ALU.mult, op1=ALU.add))(),
                           reads=[PB[pb], B_xw[sx][oc], B_modv[l]], writes=[B_xw[sx][oc]])

                def POSTA(i, b, ti):
                    sx = i % 3
                    ln_reduce(lnr, [xw[sx][:, c, :] for c in range(NC8)], B_xw[sx])

                def POSTB(i, b, ti):
                    sx = i % 3
                    ln_finish(lnr, [xw[sx][:, c, :] for c in range(NC8)], B_xw[sx], l, 0, 6, 7)
                    op("sp", lambda e: e.dma_start(out=tile_src(dst, b, ti * T, T), in_=xw[sx][:]), reads=B_xw[sx], semkey=f"st{sx}")

                run_tiles(tiles, LOAD, PRE, MAIN, POSTA, POSTB)
                S_.flush()

        def mla_a1(l, src):
            T = 256 if S >= 256 else S
            NT = S // T
            with ExitStack() as es:
                wa = sbt(es, "wa", [128, NC8, 1056 + 32], BF16)
                wqp = sbt(es, "wqp", [128, 6, 1024], BF16)
                xw = [sbt(es, f"axw{i}", [128, NC8, T], F32) for i in range(2)]
                ub = [sbt(es, f"aub{i}", [128, NC8, T], BF16) for i in range(2)]
                cqf = sbt(es, "cqf", [128, 8, T], F32)
                sq = [sbt(es, f"asq{i}", [128, T], F32) for i in range(2)]
                rs = [sbt(es, f"ars{i}", [128, T], F32) for i in range(2)]
                cqn = [sbt(es, f"acqn{i}", [128, 8, T], BF16) for i in range(2)]
                cosT = sbt(es, "cosT", [128, S], F32)
                sinT = sbt(es, "sinT", [128, S], F32)
                tscr = sbt(es, "tscr", [128, S], F32)
                posi = sbt(es, "posi", [128, S], I32)
                tki = posi
                rt = [sbt(es, f"art{i}", [128, T], F32) for i in range(4)]
                rpe = [sbt(es, f"arpe{i}", [128, 5, T], BF16) for i in range(2)]
                B_wa = Buf("wa")
                B_wqp = Buf("wqp")
                B_xw = [bufs(NC8, f"axw{i}_") for i in range(2)]
                B_ub = [bufs(NC8, f"aub{i}_") for i in range(2)]
                B_cqf = bufs(8, "cqf")
                B_sq = bufs(2, "asq")
                B_rs = bufs(2, "ars")
                B_cqn = [bufs(8, f"acqn{i}_") for i in range(2)]
                B_tab = Buf("tab")
                B_tscr = Buf("tscr")
                B_rt = bufs(4, "art")
                B_rpe = [bufs(5, f"arpe{i}_") for i in range(2)]
                op("pool", lambda e: e.dma_start(out=wa[:, :, 0:1056], in_=w_a.rearrange("(kc p) n -> p kc n", p=128)), writes=[B_wa], semkey="wA0")
                op("pool", lambda e: e.dma_start(out=wa[:, :, 1056:1088], in_=w_a_sw.rearrange("(kc p) n -> p kc n", p=128)), writes=[B_wa], semkey="wA0")
                op("pool", lambda e: e.dma_start(out=wqp[:, :, 0:512], in_=w_uq_pe.rearrange("(kc p) n -> p kc n", p=128)), writes=[B_wqp], semkey="wA1")
                op("pool", lambda e: e.dma_start(out=wqp[:, :, 512:1024], in_=w_uq_pesw.rearrange("(kc p) n -> p kc n", p=128)), writes=[B_wqp], semkey="wA1")
                C1 = 6.28125
                C2 = float(2 * np.pi - 6.28125)
                PI = float(np.pi)

                def tables(b):
                    op("sp", lambda e: e.dma_start(out=posi[:], in_=pos[b:b + 1, :].partition_broadcast(128)), writes=[B_tscr], semkey="ld2")
                    op("dve", lambda e: e.tensor_copy(out=sinT[:], in_=posi[:]), reads=[B_tscr], writes=[B_tab])
                    op("dve", lambda e: e.tensor_scalar(out=sinT[:], in0=sinT[:], scalar1=col("invf"), scalar2=None, op0=ALU.mult), reads=[B_tab, B_cv], writes=[B_tab])
                    op("dve", lambda e: e.tensor_scalar(out=cosT[:], in0=sinT[:], scalar1=PI / 2, scalar2=None, op0=ALU.add), reads=[B_tab], writes=[B_tab])
                    for tb in (sinT, cosT):
                        op("dve", (lambda tb=tb: lambda e: e.tensor_scalar(out=tscr[:], in0=tb[:], scalar1=float(1 / (2 * np.pi)), scalar2=None, op0=ALU.mult))(), reads=[B_tab], writes=[B_tscr])
                        op("dve", lambda e: e.tensor_copy(out=tki[:], in_=tscr[:]), reads=[B_tscr], writes=[B_tscr])
                        op("dve", lambda e: e.tensor_copy(out=tscr[:], in_=tki[:]), reads=[B_tscr], writes=[B_tscr])
                        op("dve", (lambda tb=tb: lambda e: e.scalar_tensor_tensor(out=tb[:], in0=tscr[:], scalar=-C1, in1=tb[:], op0=ALU.mult, op1=ALU.add))(), reads=[B_tscr, B_tab], writes=[B_tab])
                        op("dve", (lambda tb=tb: lambda e: e.scalar_tensor_tensor(out=tb[:], in0=tscr[:], scalar=-C2, in1=tb[:], op0=ALU.mult, op1=ALU.add))(), reads=[B_tscr, B_tab], writes=[B_tab])
                        op("dve", (lambda tb=tb: lambda e: e.tensor_scalar(out=tscr[:], in0=tb[:], scalar1=PI, scalar2=float(-2 * np.pi), op0=ALU.is_gt, op1=ALU.mult))(), reads=[B_tab], writes=[B_tscr])
                        op("dve", (lambda tb=tb: lambda e: e.tensor_tensor(out=tb[:], in0=tb[:], in1=tscr[:], op=ALU.add))(), reads=[B_tscr, B_tab], writes=[B_tab])
                        op("dve", (lambda tb=tb: lambda e: e.tensor_scalar(out=tscr[:], in0=tb[:], scalar1=-PI, scalar2=float(2 * np.pi), op0=ALU.is_lt, op1=ALU.mult))(), reads=[B_tab], writes=[B_tscr])
                        op("dve", (lambda tb=tb: lambda e: e.tensor_tensor(out=tb[:], in0=tb[:], in1=tscr[:], op=ALU.add))(), reads=[B_tscr, B_tab], writes=[B_tab])
                        op("dve", (lambda tb=tb: lambda e: e.tensor_scalar(out=tb[:], in0=tb[:], scalar1=PI, scalar2=-PI, op0=ALU.min, op1=ALU.max))(), reads=[B_tab], writes=[B_tab])
                        op("act", (lambda tb=tb: lambda e: e.activation(out=tb[:], in_=tb[:], func=AF.Sin))(), reads=[B_tab], writes=[B_tab])
                    op("dve", lambda e: e.tensor_scalar(out=sinT[:], in0=sinT[:], scalar1=col("sgn"), scalar2=None, op0=ALU.mult), reads=[B_tab, B_cv], writes=[B_tab])

                tiles = [(b, ti) for b in range(nseq) for ti in range(NT)]

                def LOAD(i, b, ti):
                    s = i % 2
                    op("sp", lambda e: e.dma_start(out=xw[s][:], in_=tile_src(src, b, ti * T, T)), writes=B_xw[s], semkey=f"ld{s}")

                def PRE(i, b, ti):
                    s = i % 2
                    for c in range(NC8):
                        op("dve", (lambda c=c: lambda e: e.tensor_scalar(out=ub[s][:, c, :], in0=xw[s][:, c, :], scalar1=mcol(l, 1, b, c), scalar2=mcol(l, 0, b, c), op0=ALU.mult, op1=ALU.add))(),
                           reads=[B_xw[s][c], B_modv[l]], writes=[B_ub[s][c]])

                def MAIN(i, b, ti, pre_next):
                    s = i % 2
                    t0 = ti * T
                    if ti == 0:
                        tables(b)
                    for c in range(8):
                        pb = c % 3
                        grp = 0 if c < 6 else 1

                        def mm(e, c=c, pb=pb):
                            ins = None
                            for kc in range(NC8):
                                ins = e.matmul(bank(pb, T), lhsT=wa[:, kc, c * 128:(c + 1) * 128], rhs=ub[s][:, kc, :], start=(kc == 0), stop=(kc == NC8 - 1))
                            return ins
                        op("pe", mm, reads=[B_wa] + B_ub[s], writes=[PB[pb]])
                        op("act", (lambda c=c, pb=pb: lambda e: e.activation(out=cqf[:, c, :], in_=bank(pb, T), func=AF.Copy))(), reads=[PB[pb]], writes=[B_cqf[c]])
                        op("act", (lambda c=c, pb=pb: lambda e: e.activation(out=sq[c % 2][:], in_=bank(pb, T), func=AF.Square))(), reads=[PB[pb]], writes=[B_sq[c % 2]])
                        first = c in (0, 6)
                        last = c in (5, 7)
                        op("pe", (lambda c=c, grp=grp, first=first, last=last: lambda e: e.matmul(bank(6 + grp, T), lhsT=(ones_q if grp == 0 else ones_kv)[:], rhs=sq[c % 2][:], start=first, stop=last))(),
                           reads=[B_sq[c % 2], B_ones], writes=[PB[6 + grp]])
                    for grp in range(2):
                        op("dve", (lambda grp=grp: lambda e: e.tensor_scalar(out=rs[grp][:], in0=bank(6 + grp, T), scalar1=0.0, scalar2=None, op0=ALU.max))(), reads=[PB[6 + grp]], writes=[B_rs[grp]])
                        op("act", (lambda grp=grp: lambda e: e.activation(out=rs[grp][:], in_=rs[grp][:], func=AF.Sqrt, bias=col("eps_rms")))(), reads=[B_rs[grp], B_cv], writes=[B_rs[grp]])
                        op("dve", (lambda grp=grp: lambda e: e.reciprocal(out=rs[grp][:], in_=rs[grp][:]))(), reads=[B_rs[grp]], writes=[B_rs[grp]])
                    for c in range(8):
                        grp = 0 if c < 6 else 1
                        nm = col("qnorm", c) if c < 6 else col("kvnorm", c - 6)
                        op("dve", (lambda c=c, grp=grp, nm=nm: lambda e: e.scalar_tensor_tensor(out=cqn[s][:, c, :], in0=cqf[:, c, :], scalar=nm, in1=rs[grp][:], op0=ALU.mult, op1=ALU.mult))(),
                           reads=[B_cqf[c], B_rs[grp], B_cv], writes=[B_cqn[s][c]])
                    op("sp", lambda e: e.dma_start(out=cqn_d[b].rearrange("(c p) s -> p c s", p=128)[:, :, t0:t0 + T], in_=cqn[s][:, 0:6, :]), reads=B_cqn[s][0:6], semkey=f"st{s}")
                    op("sp", lambda e: e.dma_start(out=ckvn_d[b].rearrange("(c p) s -> p c s", p=128)[:, :, t0:t0 + T], in_=cqn[s][:, 6:8, :]), reads=B_cqn[s][6:8], semkey=f"st{s}")
                    pre_next()
                    for c in range(5):
                        pa = 3 if c % 2 == 0 else 0
                        pbk = 4 if c % 2 == 0 else 5
                        M = 128 if c < 4 else 32

                        def mmA(e, c=c, pa=pa):
                            ins = None
                            if c < 4:
                                for kc in range(6):
                                    ins = e.matmul(bank(pa, T), lhsT=wqp[:, kc, c * 128:(c + 1) * 128], rhs=cqn[s][:, kc, :], start=(kc == 0), stop=(kc == 5))
                            else:
                                for kc in range(NC8):
                                    ins = e.matmul(bank(pa, T)[0:32, :], lhsT=wa[:, kc, 1024:1056], rhs=ub[s][:, kc, :], start=(kc == 0), stop=(kc == NC8 - 1))
                            return ins

                        def mmB(e, c=c, pbk=pbk):
                            ins = None
                            if c < 4:
                                for kc in range(6):
                                    ins = e.matmul(bank(pbk, T), lhsT=wqp[:, kc, 512 + c * 128:512 + (c + 1) * 128], rhs=cqn[s][:, kc, :], start=(kc == 0), stop=(kc == 5))
                            else:
                                for kc in range(NC8):
                                    ins = e.matmul(bank(pbk, T)[0:32, :], lhsT=wa[:, kc, 1056:1088], rhs=ub[s][:, kc, :], start=(kc == 0), stop=(kc == NC8 - 1))
                            return ins
                        rd = (B_cqn[s][0:6] + [B_wqp]) if c < 4 else (B_ub[s] + [B_wa])
                        op("pe", mmA, reads=rd, writes=[PB[pa]])
                        op("pe", mmB, reads=rd, writes=[PB[pbk]])
                        r0 = (c % 2) * 2
                        op("dve", (lambda pa=pa, M=M, r0=r0: lambda e: e.tensor_tensor(out=rt[r0][0:M, :], in0=bank(pa, T)[0:M, :], in1=cosT[0:M, t0:t0 + T], op=ALU.mult))(),
                           reads=[PB[pa], B_tab], writes=[B_rt[r0]])
                        op("dve", (lambda pbk=pbk, M=M, r0=r0: lambda e: e.tensor_tensor(out=rt[r0 + 1][0:M, :], in0=bank(pbk, T)[0:M, :], in1=sinT[0:M, t0:t0 + T], op=ALU.mult))(),
                           reads=[PB[pbk], B_tab], writes=[B_rt[r0 + 1]])
                        op("pool", (lambda c=c, M=M, r0=r0: lambda e: e.tensor_tensor(out=rpe[s][0:M, c, :], in0=rt[r0][0:M, :], in1=rt[r0 + 1][0:M, :], op=ALU.add))(),
                           reads=[B_rt[r0], B_rt[r0 + 1]], writes=[B_rpe[s][c]])
                    op("sp", lambda e: e.dma_start(out=qpe_d[b].rearrange("(c p) s -> p c s", p=128)[:, :, t0:t0 + T], in_=rpe[s][:, 0:4, :]), reads=B_rpe[s][0:4], semkey=f"st{s}")
                    op("sp", lambda e: e.dma_start(out=kpe_d[b][:, t0:t0 + T], in_=rpe[s][0:32, 4, :]), reads=[B_rpe[s][4]], semkey=f"st{s}")

                run_tiles(tiles, LOAD, PRE, MAIN)
                S_.flush()

        def mla_a2():
            QT = 512 if S >= 512 else S
            NQ = S // QT
            NKT = S // 128
            KPQ = QT // 128
            with ExitStack() as es:
                wq = sbt(es, "wq", [128, 6, 1024], BF16)
                wk = sbt(es, "wk", [128, 2, 1024], BF16)
                wv = sbt(es, "wv", [128, 2, 1024], BF16)
                msk = sbt(es, "msk", [128, 4, 512], BF16)
                cq = sbt(es, "cq", [128, 6, S], BF16)
                ckv = sbt(es, "ckv", [128, 2, S], BF16)
                KTb = [sbt(es, f"KT{i}", [128, S], BF16) for i in range(2)]
                QTb = [sbt(es, f"QT{i}", [128, S], BF16) for i in range(2)]
                Vb = [sbt(es, f"V{i}", [128, NKT, 128], BF16) for i in range(2)]
                NPT = 4
                PT = [sbt(es, f"PT{i}", [128, QT], BF16) for i in range(NPT)]
                rden = sbt(es, "rden", [128, QT], F32)
                rden0 = sbt(es, "rden0", [128, QT], F32)
                ost = [sbt(es, f"ost{i}", [128, QT], BF16) for i in range(2)]
                B_w = Buf("a2w")
                B_msk = Buf("msk")
                B_cq = Buf("cq")
                B_ckv = Buf("ckv")
                B_KT = [Buf("KTn0"), Buf("KTn1")]
                B_KTp = [Buf("KTp0"), Buf("KTp1")]
                B_QT = [bufs(NQ, "QTn0_"), bufs(NQ, "QTn1_")]
                B_QTp = [Buf("QTp0"), Buf("QTp1")]
                B_V = [Buf("V0"), Buf("V1")]
                B_Vones = [Buf("Vo0"), Buf("Vo1")]
                B_PT = bufs(NPT, "PT")
                B_rden = Buf("rden")
                B_rden0 = Buf("rden0")
                B_ost = bufs(2, "ost")
                op("pool", lambda e: e.dma_start(out=wq[:], in_=w_uq_nope.rearrange("(kc p) n -> p kc n", p=128)), writes=[B_w], semkey="wA0")
                op("pool", lambda e: e.dma_start(out=wk[:], in_=w_uk.rearrange("(kc p) n -> p kc n", p=128)), writes=[B_w], semkey="wA0")
                op("pool", lambda e: e.dma_start(out=wv[:], in_=w_uv.rearrange("(kc p) n -> p kc n", p=128)), writes=[B_w], semkey="wA0")
                op("sp", lambda e: e.dma_start(out=msk[:], in_=masks_d), writes=[B_msk], semkey="ld2")
                for i in range(2):
                    op("pool", (lambda i=i: lambda e: e.memset(Vb[i][:, :, 64:128], 1.0))(), writes=[B_Vones[i]])

                def proj(b, h, sl):
                    if h == 0:
                        op("sp", lambda e: e.dma_start(out=cq[:], in_=cqn_d[b].rearrange("(c p) s -> p c s", p=128)), writes=[B_cq], semkey="ld0")
                        op("sp", lambda e: e.dma_start(out=ckv[:], in_=ckvn_d[b].rearrange("(c p) s -> p c s", p=128)), writes=[B_ckv], semkey="ld1")
                    op("sp", lambda e: e.dma_start(out=KTb[sl][64:96, :], in_=kpe_d[b]), writes=[B_KTp[sl]], semkey=f"kp{sl}")
                    op("sp", lambda e: e.dma_start(out=QTb[sl][64:96, :], in_=qpe_d[b][h * 32:(h + 1) * 32, :]), writes=[B_QTp[sl]], semkey=f"qp{sl}")
                    for qi in range(NQ):
                        pb = qi % 2

                        def mmk(e, qi=qi, pb=pb):
                            ins = None
                            for kc in range(2):
                                ins = e.matmul(bank(pb, QT)[0:64, :], lhsT=wk[:, kc, h * 64:(h + 1) * 64], rhs=ckv[:, kc, qi * QT:(qi + 1) * QT], start=(kc == 0), stop=(kc == 1))
                            return ins
                        op("pe", mmk, reads=[B_w, B_ckv], writes=[PB[pb]])
                        op("dve", (lambda qi=qi, pb=pb: lambda e: e.tensor_copy(out=KTb[sl][0:64, qi * QT:(qi + 1) * QT], in_=bank(pb, QT)[0:64, :]))(), reads=[PB[pb]], writes=[B_KT[sl]])
                    for qi in range(NQ):
                        pb = qi % 2

                        def mmq(e, qi=qi, pb=pb):
                            ins = None
                            for kc in range(6):
                                ins = e.matmul(bank(pb, QT)[0:64, :], lhsT=wq[:, kc, h * 64:(h + 1) * 64], rhs=cq[:, kc, qi * QT:(qi + 1) * QT], start=(kc == 0), stop=(kc == 5))
                            return ins
                        op("pe", mmq, reads=[B_w, B_cq], writes=[PB[pb]])
                        op("dve", (lambda qi=qi, pb=pb: lambda e: e.tensor_copy(out=QTb[sl][0:64, qi * QT:(qi + 1) * QT], in_=bank(pb, QT)[0:64, :]))(), reads=[PB[pb]], writes=[B_QT[sl][qi]])
                    for t8 in range(NKT // 8 if NKT >= 8 else 1):
                        nt8 = min(8, NKT)
                        pb = 2

                        def mmv(e, t8=t8, pb=pb, nt8=nt8):
                            ins = None
                            for k in range(nt8):
                                tc = t8 * 8 + k
                                for kc in range(2):
                                    ins = e.matmul(bank(pb)[:, k * 64:(k + 1) * 64], lhsT=ckv[:, kc, tc * 128:(tc + 1) * 128], rhs=wv[:, kc, h * 64:(h + 1) * 64], start=(kc == 0), stop=(kc == 1))
                            return ins
                        op("pe", mmv, reads=[B_w, B_ckv], writes=[PB[pb]])
                        op("dve", (lambda t8=t8, pb=pb, nt8=nt8: lambda e: e.tensor_copy(out=Vb[sl][:, t8 * 8:t8 * 8 + nt8, 0:64], in_=bank(pb)[:, 0:nt8 * 64].rearrange("p (k d) -> p k d", d=64)))(),
                           reads=[PB[pb]], writes=[B_V[sl]])

                cnt = [0]

                def emitS(b, h, sl, qi, ki):
                    k = cnt[0]
                    cnt[0] += 1
                    pbs = 3 + k % 3
                    pts = k % NPT
                    dg = ki - KPQ * qi
                    c0 = dg * 128 if dg > 0 else 0
                    op("pe", lambda e: e.matmul(bank(pbs, QT)[:, c0:QT], lhsT=KTb[sl][0:96, ki * 128:(ki + 1) * 128], rhs=QTb[sl][0:96, qi * QT + c0:(qi + 1) * QT], start=True, stop=True),
                       reads=[B_KT[sl], B_KTp[sl], B_QT[sl][qi], B_QTp[sl]], writes=[PB[pbs]])
                    op("act", lambda e: e.activation(out=PT[pts][:, c0:QT], in_=bank(pbs, QT)[:, c0:QT], func=AF.Exp, scale=SM_SCALE), reads=[PB[pbs]], writes=[B_PT[pts]])
                    if dg >= 0:
                        op("pool", lambda e: e.tensor_tensor(out=PT[pts][:, c0:c0 + 128], in0=PT[pts][:, c0:c0 + 128], in1=msk[:, 0, 0:128], op=ALU.mult), reads=[B_PT[pts], B_msk], writes=[B_PT[pts]])
                    return (pts, c0)

                def emitPV(b, h, sl, qi, ki, nk, st):
                    pts, c0 = st
                    po = 6 + qi % 2
                    op("pe", lambda e: e.matmul(bank(po, QT)[:, c0:QT], lhsT=Vb[sl][:, ki, :], rhs=PT[pts][:, c0:QT], start=(ki == 0), stop=(ki == nk - 1)),
                       reads=[B_V[sl], B_Vones[sl], B_PT[pts]], writes=[PB[po]])
                    if ki == nk - 1:
                        os_ = qi % 2
                        op("dve", lambda e: e.reciprocal(out=rden[64:128, :], in_=bank(po, QT)[64:128, :]), reads=[PB[po]], writes=[B_rden])
                        op("dve", lambda e: e.tensor_copy(out=rden0[0:64, :], in_=rden[64:128, :]), reads=[B_rden], writes=[B_rden0])
                        op("dve", lambda e: e.tensor_tensor(out=ost[os_][0:64, :], in0=bank(po, QT)[0:64, :], in1=rden0[0:64, :], op=ALU.mult), reads=[PB[po], B_rden0], writes=[B_ost[os_]])
                        op("sp", lambda e: e.dma_start(out=oT_d[b][h * 64:(h + 1) * 64, qi * QT:(qi + 1) * QT], in_=ost[os_][0:64, :]), reads=[B_ost[os_]], semkey=f"st{os_}")

                heads = [(b, h) for b in range(nseq) for h in range(HEADS)]
                LOOK = 2
                proj(heads[0][0], heads[0][1], 0)
                for gi, (b, h) in enumerate(heads):
                    sl = gi % 2
                    pairs = [(qi, ki) for qi in range(NQ) for ki in range(KPQ * (qi + 1))]
                    mid = len(pairs) // 2
                    states = {}
                    for j in range(min(LOOK, len(pairs))):
                        states[j] = emitS(b, h, sl, *pairs[j])
                    for j, (qi, ki) in enumerate(pairs):
                        if j + LOOK < len(pairs):
                            states[j + LOOK] = emitS(b, h, sl, *pairs[j + LOOK])
                        emitPV(b, h, sl, qi, ki, KPQ * (qi + 1), states.pop(j))
                        if j == mid and gi + 1 < len(heads):
                            proj(heads[gi + 1][0], heads[gi + 1][1], (gi + 1) % 2)
                S_.flush()

        def mla_a3(l, src, dst):
            T = 512 if S >= 512 else S
            NT = S // T
            with ExitStack() as es:
                wo = sbt(es, "wo", [128, NC8, D], BF16)
                xw = [sbt(es, f"oxw{i}", [128, NC8, T], F32) for i in range(3)]
                ob = [sbt(es, f"oob{i}", [128, NC8, T], BF16) for i in range(2)]
                lnr = ln_alloc(es, T)
                B_wo = Buf("wo")
                B_xw = [bufs(NC8, f"oxw{i}_") for i in range(3)]
                B_ob = [Buf("oob0"), Buf("oob1")]
                op("pool", lambda e: e.dma_start(out=wo[:], in_=w_o.rearrange("(kc p) n -> p kc n", p=128)), writes=[B_wo], semkey="wA0")
                tiles = [(b, ti) for b in range(nseq) for ti in range(NT)]

                def LOAD(i, b, ti):
                    s = i % 2
                    sx = i % 3
                    op("sp", lambda e: e.dma_start(out=xw[sx][:], in_=tile_src(src, b, ti * T, T)), writes=B_xw[sx], semkey=f"ld{sx}")
                    op("sp", lambda e: e.dma_start(out=ob[s][:], in_=tile_src(oT_d, b, ti * T, T)), writes=[B_ob[s]], semkey=f"lo{s}")

                def PRE(i, b, ti):
                    s = i % 2
                    sx = i % 3
                    for c in range(NC8):
                        op("act", (lambda c=c: lambda e: e.activation(out=xw[sx][:, c, :], in_=xw[sx][:, c, :], func=AF.Identity, scale=ALPHA))(), reads=[B_xw[sx][c]], writes=[B_xw[sx][c]])

                def MAIN(i, b, ti, pre_next):
                    s = i % 2
                    sx = i % 3
                    for oc in range(NC8):
                        pb = oc % 4

                        def mm(e, oc=oc, pb=pb):
                            ins = None
                            for kc in range(NC8):
                                ins = e.matmul(bank(pb, T), lhsT=wo[:, kc, oc * 128:(oc + 1) * 128], rhs=ob[s][:, kc, :], start=(kc == 0), stop=(kc == NC8 - 1))
                            return ins
                        op("pe", mm, reads=[B_wo, B_ob[s]], writes=[PB[pb]])
                        op("dve", (lambda oc=oc, pb=pb: lambda e: e.scalar_tensor_tensor(out=xw[sx][:, oc, :], in0=bank(pb, T), scalar=mcol(l, 2, b, oc), in1=xw[sx][:, oc, :], op0=ALU.mult, op1=ALU.add))(),
                           reads=[PB[pb], B_xw[sx][oc], B_modv[l]], writes=[B_xw[sx][oc]])

                def POSTA(i, b, ti):
                    sx = i % 3
                    ln_reduce(lnr, [xw[sx][:, c, :] for c in range(NC8)], B_xw[sx])

                def POSTB(i, b, ti):
                    sx = i % 3
                    ln_finish(lnr, [xw[sx][:, c, :] for c in range(NC8)], B_xw[sx], l, 0, 6, 7)
                    op("sp", lambda e: e.dma_start(out=tile_src(dst, b, ti * T, T), in_=xw[sx][:]), reads=B_xw[sx], semkey=f"st{sx}")

                run_tiles(tiles, LOAD, PRE, MAIN, POSTA, POSTB)
                S_.flush()

        phases = []
        for l in layers:
            phases.append(("mix", l))
            phases.append(("ffn", l))
        if stop_after is not None:
            phases = phases[:stop_after]
        cur = xT
        pp = [sA, sB]
        for pi_, (kind, l) in enumerate(phases):
            dst = yT if pi_ == len(phases) - 1 else pp[pi_ % 2]
            if kind == "ffn":
                ffn_up_phase(l, cur)
                ffn_down_phase(l, cur, dst)
            elif l % 3 == 0:
                pool_phase(l, cur, dst)
            elif l % 3 == 1:
                mla_a1(l, cur)
                mla_a2()
                mla_a3(l, cur, dst)
            else:
                sconv_phase(l, cur, dst)
            cur = dst
    return nc


def host_weights(inp):
    w = {}
    f32 = lambda a: np.ascontiguousarray(np.asarray(a, dtype=np.float32))
    w["cvec"] = cvec_layout(inp).build()
    w["masks"] = make_masks()
    w["mod_w"] = f32(inp["mod_w"])
    w["pool_w"] = f32(inp["pool_w"])
    wa = np.asarray(inp["mla_w_a"][0], np.float32)
    w["w_a"] = f32(wa)
    w["w_a_sw"] = f32(np.concatenate([wa[:, 1040:1056], wa[:, 1024:1040]], axis=1))
    wuq = np.asarray(inp["mla_w_uq"][0], np.float32).reshape(768, HEADS, 96)
    w["w_uq_nope"] = f32(wuq[:, :, 0:64].reshape(768, 1024))
    w["w_uq_pe"] = f32(wuq[:, :, 64:96].reshape(768, 512))
    w["w_uq_pesw"] = f32(np.concatenate([wuq[:, :, 80:96], wuq[:, :, 64:80]], axis=2).reshape(768, 512))
    wukv = np.asarray(inp["mla_w_ukv"][0], np.float32).reshape(256, HEADS, 128)
    w["w_uk"] = f32(wukv[:, :, 0:64].reshape(256, 1024))
    w["w_uv"] = f32(wukv[:, :, 64:128].reshape(256, 1024))
    w["w_o"] = f32(inp["mla_w_o"][0])
    w["sc_w_in"] = f32(inp["sc_w_in"][0])
    w["sc_w_out"] = f32(inp["sc_w_out"][0])
    w["ffn_w_up"] = f32(inp["ffn_w_up"])
    w["ffn_w_down"] = f32(inp["ffn_w_down"])
    return w


def core_inputs(inp, w, b0, nseq, S):
    x = np.asarray(inp["x"], np.float32)[b0:b0 + nseq, :S]
    m = dict(w)
    m["xT"] = np.ascontiguousarray(x.transpose(0, 2, 1))
    c = np.asarray(inp["c"], np.float32)[b0:b0 + nseq]
    m["cT"] = np.ascontiguousarray(c.reshape(nseq, NC8, 128).transpose(2, 1, 0).reshape(128, NC8 * nseq))
    m["pos"] = np.ascontiguousarray(np.asarray(inp["positions"], np.int32)[b0:b0 + nseq, :S])
    return m


_PROG_CACHE = {}


def kernel(**inputs):
    B, S, _ = inputs["x"].shape
    ncores = 8
    nseq = B // ncores
    key = (nseq, S)
    if key not in _PROG_CACHE:
        _PROG_CACHE[key] = build_program(nseq, S)
    nc = _PROG_CACHE[key]
    w = host_weights(inputs)
    in_maps = [core_inputs(inputs, w, i * nseq, nseq, S) for i in range(ncores)]
    res = run_bass_kernel_spmd(nc, in_maps, core_ids=list(range(ncores)))
    out = np.empty((B, S, D), np.float32)
    for i in range(ncores):
        out[i * nseq:(i + 1) * nseq] = res.results[i]["yT"].transpose(0, 2, 1)
    return out
```

```python
import numpy as np
import ml_dtypes
import concourse.bass as bass
import concourse.mybir as mybir
from concourse.bass_utils import run_bass_kernel_spmd
from contextlib import ExitStack

F32 = mybir.dt.float32
BF16 = mybir.dt.bfloat16
I32 = mybir.dt.int32
AF = mybir.ActivationFunctionType
ALU = mybir.AluOpType

D = 1024
DEPTH = 4
NC8 = 8
FF = 2816
HC = 44
GC = 22
HEADS = 16
ALPHA = float((2 * DEPTH) ** 0.25)
LN_EPS = 1e-5
RMS_EPS = 1e-6
SM_SCALE = float(96 ** -0.5)


class Buf:
    __slots__ = ("name", "w", "r")

    def __init__(self, name=""):
        self.name = name
        self.w = None
        self.r = []


def bufs(n, name=""):
    return [Buf(f"{name}{i}") for i in range(n)]


class Op:
    __slots__ = ("eng", "fn", "deps", "dwaits", "sig", "idx", "semkey", "phase")


class Sched:
    ENGS = ("pe", "act", "dve", "pool", "sp")

    def __init__(self, nc, top, same_engine_sync=False):
        self.nc = nc
        self.top = top
        self.ops = {e: [] for e in self.ENGS}
        self.last_real = {e: None for e in self.ENGS}
        self.dma_cnt = {}
        self.same_engine_sync = same_engine_sync
        self.phase = 0
        self.esem = None
        self.dsem = {}
        self.sigbase = {e: 0 for e in self.ENGS}
        self.waited = {e: {} for e in self.ENGS}
        self.nops = 0

    def op(self, eng, fn, reads=(), writes=(), semkey=None):
        o = Op()
        o.eng = eng
        o.fn = fn
        o.deps = set()
        o.dwaits = {}
        o.sig = False
        o.idx = 0
        o.semkey = semkey
        o.phase = self.phase
        for b in reads:
            if b.w is not None:
                self._dep(o, b.w)
        for b in writes:
            if b.w is not None:
                self._dep(o, b.w)
            for r in b.r:
                self._dep(o, r)
        for b in reads:
            b.r.append(o)
        for b in writes:
            b.w = o
            b.r = []
        if semkey is not None:
            self.dma_cnt[semkey] = self.dma_cnt.get(semkey, 0) + 1
        self.ops[eng].append(o)
        self.last_real[eng] = o
        self.nops += 1
        return o

    def _dep(self, o, d):
        if d is o:
            return
        if d.phase != self.phase:
            return
        if d.semkey is not None:
            k = d.semkey
            o.dwaits[k] = max(o.dwaits.get(k, 0), self.dma_cnt[k])
            return
        if d.eng == o.eng and (d.eng == "pe" or not self.same_engine_sync):
            return
        d.sig = True
        o.deps.add(d)

    def barrier(self):
        lasts = [self.last_real[e] for e in self.ENGS if self.last_real[e] is not None and self.last_real[e].phase == self.phase]
        for e in self.ENGS:
            o = Op()
            o.eng = e
            o.fn = None
            o.deps = set()
            o.dwaits = dict(self.dma_cnt)
            o.sig = False
            o.idx = 0
            o.semkey = None
            o.phase = self.phase
            for d in lasts:
                if d.semkey is not None or d.eng == e:
                    continue
                d.sig = True
                o.deps.add(d)
            self.ops[e].append(o)

    def flush(self):
        nc = self.nc
        self.barrier()
        if self.esem is None:
            self.esem = {e: self.top.enter_context(nc.semaphore("s_" + e)) for e in self.ENGS if e != "sp"}
        for k in self.dma_cnt:
            if k not in self.dsem:
                self.dsem[k] = self.top.enter_context(nc.semaphore("d_" + str(k)))
        esem, dsem = self.esem, self.dsem
        for e in self.ENGS:
            c = self.sigbase[e]
            for o in self.ops[e]:
                if o.sig:
                    c += 1
                    o.idx = c
            self.sigbase[e] = c
        ops = self.ops
        waited_all = self.waited

        def run(engname):
            def body(eng):
                waited = waited_all[engname]
                for o in ops[engname]:
                    ws = {}
                    for d in o.deps:
                        s = esem[d.eng]
                        if ws.get(s, 0) < d.idx:
                            ws[s] = d.idx
                    for k, v in o.dwaits.items():
                        s = dsem[k]
                        if ws.get(s, 0) < 16 * v:
                            ws[s] = 16 * v
                    for s, v in ws.items():
                        if waited.get(s, 0) < v:
                            eng.wait_ge(s, v)
                            waited[s] = v
                    if o.fn is None:
                        continue
                    ins = o.fn(eng)
                    if o.semkey is not None:
                        ins.then_inc(dsem[o.semkey], 16)
                    elif o.sig:
                        ins.then_inc(esem[engname], 1)
            return body

        with nc.Block() as block:
            block.tensor(run("pe"))
            block.scalar(run("act"))
            block.vector(run("dve"))
            block.gpsimd(run("pool"))
            block.sync(run("sp"))
        self.ops = {e: [] for e in self.ENGS}
        self.phase += 1


def colpack(vec):
    v = np.asarray(vec, dtype=np.float32).reshape(-1, 128)
    return np.ascontiguousarray(v.T)


class ColTable:
    def __init__(self):
        self.blocks = []
        self.off = {}
        self.n = 0

    def add(self, name, arr):
        arr = np.asarray(arr, dtype=np.float32)
        assert arr.shape[0] == 128
        self.off[name] = self.n
        self.blocks.append(arr)
        self.n += arr.shape[1]

    def build(self):
        return np.ascontiguousarray(np.concatenate(self.blocks, axis=1))


def cvec_layout(inp=None):
    ct = ColTable()
    z = lambda n: np.zeros((n,), np.float32)
    g = (lambda k, n: inp[k]) if inp is not None else None
    for l in range(DEPTH):
        for j in range(2):
            ct.add(f"lng{l}_{j}", colpack(inp["ln_g"][l, j] if inp else z(D)))
            ct.add(f"lnb{l}_{j}", colpack(inp["ln_b"][l, j] if inp else z(D)))
        ct.add(f"modb{l}", colpack(inp["mod_b"][l] if inp else z(6 * D)))
        for k in range(3):
            ct.add(f"fcw{l}_{k}", colpack(inp["ffn_conv"][l, k] if inp else z(2 * FF)))
        ct.add(f"fcb{l}", colpack(inp["ffn_conv_b"][l] if inp else z(2 * FF)))
    for j in range(2):
        ct.add(f"pscale{j}", colpack(inp["pool_scale"][j] if inp else z(D)))
    ct.add("qnorm", colpack(inp["mla_q_norm"][0] if inp else z(768)))
    ct.add("kvnorm", colpack(inp["mla_kv_norm"][0] if inp else z(256)))
    for k in range(3):
        ct.add(f"scw{k}", colpack(inp["sc_conv"][0, k] if inp else z(D)))
    p = np.arange(128)
    invf = (10000.0 ** (-np.arange(0, 32, 2, dtype=np.float32) / 32)).astype(np.float32)
    ct.add("invf", invf[p % 16].reshape(128, 1))
    ct.add("sgn", np.where((p % 32) < 16, -1.0, 1.0).astype(np.float32).reshape(128, 1))
    corr = np.zeros((128, 64), np.float32)
    for gi, w in enumerate((2, 4, 8, 16)):
        t = np.arange(16)
        corr[:, gi * 16:(gi + 1) * 16] = (w / np.minimum(t + 1, w)).astype(np.float32)[None, :]
    ct.add("corr", corr)
    ct.add("eps_ln", np.full((128, 1), LN_EPS, np.float32))
    ct.add("eps_rms", np.full((128, 1), RMS_EPS, np.float32))
    return ct


def make_masks():
    k = np.arange(128)[:, None]
    q = np.arange(512)[None, :]
    m = np.stack([((d * 128 + k) <= q) for d in range(4)], axis=1)
    return np.ascontiguousarray(m.astype(np.float32).astype(ml_dtypes.bfloat16))


def build_program(nseq, S, layers=(0, 1, 2, 3), stop_after=None):
    nc = bass.Bass("TRN2", target_bir_lowering=False)
    NCV = cvec_layout().n
    coff = cvec_layout().off

    def din(name, shape, dt=F32):
        return nc.dram_tensor(name, list(shape), dt, kind="ExternalInput").ap()

    def dscr(name, shape, dt=F32):
        return nc.dram_tensor(name, list(shape), dt).ap()

    xT = din("xT", [nseq, D, S])
    cT = din("cT", [128, NC8 * nseq])
    pos = din("pos", [nseq, S], I32)
    cvec_d = din("cvec", [128, NCV])
    masks_d = din("masks", [128, 4, 512], BF16)
    mod_w = din("mod_w", [DEPTH, D, 6 * D])
    pool_w = din("pool_w", [2, 4, 256, 256])
    w_a = din("w_a", [D, 1056])
    w_a_sw = din("w_a_sw", [D, 32])
    w_uq_nope = din("w_uq_nope", [768, 1024])
    w_uq_pe = din("w_uq_pe", [768, 512])
    w_uq_pesw = din("w_uq_pesw", [768, 512])
    w_uk = din("w_uk", [256, 1024])
    w_uv = din("w_uv", [256, 1024])
    w_o = din("w_o", [D, D])
    sc_w_in = din("sc_w_in", [D, 3 * D])
    sc_w_out = din("sc_w_out", [D, D])
    ffn_w_up = din("ffn_w_up", [DEPTH, D, 2 * FF])
    ffn_w_down = din("ffn_w_down", [DEPTH, FF, D])
    yT = nc.dram_tensor("yT", [nseq, D, S], F32, kind="ExternalOutput").ap()
    sA = dscr("sA", [nseq, D, S])
    sB = dscr("sB", [nseq, D, S])
    cqn_d = dscr("cqn_d", [nseq, 768, S], BF16)
    ckvn_d = dscr("ckvn_d", [nseq, 256, S], BF16)
    kpe_d = dscr("kpe_d", [nseq, 32, S], BF16)
    qpe_d = dscr("qpe_d", [nseq, 512, S], BF16)
    oT_d = dscr("oT_d", [nseq, D, S], BF16)
    g_d = dscr("g_d", [nseq, FF, S], BF16)

    top = ExitStack()
    S_ = Sched(nc, top)
    op = S_.op
    with top:
        uid = [0]

        def sbt(es, name, shape, dt):
            uid[0] += 1
            return es.enter_context(nc.sbuf_tensor(f"{name}_u{uid[0]}", list(shape), dt))

        cv = sbt(top, "cv", [128, NCV], F32)
        modv = sbt(top, "modv", [128, DEPTH, 6, nseq, NC8], F32)
        ones_ln = sbt(top, "ones_ln", [128, 128], F32)
        ones_q = sbt(top, "ones_q", [128, 128], F32)
        ones_kv = sbt(top, "ones_kv", [128, 128], F32)
        psum = top.enter_context(nc.psum_tensor("psum", [128, 8 * 512], F32))
        PB = bufs(8, "psb")
        B_cv = Buf("cv")
        B_modv = {l_: Buf(f"modv{l_}") for l_ in range(DEPTH)}
        B_ones = Buf("ones")

        def bank(b, n=512):
            return psum[:, b * 512:b * 512 + n]

        def col(name, i=0):
            o_ = coff[name] + i
            return cv[:, o_:o_ + 1]

        op("sp", lambda e: e.dma_start(out=cv[:], in_=cvec_d), writes=[B_cv], semkey="cv")
        op("pool", lambda e: e.memset(ones_ln[:], 1.0 / D), writes=[B_ones])
        op("pool", lambda e: e.memset(ones_q[:], 1.0 / 768), writes=[B_ones])
        op("pool", lambda e: e.memset(ones_kv[:], 1.0 / 256), writes=[B_ones])

        cond = sbt(top, "cond", [128, NC8 * nseq], F32)
        B_cond = Buf("cond")
        op("sp", lambda e: e.dma_start(out=cond[:], in_=cT), writes=[B_cond], semkey="cond")
        op("act", lambda e: e.activation(out=cond[:], in_=cond[:], func=AF.Silu), reads=[B_cond], writes=[B_cond])
        mod_state = {"pi": 0}

        def mod_dma(l, j, mw, B_mw):
            pi = mod_state["pi"]
            mod_state["pi"] += 1
            s = pi % 2
            src = mod_w[l].rearrange("(kc p) n -> p kc n", p=128)[:, :, j * D:(j + 1) * D]
            op("sp", lambda e: e.dma_start(out=mw[s][:], in_=src), writes=[B_mw[s]], semkey=f"mw{s}")
            return (l, j, pi)

        def mod_compute(tok, mw, B_mw, pb0):
            l, j, pi = tok
            s = pi % 2
            pb = pb0 + pi % 2
            for oc in range(NC8):
                def mm(e, oc=oc):
                    ins = None
                    for kc in range(NC8):
                        ins = e.matmul(bank(pb)[:, oc * nseq:(oc + 1) * nseq], lhsT=mw[s][:, kc, oc * 128:(oc + 1) * 128],
                                       rhs=cond[:, kc * nseq:(kc + 1) * nseq], start=(kc == 0), stop=(kc == NC8 - 1))
                    return ins
                op("pe", mm, reads=[B_mw[s], B_cond], writes=[PB[pb]])
            for b in range(nseq):
                def ev(e, b=b):
                    src_ = bank(pb)[:, 0:NC8 * nseq].rearrange("p (o b) -> p o b", b=nseq)[:, :, b]
                    mb = cv[:, coff[f"modb{l}"] + j * NC8: coff[f"modb{l}"] + (j + 1) * NC8]
                    return e.tensor_tensor(out=modv[:, l, j, b, :], in0=src_, in1=mb, op=ALU.add)
                op("dve", ev, reads=[PB[pb], B_cv], writes=[B_modv[l]])
            if j == 5:
                for jj in (1, 2, 4, 5):
                    for b in range(nseq):
                        op("dve", (lambda jj=jj, b=b: lambda e: e.tensor_scalar(out=modv[:, l, jj, b, :], in0=modv[:, l, jj, b, :], scalar1=1.0, scalar2=None, op0=ALU.add))(),
                           reads=[B_modv[l]], writes=[B_modv[l]])
                if l % 3 == 0:
                    pj = l // 3
                    for b in range(nseq):
                        def gs(e, b=b, pj=pj):
                            ps_ = cv[:, coff[f"pscale{pj}"]: coff[f"pscale{pj}"] + NC8]
                            return e.tensor_tensor(out=modv[:, l, 2, b, :], in0=modv[:, l, 2, b, :], in1=ps_, op=ALU.mult)
                        op("dve", gs, reads=[B_modv[l], B_cv], writes=[B_modv[l]])

        def mod_piece(l, j, mw, B_mw, pb0):
            mod_compute(mod_dma(l, j, mw, B_mw), mw, B_mw, pb0)

        overlap_mod = (layers[0] % 3 == 0 and len(layers) > 1)
        pre_layers = [layers[0]] if overlap_mod else list(layers)
        deferred = [(l_, j_) for l_ in layers[1:] for j_ in range(6)] if overlap_mod else []
        with ExitStack() as es:
            mw = [sbt(es, f"mw{i}", [128, NC8, D], F32) for i in range(2)]
            B_mw = bufs(2, "mw")
            for l_ in pre_layers:
                for j_ in range(6):
                    mod_piece(l_, j_, mw, B_mw, 0)
            S_.flush()

        def mcol(l, j, b, c):
            return modv[:, l, j, b, c:c + 1]

        class LNRes:
            pass

        def ln_alloc(es, T):
            r = LNRes()
            r.sq = [sbt(es, f"ln_sq{i}", [128, T], F32) for i in range(2)]
            r.zs = sbt(es, "ln_zs", [128, T], F32)
            r.zq = sbt(es, "ln_zq", [128, T], F32)
            r.mean = sbt(es, "ln_mean", [128, T], F32)
            r.rstd = sbt(es, "ln_rstd", [128, T], F32)
            r.B_sq = bufs(2, "lnsq")
            r.B_zs = Buf("lnzs")
            r.B_zq = Buf("lnzq")
            r.B_mean = Buf("lnmean")
            r.B_rstd = Buf("lnrstd")
            r.T = T
            return r

        def ln_reduce(r, zc, Bz):
            op("pool", lambda e: e.tensor_tensor(out=r.zs[:], in0=zc[0], in1=zc[1], op=ALU.add), reads=[Bz[0], Bz[1]], writes=[r.B_zs])
            for c in range(2, NC8):
                op("pool", (lambda c=c: lambda e: e.tensor_tensor(out=r.zs[:], in0=r.zs[:], in1=zc[c], op=ALU.add))(), reads=[Bz[c], r.B_zs], writes=[r.B_zs])
            for c in range(NC8):
                s = c % 2
                op("act", (lambda c=c, s=s: lambda e: e.activation(out=r.sq[s][:], in_=zc[c], func=AF.Square))(), reads=[Bz[c]], writes=[r.B_sq[s]])
                if c == 1:
                    op("pool", lambda e: e.tensor_tensor(out=r.zq[:], in0=r.sq[0][:], in1=r.sq[1][:], op=ALU.add), reads=[r.B_sq[0], r.B_sq[1]], writes=[r.B_zq])
                elif c >= 2:
                    op("pool", (lambda s=s: lambda e: e.tensor_tensor(out=r.zq[:], in0=r.zq[:], in1=r.sq[s][:], op=ALU.add))(), reads=[r.B_sq[s], r.B_zq], writes=[r.B_zq])

        def ln_finish(r, zc, Bz, l, j, pb_m, pb_q):
            T = r.T
            op("pe", lambda e: e.matmul(bank(pb_m, T), lhsT=ones_ln[:], rhs=r.zs[:], start=True, stop=True), reads=[r.B_zs, B_ones], writes=[PB[pb_m]])
            op("pe", lambda e: e.matmul(bank(pb_q, T), lhsT=ones_ln[:], rhs=r.zq[:], start=True, stop=True), reads=[r.B_zq, B_ones], writes=[PB[pb_q]])
            op("act", lambda e: e.activation(out=r.mean[:], in_=bank(pb_m, T), func=AF.Copy), reads=[PB[pb_m]], writes=[r.B_mean])
            op("dve", lambda e: e.tensor_tensor(out=r.rstd[:], in0=r.mean[:], in1=r.mean[:], op=ALU.mult), reads=[r.B_mean], writes=[r.B_rstd])
            op("dve", lambda e: e.tensor_tensor(out=r.rstd[:], in0=bank(pb_q, T), in1=r.rstd[:], op=ALU.subtract), reads=[PB[pb_q], r.B_rstd], writes=[r.B_rstd])
            op("dve", lambda e: e.tensor_scalar(out=r.rstd[:], in0=r.rstd[:], scalar1=0.0, scalar2=None, op0=ALU.max), reads=[r.B_rstd], writes=[r.B_rstd])
            op("act", lambda e: e.activation(out=r.rstd[:], in_=r.rstd[:], func=AF.Sqrt, bias=col("eps_ln")), reads=[r.B_rstd, B_cv], writes=[r.B_rstd])
            op("dve", lambda e: e.reciprocal(out=r.rstd[:], in_=r.rstd[:]), reads=[r.B_rstd], writes=[r.B_rstd])
            for c in range(NC8):
                op("dve", (lambda c=c: lambda e: e.tensor_tensor(out=zc[c], in0=zc[c], in1=r.mean[:], op=ALU.subtract))(), reads=[Bz[c], r.B_mean], writes=[Bz[c]])
                op("dve", (lambda c=c: lambda e: e.tensor_tensor(out=zc[c], in0=zc[c], in1=r.rstd[:], op=ALU.mult))(), reads=[Bz[c], r.B_rstd], writes=[Bz[c]])
                op("act", (lambda c=c: lambda e: e.activation(out=zc[c], in_=zc[c], func=AF.Identity, scale=col(f"lng{l}_{j}", c), bias=col(f"lnb{l}_{j}", c)))(),
                   reads=[Bz[c], B_cv], writes=[Bz[c]])

        def tile_src(t, b, t0, T):
            return t[b].rearrange("(c p) s -> p c s", p=128)[:, :, t0:t0 + T]

        def run_tiles(tiles, LOAD, PRE, MAIN, POSTA=None, POSTB=None):
            n = len(tiles)
            LOAD(0, *tiles[0])
            PRE(0, *tiles[0])
            for i in range(n):
                if i + 1 < n:
                    LOAD(i + 1, *tiles[i + 1])
                done = [False]

                def pre_next(i=i, done=done):
                    if not done[0] and i + 1 < n:
                        PRE(i + 1, *tiles[i + 1])
                    done[0] = True
                MAIN(i, *tiles[i], pre_next)
                pre_next()
                if POSTB is not None and i > 0:
                    POSTB(i - 1, *tiles[i - 1])
                if POSTA is not None:
                    POSTA(i, *tiles[i])
            if POSTB is not None:
                POSTB(n - 1, *tiles[n - 1])

        def ffn_up_phase(l, src):
            T = 512 if S >= 512 else S
            NT = S // T
            HG = GC // 2
            with ExitStack() as es:
                wup = sbt(es, "wup", [128, NC8, 2 * FF], BF16)
                xin = [sbt(es, f"xin{i}", [128, NC8, T], F32) for i in range(2)]
                ub = [sbt(es, f"ub{i}", [128, NC8, T], BF16) for i in range(2)]
                gb = [sbt(es, f"gb{i}", [128, HG, T], BF16) for i in range(2)]
                NAB = 8
                ab = [sbt(es, f"ab{i}", [128, T], F32) for i in range(NAB)]
                sg = [sbt(es, f"sg{i}", [128, T], F32) for i in range(2)]
                tb = [sbt(es, f"tb{i}", [128, T], F32) for i in range(2)]
                B_tb = bufs(2, "tb")
                tails = [sbt(es, f"tails{i}", [128, HC, 2], F32) for i in range(2)]
                NWU = 4
                B_wup = bufs(NWU, "wup")
                B_xin = [bufs(NC8, f"xin{i}_") for i in range(2)]
                B_ub = [bufs(NC8, f"ub{i}_") for i in range(2)]
                B_gb = [Buf("gb0"), Buf("gb1")]
                B_ab = bufs(NAB, "ab")
                B_sg = bufs(2, "sg")
                B_tails = [bufs(HC, "tails0_"), bufs(HC, "tails1_")]
                wsrc = ffn_w_up[l].rearrange("(kc p) n -> p kc n", p=128)
                cw = 2 * FF // NWU
                for i in range(NWU):
                    op("pool", (lambda i=i: lambda e: e.dma_start(out=wup[:, :, i * cw:(i + 1) * cw], in_=wsrc[:, :, i * cw:(i + 1) * cw]))(), writes=[B_wup[i]], semkey=f"wA{i}")
                tiles = [(b, ti) for b in range(nseq) for ti in range(NT)]

                def LOAD(i, b, ti):
                    s = i % 2
                    op("sp", lambda e: e.dma_start(out=xin[s][:], in_=tile_src(src, b, ti * T, T)), writes=B_xin[s], semkey=f"ld{s}")

                def PRE(i, b, ti):
                    s = i % 2
                    for c in range(NC8):
                        op("dve", (lambda c=c: lambda e: e.tensor_scalar(out=ub[s][:, c, :], in0=xin[s][:, c, :], scalar1=mcol(l, 4, b, c), scalar2=mcol(l, 3, b, c), op0=ALU.mult, op1=ALU.add))(),
                           reads=[B_xin[s][c], B_modv[l]], writes=[B_ub[s][c]])

                def MAIN(i, b, ti, pre_next):
                    s = i % 2
                    par = i % 2
                    t0 = ti * T
                    if ti == 0:
                        op("pool", lambda e: e.memset(tails[1 - par][:], 0.0), writes=B_tails[1 - par])
                    pend = None

                    def glu(p, av, ag):
                        ss = p % 2
                        hf = p // HG
                        op("act", lambda e: e.activation(out=sg[ss][:], in_=ab[ag][:], func=AF.Silu), reads=[B_ab[ag]], writes=[B_sg[ss]])
                        op("pool", lambda e: e.tensor_tensor(out=gb[hf][:, p - hf * HG, :], in0=ab[av][:], in1=sg[ss][:], op=ALU.mult), reads=[B_ab[av], B_sg[ss]], writes=[B_gb[hf]])
                        if p % HG == HG - 1:
                            op("sp", lambda e: e.dma_start(out=g_d[b].rearrange("(c p) s -> p c s", p=128)[:, hf * HG:(hf + 1) * HG, t0:t0 + T], in_=gb[hf][:]), reads=[B_gb[hf]], semkey=f"sg{hf}")

                    for p in range(GC):
                        slots = []
                        for half in range(2):
                            c = p + half * GC
                            pb = (2 * p + half) % 6
                            a = (2 * p + half) % NAB
                            slots.append(a)

                            def mm(e, c=c, pb=pb):
                                ins = None
                                for kc in range(NC8):
                                    ins = e.matmul(bank(pb, T), lhsT=wup[:, kc, c * 128:(c + 1) * 128], rhs=ub[s][:, kc, :], start=(kc == 0), stop=(kc == NC8 - 1))
                                return ins
                            op("pe", mm, reads=[B_wup[c * 128 // cw]] + B_ub[s], writes=[PB[pb]])
                            op("act", (lambda c=c, pb=pb, a=a: lambda e: e.activation(out=ab[a][:], in_=bank(pb, T), func=AF.Identity, scale=col(f"fcw{l}_2", c), bias=col(f"fcb{l}", c)))(),
                               reads=[PB[pb], B_cv], writes=[B_ab[a]])
                            op("act", (lambda c=c, pb=pb: lambda e: e.activation(out=tails[par][:, c, :], in_=bank(pb, T)[:, T - 2:T], func=AF.Copy))(), reads=[PB[pb]], writes=[B_tails[par][c]])
                            if half == 0:
                                tt_ = p % 2
                                op("act", (lambda c=c, pb=pb, tt_=tt_: lambda e: e.activation(out=tb[tt_][:, 1:T], in_=bank(pb, T)[:, 0:T - 1], func=AF.Identity, scale=col(f"fcw{l}_1", c)))(),
                                   reads=[PB[pb], B_cv], writes=[B_tb[tt_]])
                                op("act", (lambda c=c, tt_=tt_: lambda e: e.activation(out=tb[tt_][:, 0:1], in_=tails[1 - par][:, c, 1:2], func=AF.Identity, scale=col(f"fcw{l}_1", c)))(),
                                   reads=[B_tails[1 - par][c], B_cv], writes=[B_tb[tt_]])
                                op("dve", (lambda c=c, pb=pb, a=a: lambda e: e.scalar_tensor_tensor(out=ab[a][:, 2:T], in0=bank(pb, T)[:, 0:T - 2], scalar=col(f"fcw{l}_0", c), in1=ab[a][:, 2:T], op0=ALU.mult, op1=ALU.add))(),
                                   reads=[PB[pb], B_ab[a], B_cv, B_tb[tt_]], writes=[B_ab[a]])
                                op("dve", (lambda c=c, a=a: lambda e: e.scalar_tensor_tensor(out=ab[a][:, 0:2], in0=tails[1 - par][:, c, 0:2], scalar=col(f"fcw{l}_0", c), in1=ab[a][:, 0:2], op0=ALU.mult, op1=ALU.add))(),
                                   reads=[B_tails[1 - par][c], B_ab[a], B_cv], writes=[B_ab[a]])
                                op("pool", (lambda a=a, tt_=tt_: lambda e: e.tensor_tensor(out=ab[a][:], in0=ab[a][:], in1=tb[tt_][:], op=ALU.add))(), reads=[B_ab[a], B_tb[tt_]], writes=[B_ab[a]])
                            else:
                                op("dve", (lambda c=c, pb=pb, a=a: lambda e: e.scalar_tensor_tensor(out=ab[a][:, 1:T], in0=bank(pb, T)[:, 0:T - 1], scalar=col(f"fcw{l}_1", c), in1=ab[a][:, 1:T], op0=ALU.mult, op1=ALU.add))(),
                                   reads=[PB[pb], B_ab[a], B_cv, B_tails[par][c]], writes=[B_ab[a]])
                                op("dve", (lambda c=c, pb=pb, a=a: lambda e: e.scalar_tensor_tensor(out=ab[a][:, 2:T], in0=bank(pb, T)[:, 0:T - 2], scalar=col(f"fcw{l}_0", c), in1=ab[a][:, 2:T], op0=ALU.mult, op1=ALU.add))(),
                                   reads=[PB[pb], B_ab[a], B_cv], writes=[B_ab[a]])
                                op("dve", (lambda c=c, a=a: lambda e: e.scalar_tensor_tensor(out=ab[a][:, 0:1], in0=tails[1 - par][:, c, 1:2], scalar=col(f"fcw{l}_1", c), in1=ab[a][:, 0:1], op0=ALU.mult, op1=ALU.add))(),
                                   reads=[B_tails[1 - par][c], B_ab[a], B_cv], writes=[B_ab[a]])
                                op("dve", (lambda c=c, a=a: lambda e: e.scalar_tensor_tensor(out=ab[a][:, 0:2], in0=tails[1 - par][:, c, 0:2], scalar=col(f"fcw{l}_0", c), in1=ab[a][:, 0:2], op0=ALU.mult, op1=ALU.add))(),
                                   reads=[B_tails[1 - par][c], B_ab[a], B_cv], writes=[B_ab[a]])
                        if pend is not None:
                            glu(*pend)
                        pend = (p, slots[0], slots[1])
                        if p == 14:
                            pre_next()
                    glu(*pend)

                run_tiles(tiles, LOAD, PRE, MAIN)
                S_.flush()

        def ffn_down_phase(l, src, dst):
            T = 512 if S >= 512 else S
            NT = S // T
            with ExitStack() as es:
                wdn = sbt(es, "wdn", [128, GC, D], BF16)
                zb = [sbt(es, f"zb{i}", [128, NC8, T], F32) for i in range(3)]
                gt = [sbt(es, f"gt{i}", [128, GC, T], BF16) for i in range(2)]
                lnr = ln_alloc(es, T)
                B_wdn = Buf("wdn")
                B_zb = [bufs(NC8, f"zb{i}_") for i in range(3)]
                B_gt = [Buf("gt0"), Buf("gt1")]
                op("pool", lambda e: e.dma_start(out=wdn[:], in_=ffn_w_down[l].rearrange("(kc p) n -> p kc n", p=128)), writes=[B_wdn], semkey="wB0")
                tiles = [(b, ti) for b in range(nseq) for ti in range(NT)]

                def LOAD(i, b, ti):
                    s = i % 2
                    sx = i % 3
                    op("sp", lambda e: e.dma_start(out=gt[s][:], in_=tile_src(g_d, b, ti * T, T)), writes=[B_gt[s]], semkey=f"lo{s}")
                    op("sp", lambda e: e.dma_start(out=zb[sx][:], in_=tile_src(src, b, ti * T, T)), writes=B_zb[sx], semkey=f"ld{sx}")

                def PRE(i, b, ti):
                    s = i % 2
                    sx = i % 3
                    for c in range(NC8):
                        op("act", (lambda c=c: lambda e: e.activation(out=zb[sx][:, c, :], in_=zb[sx][:, c, :], func=AF.Identity, scale=ALPHA))(), reads=[B_zb[sx][c]], writes=[B_zb[sx][c]])

                def MAIN(i, b, ti, pre_next):
                    s = i % 2
                    sx = i % 3
                    for oc in range(NC8):
                        pb = oc % 4

                        def mm2(e, oc=oc, pb=pb):
                            ins = None
                            for c in range(GC):
                                ins = e.matmul(bank(pb, T), lhsT=wdn[:, c, oc * 128:(oc + 1) * 128], rhs=gt[s][:, c, :], start=(c == 0), stop=(c == GC - 1))
                            return ins
                        op("pe", mm2, reads=[B_wdn, B_gt[s]], writes=[PB[pb]])
                        op("dve", (lambda oc=oc, pb=pb: lambda e: e.scalar_tensor_tensor(out=zb[sx][:, oc, :], in0=bank(pb, T), scalar=mcol(l, 5, b, oc), in1=zb[sx][:, oc, :], op0=ALU.mult, op1=ALU.add))(),
                           reads=[PB[pb], B_zb[sx][oc], B_modv[l]], writes=[B_zb[sx][oc]])

                def POSTA(i, b, ti):
                    sx = i % 3
                    ln_reduce(lnr, [zb[sx][:, c, :] for c in range(NC8)], B_zb[sx])

                def POSTB(i, b, ti):
                    sx = i % 3
                    ln_finish(lnr, [zb[sx][:, c, :] for c in range(NC8)], B_zb[sx], l, 1, 6, 7)
                    op("sp", lambda e: e.dma_start(out=tile_src(dst, b, ti * T, T), in_=zb[sx][:]), reads=B_zb[sx], semkey=f"st{sx}")

                run_tiles(tiles, LOAD, PRE, MAIN, POSTA, POSTB)
                S_.flush()

        def pool_phase(l, src, dst):
            T = 512 if S >= 512 else S
            NT = S // T
            H = 16
            E = H + T
            pj = l // 3
            with ExitStack() as es:
                pw = sbt(es, "pw", [128, 4, 2, 256], BF16)
                xw = [sbt(es, f"xw{i}", [128, NC8, H + T], F32) for i in range(3)]
                uw = sbt(es, "uw", [128, NC8, H + T], F32)
                scr = [sbt(es, f"pscr{i}", [128, 2, H + T], F32) for i in range(2)]
                pl = [sbt(es, f"pl{i}", [128, NC8, T], BF16) for i in range(2)]
                lnr = ln_alloc(es, T)
                inflight = []
                if deferred:
                    mwp = [sbt(es, f"mwp{i}", [128, NC8, D], F32) for i in range(2)]
                    B_mwp = bufs(2, "mwp")
                B_pw = Buf("pw")
                B_xw = [bufs(NC8, f"xw{i}_") for i in range(3)]
                B_uw = bufs(NC8, "uw")
                B_scr = bufs(2, "pscr")
                B_pl = [bufs(NC8, f"pl{i}_") for i in range(2)]
                op("pool", lambda e: e.dma_start(out=pw[:], in_=pool_w[pj].rearrange("g (kc p) n -> p g kc n", p=128)), writes=[B_pw], semkey="wA0")
                tiles = [(b, ti) for b in range(nseq) for ti in range(NT)]

                def LOAD(i, b, ti):
                    s = i % 2
                    sx = i % 3
                    t0 = ti * T
                    if ti == 0:
                        op("sp", lambda e: e.dma_start(out=xw[sx][:, :, H:H + T], in_=tile_src(src, b, t0, T)), writes=B_xw[sx], semkey=f"ld{sx}")
                    else:
                        op("sp", lambda e: e.dma_start(out=xw[sx][:], in_=tile_src(src, b, t0 - H, T + H)), writes=B_xw[sx], semkey=f"ld{sx}")

                def PRE(i, b, ti):
                    s = i % 2
                    sx = i % 3
                    lo = H if ti == 0 else 0
                    for c in range(NC8):
                        if ti == 0:
                            op("pool", (lambda c=c: lambda e: e.memset(uw[:, c, 0:H], 0.0))(), writes=[B_uw[c]])
                        op("act", (lambda c=c: lambda e: e.activation(out=uw[:, c, lo:E], in_=xw[sx][:, c, lo:E], func=AF.Identity, scale=mcol(l, 1, b, c), bias=mcol(l, 0, b, c)))(),
                           reads=[B_xw[sx][c], B_modv[l]], writes=[B_uw[c]])
                    for c in range(NC8):
                        op("act", (lambda c=c: lambda e: e.activation(out=xw[sx][:, c, H:E], in_=xw[sx][:, c, H:E], func=AF.Identity, scale=ALPHA))(), reads=[B_xw[sx][c]], writes=[B_xw[sx][c]])
                    for g in range(4):
                        w = 2 << g
                        cs = slice(2 * g, 2 * g + 2)
                        Bu = [B_uw[2 * g], B_uw[2 * g + 1]]
                        starts = {0: [16], 1: [14, 16], 2: [10, 12, 16], 3: [2, 4, 8, 16]}[g]
                        cur = None
                        SC, BSC, weng = (scr, B_scr, "dve")
                        for lvl, st in enumerate(starts):
                            sh = 1 << lvl
                            dsti = lvl % 2

                            def lv(e, cur=cur, dsti=dsti, st=st, sh=sh, cs=cs, SC=SC):
                                srcT = uw[:, cs, :] if cur is None else SC[cur][:, :, :]
                                return e.tensor_tensor(out=SC[dsti][:, :, st:E], in0=srcT[:, :, st:E], in1=srcT[:, :, st - sh:E - sh], op=ALU.add)
                            op(weng, lv, reads=(Bu if cur is None else [BSC[cur]]), writes=[BSC[dsti]])
                            cur = dsti
                        if ti == 0:
                            def cr(e, cur=cur, g=g, SC=SC):
                                cc = cv[:, coff["corr"] + g * 16: coff["corr"] + (g + 1) * 16]
                                ins = None
                                for k in range(2):
                                    ins = e.tensor_tensor(out=SC[cur][:, k, H:H + 16], in0=SC[cur][:, k, H:H + 16], in1=cc, op=ALU.mult)
                                return ins
                            op("dve", cr, reads=[BSC[cur], B_cv], writes=[BSC[cur]])
                        op("dve", (lambda cur=cur, cs=cs, w=w, SC=SC: lambda e: e.scalar_tensor_tensor(out=pl[s][:, cs, :], in0=SC[cur][:, :, H:E], scalar=1.0 / w, in1=uw[:, cs, H:E], op0=ALU.mult, op1=ALU.subtract))(),
                           reads=[BSC[cur]] + Bu, writes=[B_pl[s][2 * g], B_pl[s][2 * g + 1]])

                def MAIN(i, b, ti, pre_next):
                    s = i % 2
                    sx = i % 3
                    if deferred or inflight:
                        if inflight:
                            mod_compute(inflight.pop(0), mwp, B_mwp, 4)
                        if deferred:
                            inflight.append(mod_dma(*deferred.pop(0), mwp, B_mwp))
                    for g in range(4):
                        for oc in range(2):
                            c = 2 * g + oc
                            pb = c % 4

                            def mm(e, g=g, oc=oc, pb=pb):
                                ins = None
                                for kc in range(2):
                                    ins = e.matmul(bank(pb, T), lhsT=pw[:, g, kc, oc * 128:(oc + 1) * 128], rhs=pl[s][:, 2 * g + kc, :], start=(kc == 0), stop=(kc == 1))
                                return ins
                            op("pe", mm, reads=[B_pw, B_pl[s][2 * g], B_pl[s][2 * g + 1]], writes=[PB[pb]])
                            op("dve", (lambda c=c, pb=pb: lambda e: e.scalar_tensor_tensor(out=xw[sx][:, c, H:E], in0=bank(pb, T), scalar=mcol(l, 2, b, c), in1=xw[sx][:, c, H:E], op0=ALU.mult, op1=ALU.add))(),
                               reads=[PB[pb], B_xw[sx][c], B_modv[l]], writes=[B_xw[sx][c]])

                def POSTA(i, b, ti):
                    sx = i % 3
                    ln_reduce(lnr, [xw[sx][:, c, H:E] for c in range(NC8)], B_xw[sx])

                def POSTB(i, b, ti):
                    sx = i % 3
                    ln_finish(lnr, [xw[sx][:, c, H:E] for c in range(NC8)], B_xw[sx], l, 0, 6, 7)
                    op("sp", lambda e: e.dma_start(out=tile_src(dst, b, ti * T, T), in_=xw[sx][:, :, H:E]), reads=B_xw[sx], semkey=f"st{sx}")

                run_tiles(tiles, LOAD, PRE, MAIN, POSTA, POSTB)
                while deferred or inflight:
                    if inflight:
                        mod_compute(inflight.pop(0), mwp, B_mwp, 4)
                    if deferred:
                        inflight.append(mod_dma(*deferred.pop(0), mwp, B_mwp))
                S_.flush()

        def sconv_phase(l, src, dst):
            T = 512 if S >= 512 else S
            NT = S // T
            with ExitStack() as es:
                win = sbt(es, "win", [128, NC8, 3 * D], BF16)
                wout = sbt(es, "wout", [128, NC8, D], BF16)
                xw = [sbt(es, f"sxw{i}", [128, NC8, T], F32) for i in range(3)]
                ub = [sbt(es, f"sub{i}", [128, NC8, T], BF16) for i in range(2)]
                qb = sbt(es, "sqb", [128, NC8, T], BF16)
                t1 = [sbt(es, f"st1{i}", [128, T], F32) for i in range(2)]
                pbuf = [sbt(es, f"spb{i}", [128, T + 2], F32) for i in range(2)]
                ab = [sbt(es, f"sab{i}", [128, T], F32) for i in range(2)]
                ptl = sbt(es, "sptl", [128, NC8, 2], F32)
                lnr = ln_alloc(es, T)
                B_win = bufs(3, "win")
                B_wout = Buf("wout")
                B_xw = [bufs(NC8, f"sxw{i}_") for i in range(3)]
                B_ub = [bufs(NC8, f"sub{i}_") for i in range(2)]
                B_qb = bufs(NC8, "sqb")
                B_t1 = bufs(2, "st1")
                B_pb = bufs(2, "spb")
                B_ab = bufs(2, "sab")
                B_ptl = bufs(NC8, "sptl")
                wsrc = sc_w_in.rearrange("(kc p) n -> p kc n", p=128)
                for i in range(3):
                    op("pool", (lambda i=i: lambda e: e.dma_start(out=win[:, :, i * D:(i + 1) * D], in_=wsrc[:, :, i * D:(i + 1) * D]))(), writes=[B_win[i]], semkey=f"wA{i}")
                op("pool", lambda e: e.dma_start(out=wout[:], in_=sc_w_out.rearrange("(kc p) n -> p kc n", p=128)), writes=[B_wout], semkey="wB0")
                tiles = [(b, ti) for b in range(nseq) for ti in range(NT)]

                def LOAD(i, b, ti):
                    s = i % 2
                    sx = i % 3
                    op("sp", lambda e: e.dma_start(out=xw[sx][:], in_=tile_src(src, b, ti * T, T)), writes=B_xw[sx], semkey=f"ld{sx}")

                def PRE(i, b, ti):
                    s = i % 2
                    sx = i % 3
                    for c in range(NC8):
                        op("dve", (lambda c=c: lambda e: e.tensor_scalar(out=ub[s][:, c, :], in0=xw[sx][:, c, :], scalar1=mcol(l, 1, b, c), scalar2=mcol(l, 0, b, c), op0=ALU.mult, op1=ALU.add))(),
                           reads=[B_xw[sx][c], B_modv[l]], writes=[B_ub[s][c]])
                    for c in range(NC8):
                        op("act", (lambda c=c: lambda e: e.activation(out=xw[sx][:, c, :], in_=xw[sx][:, c, :], func=AF.Identity, scale=ALPHA))(), reads=[B_xw[sx][c]], writes=[B_xw[sx][c]])

                def MAIN(i, b, ti, pre_next):
                    s = i % 2
                    sx = i % 3
                    if ti == 0:
                        op("pool", lambda e: e.memset(ptl[:], 0.0), writes=B_ptl)
                    for c in range(NC8):
                        k2 = c % 2
                        banks3 = [0 + 3 * k2, 1 + 3 * k2, 2 + 3 * k2]
                        for which in range(3):
                            def mm(e, which=which, c=c, pbk=banks3[which]):
                                ins = None
                                for kc in range(NC8):
                                    ins = e.matmul(bank(pbk, T), lhsT=win[:, kc, which * D + c * 128: which * D + (c + 1) * 128], rhs=ub[s][:, kc, :], start=(kc == 0), stop=(kc == NC8 - 1))
                                return ins
                            op("pe", mm, reads=[B_win[which]] + B_ub[s], writes=[PB[banks3[which]]])
                        op("act", (lambda k2=k2, pbk=banks3[1]: lambda e: e.activation(out=t1[k2][:], in_=bank(pbk, T), func=AF.Copy))(), reads=[PB[banks3[1]]], writes=[B_t1[k2]])
                        op("pool", (lambda k2=k2, c=c: lambda e: e.tensor_copy(out=pbuf[k2][:, 0:2], in_=ptl[:, c, :]))(), reads=[B_ptl[c]], writes=[B_pb[k2]])
                        op("dve", (lambda k2=k2, pbk=banks3[2]: lambda e: e.tensor_tensor(out=pbuf[k2][:, 2:T + 2], in0=bank(pbk, T), in1=t1[k2][:], op=ALU.mult))(),
                           reads=[PB[banks3[2]], B_t1[k2]], writes=[B_pb[k2]])
                        op("pool", (lambda k2=k2, c=c: lambda e: e.tensor_copy(out=ptl[:, c, :], in_=pbuf[k2][:, T:T + 2]))(), reads=[B_pb[k2]], writes=[B_ptl[c]])
                        op("dve", (lambda k2=k2, c=c: lambda e: e.tensor_scalar(out=ab[k2][:], in0=pbuf[k2][:, 2:T + 2], scalar1=col("scw2", c), scalar2=None, op0=ALU.mult))(),
                           reads=[B_pb[k2], B_cv], writes=[B_ab[k2]])
                        op("dve", (lambda k2=k2, c=c: lambda e: e.scalar_tensor_tensor(out=ab[k2][:], in0=pbuf[k2][:, 1:T + 1], scalar=col("scw1", c), in1=ab[k2][:], op0=ALU.mult, op1=ALU.add))(),
                           reads=[B_pb[k2], B_ab[k2], B_cv], writes=[B_ab[k2]])
                        op("dve", (lambda k2=k2, c=c: lambda e: e.scalar_tensor_tensor(out=ab[k2][:], in0=pbuf[k2][:, 0:T], scalar=col("scw0", c), in1=ab[k2][:], op0=ALU.mult, op1=ALU.add))(),
                           reads=[B_pb[k2], B_ab[k2], B_cv], writes=[B_ab[k2]])
                        op("dve", (lambda k2=k2, c=c, pbk=banks3[0]: lambda e: e.tensor_tensor(out=qb[:, c, :], in0=bank(pbk, T), in1=ab[k2][:], op=ALU.mult))(),
                           reads=[PB[banks3[0]], B_ab[k2]], writes=[B_qb[c]])
                    pre_next()
                    for oc in range(NC8):
                        pb = 6 + oc % 2

                        def mm2(e, oc=oc, pb=pb):
                            ins = None
                            for kc in range(NC8):
                                ins = e.matmul(bank(pb, T), lhsT=wout[:, kc, oc * 128:(oc + 1) * 128], rhs=qb[:, kc, :], start=(kc == 0), stop=(kc == NC8 - 1))
                            return ins
                        op("pe", mm2, reads=[B_wout] + B_qb, writes=[PB[pb]])
                        op("dve", (lambda oc=oc, pb=pb: lambda e: e.scalar_tensor_tensor(out=xw[sx][:, oc, :], in0=bank(pb, T), scalar=mcol(l, 2, b, oc), in1=xw[sx][:, oc, :], op0=ALU.mult, op1=ALU.add))(),
                           reads=[PB[pb], B_xw[sx][oc], B_modv[l]], writes=[B_xw[sx][oc]])

                def POSTA(i, b, ti):
                    sx = i % 3
                    ln_reduce(lnr, [xw[sx][:, c, :] for c in range(NC8)], B_xw[sx])

                def POSTB(i, b, ti):
                    sx = i % 3
                    ln_finish(lnr, [xw[sx][:, c, :] for c in range(NC8)], B_xw[sx], l, 0, 6, 7)
                    op("sp", lambda e: e.dma_start(out=tile_src(dst, b, ti * T, T), in_=xw[sx][:]), reads=B_xw[sx], semkey=f"st{sx}")

                run_tiles(tiles, LOAD, PRE, MAIN, POSTA, POSTB)
                S_.flush()

        def mla_a1(l, src):
            T = 512 if S >= 512 else S
            NT = S // T
            with ExitStack() as es:
                wa = sbt(es, "wa", [128, NC8, 1056 + 32], BF16)
                wqp = sbt(es, "wqp", [128, 6, 1024], BF16)
                xw = [sbt(es, f"axw{i}", [128, NC8, T], F32) for i in range(1)]
                ub = [sbt(es, f"aub{i}", [128, NC8, T], BF16) for i in range(2)]
                cqf = sbt(es, "cqf", [128, 8, T], F32)
                sq = [sbt(es, f"asq{i}", [128, T], F32) for i in range(2)]
                rs = [sbt(es, f"ars{i}", [128, T], F32) for i in range(2)]
                cqn = [sbt(es, f"acqn{i}", [128, 8, T], BF16) for i in range(2)]
                cosT = sbt(es, "cosT", [128, S], F32)
                sinT = sbt(es, "sinT", [128, S], F32)
                tscr = sbt(es, "tscr", [128, S], F32)
                posi = sbt(es, "posi", [128, S], I32)
                tki = posi
                rt = [sbt(es, f"art{i}", [128, T], F32) for i in range(4)]
                rpe = [sbt(es, f"arpe{i}", [128, 5, T], BF16) for i in range(2)]
                B_wa = Buf("wa")
                B_wqp = Buf("wqp")
                B_xw = [bufs(NC8, f"axw{i}_") for i in range(1)]
                B_ub = [bufs(NC8, f"aub{i}_") for i in range(2)]
                B_cqf = bufs(8, "cqf")
                B_sq = bufs(2, "asq")
                B_rs = bufs(2, "ars")
                B_cqn = [bufs(8, f"acqn{i}_") for i in range(2)]
                B_tab = Buf("tab")
                B_tscr = Buf("tscr")
                B_rt = bufs(4, "art")
                B_rpe = [bufs(5, f"arpe{i}_") for i in range(2)]
                op("pool", lambda e: e.dma_start(out=wa[:, :, 0:1056], in_=w_a.rearrange("(kc p) n -> p kc n", p=128)), writes=[B_wa], semkey="wA0")
                op("pool", lambda e: e.dma_start(out=wa[:, :, 1056:1088], in_=w_a_sw.rearrange("(kc p) n -> p kc n", p=128)), writes=[B_wa], semkey="wA0")
                op("pool", lambda e: e.dma_start(out=wqp[:, :, 0:512], in_=w_uq_pe.rearrange("(kc p) n -> p kc n", p=128)), writes=[B_wqp], semkey="wA1")
                op("pool", lambda e: e.dma_start(out=wqp[:, :, 512:1024], in_=w_uq_pesw.rearrange("(kc p) n -> p kc n", p=128)), writes=[B_wqp], semkey="wA1")
                C1 = 6.28125
                C2 = float(2 * np.pi - 6.28125)
                PI = float(np.pi)

                def tables(b):
                    op("sp", lambda e: e.dma_start(out=posi[:], in_=pos[b:b + 1, :].partition_broadcast(128)), writes=[B_tscr], semkey="ld2")
                    op("dve", lambda e: e.tensor_copy(out=sinT[:], in_=posi[:]), reads=[B_tscr], writes=[B_tab])
                    op("dve", lambda e: e.tensor_scalar(out=sinT[:], in0=sinT[:], scalar1=col("invf"), scalar2=None, op0=ALU.mult), reads=[B_tab, B_cv], writes=[B_tab])
                    op("dve", lambda e: e.tensor_scalar(out=cosT[:], in0=sinT[:], scalar1=PI / 2, scalar2=None, op0=ALU.add), reads=[B_tab], writes=[B_tab])
                    for tb in (sinT, cosT):
                        op("dve", (lambda tb=tb: lambda e: e.tensor_scalar(out=tscr[:], in0=tb[:], scalar1=float(1 / (2 * np.pi)), scalar2=None, op0=ALU.mult))(), reads=[B_tab], writes=[B_tscr])
                        op("dve", lambda e: e.tensor_copy(out=tki[:], in_=tscr[:]), reads=[B_tscr], writes=[B_tscr])
                        op("dve", lambda e: e.tensor_copy(out=tscr[:], in_=tki[:]), reads=[B_tscr], writes=[B_tscr])
                        op("dve", (lambda tb=tb: lambda e: e.scalar_tensor_tensor(out=tb[:], in0=tscr[:], scalar=-C1, in1=tb[:], op0=ALU.mult, op1=ALU.add))(), reads=[B_tscr, B_tab], writes=[B_tab])
                        op("dve", (lambda tb=tb: lambda e: e.scalar_tensor_tensor(out=tb[:], in0=tscr[:], scalar=-C2, in1=tb[:], op0=ALU.mult, op1=ALU.add))(), reads=[B_tscr, B_tab], writes=[B_tab])
                        op("dve", (lambda tb=tb: lambda e: e.tensor_scalar(out=tscr[:], in0=tb[:], scalar1=PI, scalar2=float(-2 * np.pi), op0=ALU.is_gt, op1=ALU.mult))(), reads=[B_tab], writes=[B_tscr])
                        op("dve", (lambda tb=tb: lambda e: e.tensor_tensor(out=tb[:], in0=tb[:], in1=tscr[:], op=ALU.add))(), reads=[B_tscr, B_tab], writes=[B_tab])
                        op("dve", (lambda tb=tb: lambda e: e.tensor_scalar(out=tscr[:], in0=tb[:], scalar1=-PI, scalar2=float(2 * np.pi), op0=ALU.is_lt, op1=ALU.mult))(), reads=[B_tab], writes=[B_tscr])
                        op("dve", (lambda tb=tb: lambda e: e.tensor_tensor(out=tb[:], in0=tb[:], in1=tscr[:], op=ALU.add))(), reads=[B_tscr, B_tab], writes=[B_tab])
                        op("dve", (lambda tb=tb: lambda e: e.tensor_scalar(out=tb[:], in0=tb[:], scalar1=PI, scalar2=-PI, op0=ALU.min, op1=ALU.max))(), reads=[B_tab], writes=[B_tab])
                        op("act", (lambda tb=tb: lambda e: e.activation(out=tb[:], in_=tb[:], func=AF.Sin))(), reads=[B_tab], writes=[B_tab])
                    op("dve", lambda e: e.tensor_scalar(out=sinT[:], in0=sinT[:], scalar1=col("sgn"), scalar2=None, op0=ALU.mult), reads=[B_tab, B_cv], writes=[B_tab])

                tiles = [(b, ti) for b in range(nseq) for ti in range(NT)]

                def LOAD(i, b, ti):
                    s = i % 2
                    op("sp", lambda e: e.dma_start(out=xw[0][:], in_=tile_src(src, b, ti * T, T)), writes=B_xw[0], semkey="ld0")

                def PRE(i, b, ti):
                    s = i % 2
                    for c in range(NC8):
                        op("dve", (lambda c=c: lambda e: e.tensor_scalar(out=ub[s][:, c, :], in0=xw[0][:, c, :], scalar1=mcol(l, 1, b, c), scalar2=mcol(l, 0, b, c), op0=ALU.mult, op1=ALU.add))(),
                           reads=[B_xw[0][c], B_modv[l]], writes=[B_ub[s][c]])

                def MAIN(i, b, ti, pre_next):
                    s = i % 2
                    t0 = ti * T
                    if ti == 0:
                        tables(b)
                    for c in range(8):
                        pb = c % 3
                        grp = 0 if c < 6 else 1

                        def mm(e, c=c, pb=pb):
                            ins = None
                            for kc in range(NC8):
                                ins = e.matmul(bank(pb, T), lhsT=wa[:, kc, c * 128:(c + 1) * 128], rhs=ub[s][:, kc, :], start=(kc == 0), stop=(kc == NC8 - 1))
                            return ins
                        op("pe", mm, reads=[B_wa] + B_ub[s], writes=[PB[pb]])
                        op("act", (lambda c=c, pb=pb: lambda e: e.activation(out=cqf[:, c, :], in_=bank(pb, T), func=AF.Copy))(), reads=[PB[pb]], writes=[B_cqf[c]])
                        op("act", (lambda c=c, pb=pb: lambda e: e.activation(out=sq[c % 2][:], in_=bank(pb, T), func=AF.Square))(), reads=[PB[pb]], writes=[B_sq[c % 2]])
                        first = c in (0, 6)
                        last = c in (5, 7)
                        op("pe", (lambda c=c, grp=grp, first=first, last=last: lambda e: e.matmul(bank(6 + grp, T), lhsT=(ones_q if grp == 0 else ones_kv)[:], rhs=sq[c % 2][:], start=first, stop=last))(),
                           reads=[B_sq[c % 2], B_ones], writes=[PB[6 + grp]])
                    for grp in range(2):
                        op("dve", (lambda grp=grp: lambda e: e.tensor_scalar(out=rs[grp][:], in0=bank(6 + grp, T), scalar1=0.0, scalar2=None, op0=ALU.max))(), reads=[PB[6 + grp]], writes=[B_rs[grp]])
                        op("act", (lambda grp=grp: lambda e: e.activation(out=rs[grp][:], in_=rs[grp][:], func=AF.Sqrt, bias=col("eps_rms")))(), reads=[B_rs[grp], B_cv], writes=[B_rs[grp]])
                        op("dve", (lambda grp=grp: lambda e: e.reciprocal(out=rs[grp][:], in_=rs[grp][:]))(), reads=[B_rs[grp]], writes=[B_rs[grp]])
                    for c in range(8):
                        grp = 0 if c < 6 else 1
                        nm = col("qnorm", c) if c < 6 else col("kvnorm", c - 6)
                        op("dve", (lambda c=c, grp=grp, nm=nm: lambda e: e.scalar_tensor_tensor(out=cqn[s][:, c, :], in0=cqf[:, c, :], scalar=nm, in1=rs[grp][:], op0=ALU.mult, op1=ALU.mult))(),
                           reads=[B_cqf[c], B_rs[grp], B_cv], writes=[B_cqn[s][c]])
                    op("sp", lambda e: e.dma_start(out=cqn_d[b].rearrange("(c p) s -> p c s", p=128)[:, :, t0:t0 + T], in_=cqn[s][:, 0:6, :]), reads=B_cqn[s][0:6], semkey=f"st{s}")
                    op("sp", lambda e: e.dma_start(out=ckvn_d[b].rearrange("(c p) s -> p c s", p=128)[:, :, t0:t0 + T], in_=cqn[s][:, 6:8, :]), reads=B_cqn[s][6:8], semkey=f"st{s}")
                    pre_next()
                    for c in range(5):
                        pa = 3 if c % 2 == 0 else 0
                        pbk = 4 if c % 2 == 0 else 5
                        M = 128 if c < 4 else 32

                        def mmA(e, c=c, pa=pa):
                            ins = None
                            if c < 4:
                                for kc in range(6):
                                    ins = e.matmul(bank(pa, T), lhsT=wqp[:, kc, c * 128:(c + 1) * 128], rhs=cqn[s][:, kc, :], start=(kc == 0), stop=(kc == 5))
                            else:
                                for kc in range(NC8):
                                    ins = e.matmul(bank(pa, T)[0:32, :], lhsT=wa[:, kc, 1024:1056], rhs=ub[s][:, kc, :], start=(kc == 0), stop=(kc == NC8 - 1))
                            return ins

                        def mmB(e, c=c, pbk=pbk):
                            ins = None
                            if c < 4:
                                for kc in range(6):
                                    ins = e.matmul(bank(pbk, T), lhsT=wqp[:, kc, 512 + c * 128:512 + (c + 1) * 128], rhs=cqn[s][:, kc, :], start=(kc == 0), stop=(kc == 5))
                            else:
                                for kc in range(NC8):
                                    ins = e.matmul(bank(pbk, T)[0:32, :], lhsT=wa[:, kc, 1056:1088], rhs=ub[s][:, kc, :], start=(kc == 0), stop=(kc == NC8 - 1))
                            return ins
                        rd = (B_cqn[s][0:6] + [B_wqp]) if c < 4 else (B_ub[s] + [B_wa])
                        op("pe", mmA, reads=rd, writes=[PB[pa]])
                        op("pe", mmB, reads=rd, writes=[PB[pbk]])
                        r0 = (c % 2) * 2
                        op("dve", (lambda pa=pa, M=M, r0=r0: lambda e: e.tensor_tensor(out=rt[r0][0:M, :], in0=bank(pa, T)[0:M, :], in1=cosT[0:M, t0:t0 + T], op=ALU.mult))(),
                           reads=[PB[pa], B_tab], writes=[B_rt[r0]])
                        op("dve", (lambda pbk=pbk, M=M, r0=r0: lambda e: e.tensor_tensor(out=rt[r0 + 1][0:M, :], in0=bank(pbk, T)[0:M, :], in1=sinT[0:M, t0:t0 + T], op=ALU.mult))(),
                           reads=[PB[pbk], B_tab], writes=[B_rt[r0 + 1]])
                        op("pool", (lambda c=c, M=M, r0=r0: lambda e: e.tensor_tensor(out=rpe[s][0:M, c, :], in0=rt[r0][0:M, :], in1=rt[r0 + 1][0:M, :], op=ALU.add))(),
                           reads=[B_rt[r0], B_rt[r0 + 1]], writes=[B_rpe[s][c]])
                    op("sp", lambda e: e.dma_start(out=qpe_d[b].rearrange("(c p) s -> p c s", p=128)[:, :, t0:t0 + T], in_=rpe[s][:, 0:4, :]), reads=B_rpe[s][0:4], semkey=f"st{s}")
                    op("sp", lambda e: e.dma_start(out=kpe_d[b][:, t0:t0 + T], in_=rpe[s][0:32, 4, :]), reads=[B_rpe[s][4]], semkey=f"st{s}")

                run_tiles(tiles, LOAD, PRE, MAIN)
                S_.flush()

        def mla_a2():
            QT = 512 if S >= 512 else S
            NQ = S // QT
            NKT = S // 128
            KPQ = QT // 128
            with ExitStack() as es:
                wq = sbt(es, "wq", [128, 6, 1024], BF16)
                wk = sbt(es, "wk", [128, 2, 1024], BF16)
                wv = sbt(es, "wv", [128, 2, 1024], BF16)
                msk = sbt(es, "msk", [128, 4, 512], BF16)
                cq = sbt(es, "cq", [128, 6, S], BF16)
                ckv = sbt(es, "ckv", [128, 2, S], BF16)
                KTb = [sbt(es, f"KT{i}", [128, S], BF16) for i in range(2)]
                QTb = [sbt(es, f"QT{i}", [128, S], BF16) for i in range(2)]
                Vb = [sbt(es, f"V{i}", [128, NKT, 128], BF16) for i in range(2)]
                NPT = 4
                PT = [sbt(es, f"PT{i}", [128, QT], BF16) for i in range(NPT)]
                rden = sbt(es, "rden", [128, QT], F32)
                rden0 = sbt(es, "rden0", [128, QT], F32)
                ost = [sbt(es, f"ost{i}", [128, QT], BF16) for i in range(2)]
                B_w = Buf("a2w")
                B_msk = Buf("msk")
                B_cq = Buf("cq")
                B_ckv = Buf("ckv")
                B_KT = [Buf("KTn0"), Buf("KTn1")]
                B_KTp = [Buf("KTp0"), Buf("KTp1")]
                B_QT = [bufs(NQ, "QTn0_"), bufs(NQ, "QTn1_")]
                B_QTp = [Buf("QTp0"), Buf("QTp1")]
                B_V = [Buf("V0"), Buf("V1")]
                B_Vones = [Buf("Vo0"), Buf("Vo1")]
                B_PT = bufs(NPT, "PT")
                B_rden = Buf("rden")
                B_rden0 = Buf("rden0")
                B_ost = bufs(2, "ost")
                op("pool", lambda e: e.dma_start(out=wq[:], in_=w_uq_nope.rearrange("(kc p) n -> p kc n", p=128)), writes=[B_w], semkey="wA0")
                op("pool", lambda e: e.dma_start(out=wk[:], in_=w_uk.rearrange("(kc p) n -> p kc n", p=128)), writes=[B_w], semkey="wA0")
                op("pool", lambda e: e.dma_start(out=wv[:], in_=w_uv.rearrange("(kc p) n -> p kc n", p=128)), writes=[B_w], semkey="wA0")
                op("sp", lambda e: e.dma_start(out=msk[:], in_=masks_d), writes=[B_msk], semkey="ld2")
                for i in range(2):
                    op("pool", (lambda i=i: lambda e: e.memset(Vb[i][:, :, 64:128], 1.0))(), writes=[B_Vones[i]])

                def proj(b, h, sl):
                    if h == 0:
                        op("sp", lambda e: e.dma_start(out=cq[:], in_=cqn_d[b].rearrange("(c p) s -> p c s", p=128)), writes=[B_cq], semkey="ld0")
                        op("sp", lambda e: e.dma_start(out=ckv[:], in_=ckvn_d[b].rearrange("(c p) s -> p c s", p=128)), writes=[B_ckv], semkey="ld1")
                    op("sp", lambda e: e.dma_start(out=KTb[sl][64:96, :], in_=kpe_d[b]), writes=[B_KTp[sl]], semkey=f"kp{sl}")
                    op("sp", lambda e: e.dma_start(out=QTb[sl][64:96, :], in_=qpe_d[b][h * 32:(h + 1) * 32, :]), writes=[B_QTp[sl]], semkey=f"qp{sl}")
                    for qi in range(NQ):
                        pb = qi % 2

                        def mmk(e, qi=qi, pb=pb):
                            ins = None
                            for kc in range(2):
                                ins = e.matmul(bank(pb, QT)[0:64, :], lhsT=wk[:, kc, h * 64:(h + 1) * 64], rhs=ckv[:, kc, qi * QT:(qi + 1) * QT], start=(kc == 0), stop=(kc == 1))
                            return ins
                        op("pe", mmk, reads=[B_w, B_ckv], writes=[PB[pb]])
                        op("dve", (lambda qi=qi, pb=pb: lambda e: e.tensor_copy(out=KTb[sl][0:64, qi * QT:(qi + 1) * QT], in_=bank(pb, QT)[0:64, :]))(), reads=[PB[pb]], writes=[B_KT[sl]])
                    for qi in range(NQ):
                        pb = qi % 2

                        def mmq(e, qi=qi, pb=pb):
                            ins = None
                            for kc in range(6):
                                ins = e.matmul(bank(pb, QT)[0:64, :], lhsT=wq[:, kc, h * 64:(h + 1) * 64], rhs=cq[:, kc, qi * QT:(qi + 1) * QT], start=(kc == 0), stop=(kc == 5))
                            return ins
                        op("pe", mmq, reads=[B_w, B_cq], writes=[PB[pb]])
                        op("dve", (lambda qi=qi, pb=pb: lambda e: e.tensor_copy(out=QTb[sl][0:64, qi * QT:(qi + 1) * QT], in_=bank(pb, QT)[0:64, :]))(), reads=[PB[pb]], writes=[B_QT[sl][qi]])
                    for t8 in range(NKT // 8 if NKT >= 8 else 1):
                        nt8 = min(8, NKT)
                        pb = 2

                        def mmv(e, t8=t8, pb=pb, nt8=nt8):
                            ins = None
                            for k in range(nt8):
                                tc = t8 * 8 + k
                                for kc in range(2):
                                    ins = e.matmul(bank(pb)[:, k * 64:(k + 1) * 64], lhsT=ckv[:, kc, tc * 128:(tc + 1) * 128], rhs=wv[:, kc, h * 64:(h + 1) * 64], start=(kc == 0), stop=(kc == 1))
                            return ins
                        op("pe", mmv, reads=[B_w, B_ckv], writes=[PB[pb]])
                        op("dve", (lambda t8=t8, pb=pb, nt8=nt8: lambda e: e.tensor_copy(out=Vb[sl][:, t8 * 8:t8 * 8 + nt8, 0:64], in_=bank(pb)[:, 0:nt8 * 64].rearrange("p (k d) -> p k d", d=64)))(),
                           reads=[PB[pb]], writes=[B_V[sl]])

                cnt = [0]

                def emitS(b, h, sl, qi, ki):
                    k = cnt[0]
                    cnt[0] += 1
                    pbs = 3 + k % 3
                    pts = k % NPT
                    dg = ki - KPQ * qi
                    c0 = dg * 128 if dg > 0 else 0
                    op("pe", lambda e: e.matmul(bank(pbs, QT)[:, c0:QT], lhsT=KTb[sl][0:96, ki * 128:(ki + 1) * 128], rhs=QTb[sl][0:96, qi * QT + c0:(qi + 1) * QT], start=True, stop=True),
                       reads=[B_KT[sl], B_KTp[sl], B_QT[sl][qi], B_QTp[sl]], writes=[PB[pbs]])
                    op("act", lambda e: e.activation(out=PT[pts][:, c0:QT], in_=bank(pbs, QT)[:, c0:QT], func=AF.Exp, scale=SM_SCALE), reads=[PB[pbs]], writes=[B_PT[pts]])
                    if dg >= 0:
                        op("pool", lambda e: e.tensor_tensor(out=PT[pts][:, c0:c0 + 128], in0=PT[pts][:, c0:c0 + 128], in1=msk[:, 0, 0:128], op=ALU.mult), reads=[B_PT[pts], B_msk], writes=[B_PT[pts]])
                    return (pts, c0)

                def emitPV(b, h, sl, qi, ki, nk, st):
                    pts, c0 = st
                    po = 6 + qi % 2
                    op("pe", lambda e: e.matmul(bank(po, QT)[:, c0:QT], lhsT=Vb[sl][:, ki, :], rhs=PT[pts][:, c0:QT], start=(ki == 0), stop=(ki == nk - 1)),
                       reads=[B_V[sl], B_Vones[sl], B_PT[pts]], writes=[PB[po]])
                    if ki == nk - 1:
                        os_ = qi % 2
                        op("dve", lambda e: e.reciprocal(out=rden[64:128, :], in_=bank(po, QT)[64:128, :]), reads=[PB[po]], writes=[B_rden])
                        op("dve", lambda e: e.tensor_copy(out=rden0[0:64, :], in_=rden[64:128, :]), reads=[B_rden], writes=[B_rden0])
                        op("dve", lambda e: e.tensor_tensor(out=ost[os_][0:64, :], in0=bank(po, QT)[0:64, :], in1=rden0[0:64, :], op=ALU.mult), reads=[PB[po], B_rden0], writes=[B_ost[os_]])
                        op("sp", lambda e: e.dma_start(out=oT_d[b][h * 64:(h + 1) * 64, qi * QT:(qi + 1) * QT], in_=ost[os_][0:64, :]), reads=[B_ost[os_]], semkey=f"st{os_}")

                heads = [(b, h) for b in range(nseq) for h in range(HEADS)]
                LOOK = 2
                proj(heads[0][0], heads[0][1], 0)
                for gi, (b, h) in enumerate(heads):
                    sl = gi % 2
                    pairs = [(qi, ki) for qi in range(NQ) for ki in range(KPQ * (qi + 1))]
                    mid = len(pairs) // 2
                    states = {}
                    for j in range(min(LOOK, len(pairs))):
                        states[j] = emitS(b, h, sl, *pairs[j])
                    for j, (qi, ki) in enumerate(pairs):
                        if j + LOOK < len(pairs):
                            states[j + LOOK] = emitS(b, h, sl, *pairs[j + LOOK])
                        emitPV(b, h, sl, qi, ki, KPQ * (qi + 1), states.pop(j))
                        if j == mid and gi + 1 < len(heads):
                            proj(heads[gi + 1][0], heads[gi + 1][1], (gi + 1) % 2)
                S_.flush()

        def mla_a3(l, src, dst):
            T = 512 if S >= 512 else S
            NT = S // T
            with ExitStack() as es:
                wo = sbt(es, "wo", [128, NC8, D], BF16)
                xw = [sbt(es, f"oxw{i}", [128, NC8, T], F32) for i in range(3)]
                ob = [sbt(es, f"oob{i}", [128, NC8, T], BF16) for i in range(2)]
                lnr = ln_alloc(es, T)
                B_wo = Buf("wo")
                B_xw = [bufs(NC8, f"oxw{i}_") for i in range(3)]
                B_ob = [Buf("oob0"), Buf("oob1")]
                op("pool", lambda e: e.dma_start(out=wo[:], in_=w_o.rearrange("(kc p) n -> p kc n", p=128)), writes=[B_wo], semkey="wA0")
                tiles = [(b, ti) for b in range(nseq) for ti in range(NT)]

                def LOAD(i, b, ti):
                    s = i % 2
                    sx = i % 3
                    op("sp", lambda e: e.dma_start(out=xw[sx][:], in_=tile_src(src, b, ti * T, T)), writes=B_xw[sx], semkey=f"ld{sx}")
                    op("sp", lambda e: e.dma_start(out=ob[s][:], in_=tile_src(oT_d, b, ti * T, T)), writes=[B_ob[s]], semkey=f"lo{s}")

                def PRE(i, b, ti):
                    s = i % 2
                    sx = i % 3
                    for c in range(NC8):
                        op("act", (lambda c=c: lambda e: e.activation(out=xw[sx][:, c, :], in_=xw[sx][:, c, :], func=AF.Identity, scale=ALPHA))(), reads=[B_xw[sx][c]], writes=[B_xw[sx][c]])

                def MAIN(i, b, ti, pre_next):
                    s = i % 2
                    sx = i % 3
                    for oc in range(NC8):
                        pb = oc % 4

                        def mm(e, oc=oc, pb=pb):
                            ins = None
                            for kc in range(NC8):
                                ins = e.matmul(bank(pb, T), lhsT=wo[:, kc, oc * 128:(oc + 1) * 128], rhs=ob[s][:, kc, :], start=(kc == 0), stop=(kc == NC8 - 1))
                            return ins
                        op("pe", mm, reads=[B_wo, B_ob[s]], writes=[PB[pb]])
                        op("dve", (lambda oc=oc, pb=pb: lambda e: e.scalar_tensor_tensor(out=xw[sx][:, oc, :], in0=bank(pb, T), scalar=mcol(l, 2, b, oc), in1=xw[sx][:, oc, :], op0=ALU.mult, op1=ALU.add))(),
                           reads=[PB[pb], B_xw[sx][oc], B_modv[l]], writes=[B_xw[sx][oc]])

                def POSTA(i, b, ti):
                    sx = i % 3
                    ln_reduce(lnr, [xw[sx][:, c, :] for c in range(NC8)], B_xw[sx])

                def POSTB(i, b, ti):
                    sx = i % 3
                    ln_finish(lnr, [xw[sx][:, c, :] for c in range(NC8)], B_xw[sx], l, 0, 6, 7)
                    op("sp", lambda e: e.dma_start(out=tile_src(dst, b, ti * T, T), in_=xw[sx][:]), reads=B_xw[sx], semkey=f"st{sx}")

                run_tiles(tiles, LOAD, PRE, MAIN, POSTA, POSTB)
                S_.flush()

        phases = []
        for l in layers:
            phases.append(("mix", l))
            phases.append(("ffn", l))
        if stop_after is not None:
            phases = phases[:stop_after]
        cur = xT
        pp = [sA, sB]
        for pi_, (kind, l) in enumerate(phases):
            dst = yT if pi_ == len(phases) - 1 else pp[pi_ % 2]
            if kind == "ffn":
                ffn_up_phase(l, cur)
                ffn_down_phase(l, cur, dst)
            elif l % 3 == 0:
                pool_phase(l, cur, dst)
            elif l % 3 == 1:
                mla_a1(l, cur)
                mla_a2()
                mla_a3(l, cur, dst)
            else:
                sconv_phase(l, cur, dst)
            cur = dst
    return nc


def host_weights(inp):
    w = {}
    f32 = lambda a: np.ascontiguousarray(np.asarray(a, dtype=np.float32))
    w["cvec"] = cvec_layout(inp).build()
    w["masks"] = make_masks()
    w["mod_w"] = f32(inp["mod_w"])
    w["pool_w"] = f32(inp["pool_w"])
    wa = np.asarray(inp["mla_w_a"][0], np.float32)
    w["w_a"] = f32(wa)
    w["w_a_sw"] = f32(np.concatenate([wa[:, 1040:1056], wa[:, 1024:1040]], axis=1))
    wuq = np.asarray(inp["mla_w_uq"][0], np.float32).reshape(768, HEADS, 96)
    w["w_uq_nope"] = f32(wuq[:, :, 0:64].reshape(768, 1024))
    w["w_uq_pe"] = f32(wuq[:, :, 64:96].reshape(768, 512))
    w["w_uq_pesw"] = f32(np.concatenate([wuq[:, :, 80:96], wuq[:, :, 64:80]], axis=2).reshape(768, 512))
    wukv = np.asarray(inp["mla_w_ukv"][0], np.float32).reshape(256, HEADS, 128)
    w["w_uk"] = f32(wukv[:, :, 0:64].reshape(256, 1024))
    w["w_uv"] = f32(wukv[:, :, 64:128].reshape(256, 1024))
    w["w_o"] = f32(inp["mla_w_o"][0])
    w["sc_w_in"] = f32(inp["sc_w_in"][0])
    w["sc_w_out"] = f32(inp["sc_w_out"][0])
    w["ffn_w_up"] = f32(inp["ffn_w_up"])
    w["ffn_w_down"] = f32(inp["ffn_w_down"])
    return w


def core_inputs(inp, w, b0, nseq, S):
    x = np.asarray(inp["x"], np.float32)[b0:b0 + nseq, :S]
    m = dict(w)
    m["xT"] = np.ascontiguousarray(x.transpose(0, 2, 1))
    c = np.asarray(inp["c"], np.float32)[b0:b0 + nseq]
    m["cT"] = np.ascontiguousarray(c.reshape(nseq, NC8, 128).transpose(2, 1, 0).reshape(128, NC8 * nseq))
    m["pos"] = np.ascontiguousarray(np.asarray(inp["positions"], np.int32)[b0:b0 + nseq, :S])
    return m


_PROG_CACHE = {}


def kernel(**inputs):
    B, S, _ = inputs["x"].shape
    ncores = 8
    nseq = B // ncores
    key = (nseq, S)
    if key not in _PROG_CACHE:
        _PROG_CACHE[key] = build_program(nseq, S)
    nc = _PROG_CACHE[key]
    w = host_weights(inputs)
    in_maps = [core_inputs(inputs, w, i * nseq, nseq, S) for i in range(ncores)]
    res = run_bass_kernel_spmd(nc, in_maps, core_ids=list(range(ncores)))
    out = np.empty((B, S, D), np.float32)
    for i in range(ncores):
        out[i * nseq:(i + 1) * nseq] = res.results[i]["yT"].transpose(0, 2, 1)
    return out
```
